# Optimizing a Trainium2 kernel written in Bass

```python
import jax, jax.numpy as jnp
from jax import lax
import numpy as np

D_MODEL = 1024
BATCH = 16
SEQ = 2048
DEPTH = 2
DEC_BATCH = 32
DEC_SEQ = 16
PAST_LEN = 2048

CHUNK = 64
HEAD_DIM = 64
SB_WIDTH = D_MODEL // 2
N_SB_HEADS = SB_WIDTH // HEAD_DIM
CONV_CH = D_MODEL - SB_WIDTH
CONV_WIDTH = 31
CONV_STATE = CONV_WIDTH - 1
N_C_HEADS = D_MODEL // HEAD_DIM
C_WIDTH = N_C_HEADS * HEAD_DIM
LEFT_CHUNKS = 8
BAND_PAST = LEFT_CHUNKS * CHUNK
REL_CLIP = 128
N_REL = 2 * REL_CLIP + 1
D_FF = 4 * D_MODEL
Q_BLOCK = 128
N_AB = (DEPTH + 1) // 2
N_C = DEPTH // 2
AB_IN = 3 * SB_WIDTH + 2 * CONV_CH
C_IN = 3 * C_WIDTH
RMS_EPS = 1e-6
LN_EPS = 1e-5

kernel_name = 'hybrid_streaming_sb_conv_chunkband_step'


def rmsnorm(x, g):
    xf = x.astype(jnp.float32)
    r = lax.rsqrt(jnp.mean(xf * xf, axis=-1, keepdims=True) + RMS_EPS)
    return (xf * r).astype(x.dtype) * g


def layernorm(x, g, b):
    xf = x.astype(jnp.float32)
    mu = jnp.mean(xf, axis=-1, keepdims=True)
    var = jnp.mean(jnp.square(xf - mu), axis=-1, keepdims=True)
    return ((xf - mu) * lax.rsqrt(var + LN_EPS)).astype(x.dtype) * g + b


def sq_relu_mlp(h, w_up, w_down):
    return jnp.square(jax.nn.relu(h @ w_up)) @ w_down


def stick_breaking(q, k, v, q_pos, k_pos):
    z = jnp.einsum('bqhd,bkhd->bhqk', q, k).astype(jnp.float32) * (HEAD_DIM ** -0.5)
    mask = k_pos[None, :] < q_pos[:, None]
    log_keep = jnp.where(mask, jax.nn.log_sigmoid(-z), 0.0)
    between = lax.cumsum(log_keep, axis=3, reverse=True) - log_keep
    w = jnp.where(mask, jnp.exp(jax.nn.log_sigmoid(z) + between), 0.0)
    return jnp.einsum('bhqk,bkhd->bqhd', w.astype(v.dtype), v)


def band_attention(q, k, v, q_pos, k_pos, rel_bias):
    s = jnp.einsum('bqhd,bkhd->bhqk', q, k).astype(jnp.float32) * (HEAD_DIM ** -0.5)
    rel = jnp.clip(q_pos[:, None] - k_pos[None, :], -REL_CLIP, REL_CLIP) + REL_CLIP
    s = s + rel_bias[:, rel].astype(jnp.float32)
    qc = q_pos[:, None] // CHUNK
    kc = k_pos[None, :] // CHUNK
    mask = (k_pos[None, :] >= 0) & (kc <= qc) & (kc >= qc - LEFT_CHUNKS)
    p = jax.nn.softmax(jnp.where(mask, s, -jnp.inf), axis=-1)
    return jnp.einsum('bhqk,bkhd->bqhd', p.astype(v.dtype), v)


def conv_module(u_ext, dw_w, dw_b, ln_g, ln_b):
    y = lax.conv_general_dilated(
        u_ext, dw_w[:, None, :], window_strides=(1,), padding='VALID',
        dimension_numbers=('NWC', 'WIO', 'NWC'), feature_group_count=CONV_CH) + dw_b
    return jax.nn.silu(layernorm(y, ln_g, ln_b))


def ab_project(h, w_in):
    B, T, _ = h.shape
    p = h @ w_in
    q, k, v, a, g = jnp.split(
        p, [SB_WIDTH, 2 * SB_WIDTH, 3 * SB_WIDTH, 3 * SB_WIDTH + CONV_CH], axis=-1)
    heads = lambda t: t.reshape(B, T, N_SB_HEADS, HEAD_DIM)
    return heads(q), heads(k), heads(v), a * jax.nn.sigmoid(g)


def ab_mixer_prompt(h, w_in, w_out, dw_w, dw_b, ln_g, ln_b):
    B, T, _ = h.shape
    q, k, v, u = ab_project(h, w_in)
    k_pos = jnp.arange(T)

    def q_block(b):
        start = b * Q_BLOCK
        qb = lax.dynamic_slice_in_dim(q, start, Q_BLOCK, axis=1)
        return stick_breaking(qb, k, v, start + jnp.arange(Q_BLOCK), k_pos)

    a_out = lax.map(q_block, jnp.arange(T // Q_BLOCK))
    a_out = jnp.moveaxis(a_out, 0, 1).reshape(B, T, SB_WIDTH)
    u_ext = jnp.pad(u, ((0, 0), (CONV_STATE, 0), (0, 0)))
    c_out = conv_module(u_ext, dw_w, dw_b, ln_g, ln_b)
    out = jnp.concatenate([a_out, c_out], axis=-1) @ w_out
    return out, k, v, u[:, T - CONV_STATE:]


def ab_mixer_sample(h, cache_k, cache_v, cache_conv, w_in, w_out, dw_w, dw_b, ln_g, ln_b):
    B, T, _ = h.shape
    P = cache_k.shape[1]
    q, k, v, u = ab_project(h, w_in)
    k_all = jnp.concatenate([cache_k, k], axis=1)
    v_all = jnp.concatenate([cache_v, v], axis=1)
    a_out = stick_breaking(q, k_all, v_all, P + jnp.arange(T), jnp.arange(P + T))
    a_out = a_out.reshape(B, T, SB_WIDTH)
    u_ext = jnp.concatenate([cache_conv, u], axis=1)
    c_out = conv_module(u_ext, dw_w, dw_b, ln_g, ln_b)
    out = jnp.concatenate([a_out, c_out], axis=-1) @ w_out
    return out, k, v, u_ext[:, T:]


def c_project(h, w_in):
    B, T, _ = h.shape
    p = (h @ w_in).reshape(B, T, 3, N_C_HEADS, HEAD_DIM)
    return p[:, :, 0], p[:, :, 1], p[:, :, 2]


def c_mixer_prompt(h, w_in, w_out, rel_bias):
    B, T, _ = h.shape
    q, k, v = c_project(h, w_in)
    pad = ((0, 0), (BAND_PAST, 0), (0, 0), (0, 0))
    kp = jnp.pad(k, pad)
    vp = jnp.pad(v, pad)
    band = BAND_PAST + CHUNK

    def one_chunk(c):
        start = c * CHUNK
        qc = lax.dynamic_slice_in_dim(q, start, CHUNK, axis=1)
        kc = lax.dynamic_slice_in_dim(kp, start, band, axis=1)
        vc = lax.dynamic_slice_in_dim(vp, start, band, axis=1)
        return band_attention(qc, kc, vc, start + jnp.arange(CHUNK),
                              start - BAND_PAST + jnp.arange(band), rel_bias)

    o = lax.map(one_chunk, jnp.arange(T // CHUNK))
    o = jnp.moveaxis(o, 0, 1).reshape(B, T, C_WIDTH)
    keep = min(BAND_PAST, T)
    return o @ w_out, k[:, T - keep:], v[:, T - keep:]


def c_mixer_sample(h, cache_k, cache_v, w_in, w_out, rel_bias):
    B, T, _ = h.shape
    W = cache_k.shape[1]
    q, k, v = c_project(h, w_in)
    k_all = jnp.concatenate([cache_k, k], axis=1)
    v_all = jnp.concatenate([cache_v, v], axis=1)
    o = band_attention(q, k_all, v_all, PAST_LEN + jnp.arange(T),
                       PAST_LEN - W + jnp.arange(W + T), rel_bias)
    return o.reshape(B, T, C_WIDTH) @ w_out, k, v


def setup_inputs(seed: int = 0) -> dict:
    key = jax.random.key(seed)
    ks = jax.random.split(key, 24)
    f32 = jnp.float32
    nrm = lambda k, shape, scale: jax.random.normal(k, shape, f32) * scale
    c_win = min(BAND_PAST, PAST_LEN)
    return {
        'x_prompt': nrm(ks[0], (BATCH, SEQ, D_MODEL), 1.0),
        'x_sample': nrm(ks[1], (DEC_BATCH, DEC_SEQ, D_MODEL), 1.0),
        'cache_sb_k': nrm(ks[2], (N_AB, DEC_BATCH, PAST_LEN, N_SB_HEADS, HEAD_DIM), 1.0),
        'cache_sb_v': nrm(ks[3], (N_AB, DEC_BATCH, PAST_LEN, N_SB_HEADS, HEAD_DIM), 1.0),
        'cache_conv': nrm(ks[4], (N_AB, DEC_BATCH, CONV_STATE, CONV_CH), 0.5),
        'cache_band_k': nrm(ks[5], (N_C, DEC_BATCH, c_win, N_C_HEADS, HEAD_DIM), 1.0),
        'cache_band_v': nrm(ks[6], (N_C, DEC_BATCH, c_win, N_C_HEADS, HEAD_DIM), 1.0),
        'norm_mix': 1.0 + nrm(ks[7], (DEPTH, D_MODEL), 0.02),
        'norm_ffn': 1.0 + nrm(ks[8], (DEPTH, D_MODEL), 0.02),
        'norm_final': 1.0 + nrm(ks[9], (D_MODEL,), 0.02),
        'w_in_ab': nrm(ks[10], (N_AB, D_MODEL, AB_IN), D_MODEL ** -0.5),
        'w_out_ab': nrm(ks[11], (N_AB, SB_WIDTH + CONV_CH, D_MODEL), (SB_WIDTH + CONV_CH) ** -0.5),
        'dw_w': nrm(ks[12], (N_AB, CONV_WIDTH, CONV_CH), CONV_WIDTH ** -0.5),
        'dw_b': nrm(ks[13], (N_AB, CONV_CH), 0.02),
        'conv_ln_g': 1.0 + nrm(ks[14], (N_AB, CONV_CH), 0.02),
        'conv_ln_b': nrm(ks[15], (N_AB, CONV_CH), 0.02),
        'w_in_c': nrm(ks[16], (N_C, D_MODEL, C_IN), D_MODEL ** -0.5),
        'w_out_c': nrm(ks[17], (N_C, C_WIDTH, D_MODEL), C_WIDTH ** -0.5),
        'rel_bias': nrm(ks[18], (N_C, N_C_HEADS, N_REL), 0.5),
        'w_up': nrm(ks[19], (DEPTH, D_MODEL, D_FF), D_MODEL ** -0.5),
        'w_down': nrm(ks[20], (DEPTH, D_FF, D_MODEL), D_FF ** -0.5),
    }


def reference(x_prompt, x_sample, cache_sb_k, cache_sb_v, cache_conv, cache_band_k, cache_band_v,
              norm_mix, norm_ffn, norm_final, w_in_ab, w_out_ab, dw_w, dw_b, conv_ln_g, conv_ln_b,
              w_in_c, w_out_c, rel_bias, w_up, w_down):
    xp, xs = x_prompt, x_sample
    sbk_p, sbv_p, conv_p, bk_p, bv_p = [], [], [], [], []
    sbk_s, sbv_s, conv_s, bk_s, bv_s = [], [], [], [], []
    for layer in range(DEPTH):
        i = layer // 2
        hp = rmsnorm(xp, norm_mix[layer])
        hs = rmsnorm(xs, norm_mix[layer])
        if layer % 2 == 0:
            op, k, v, c = ab_mixer_prompt(hp, w_in_ab[i], w_out_ab[i], dw_w[i], dw_b[i],
                                          conv_ln_g[i], conv_ln_b[i])
            sbk_p.append(k); sbv_p.append(v); conv_p.append(c)
            os_, k, v, c = ab_mixer_sample(hs, cache_sb_k[i], cache_sb_v[i], cache_conv[i],
                                           w_in_ab[i], w_out_ab[i], dw_w[i], dw_b[i],
                                           conv_ln_g[i], conv_ln_b[i])
            sbk_s.append(k); sbv_s.append(v); conv_s.append(c)
        else:
            op, k, v = c_mixer_prompt(hp, w_in_c[i], w_out_c[i], rel_bias[i])
            bk_p.append(k); bv_p.append(v)
            os_, k, v = c_mixer_sample(hs, cache_band_k[i], cache_band_v[i],
                                       w_in_c[i], w_out_c[i], rel_bias[i])
            bk_s.append(k); bv_s.append(v)
        xp = xp + op
        xs = xs + os_
        xp = xp + sq_relu_mlp(rmsnorm(xp, norm_ffn[layer]), w_up[layer], w_down[layer])
        xs = xs + sq_relu_mlp(rmsnorm(xs, norm_ffn[layer]), w_up[layer], w_down[layer])
    y_prompt = rmsnorm(xp, norm_final)
    y_sample = rmsnorm(xs, norm_final)
    new_sb_k_prompt = jnp.stack(sbk_p)
    new_sb_v_prompt = jnp.stack(sbv_p)
    new_conv_prompt = jnp.stack(conv_p)
    new_band_k_prompt = jnp.stack(bk_p)
    new_band_v_prompt = jnp.stack(bv_p)
    new_sb_k_sample = jnp.stack(sbk_s)
    new_sb_v_sample = jnp.stack(sbv_s)
    new_conv_sample = jnp.stack(conv_s)
    new_band_k_sample = jnp.stack(bk_s)
    new_band_v_sample = jnp.stack(bv_s)
    return (y_prompt, y_sample, new_sb_k_prompt, new_sb_v_prompt, new_conv_prompt,
            new_band_k_prompt, new_band_v_prompt, new_sb_k_sample, new_sb_v_sample,
            new_conv_sample, new_band_k_sample, new_band_v_sample)
```

```python
import numpy as np
import concourse.bass as bass
import concourse.mybir as mybir
from concourse.bass_utils import run_bass_kernel_spmd

F32 = mybir.dt.float32
BF16 = mybir.dt.bfloat16
AF = mybir.ActivationFunctionType
ALU = mybir.AluOpType
AX = mybir.AxisListType

EPOCH = 3000
NCORES = 8
RMS_EPS = 1e-6
LN_EPS = 1e-5
NEG = -30000.0


class Res:
    __slots__ = ("name", "last_w", "readers", "excl")

    def __init__(self, name, excl=False):
        self.name = name
        self.excl = excl
        self.last_w = None
        self.readers = []


class DmaGroup:
    def __init__(self, sched, name):
        self.sem = sched.new_sem("dg_" + name)
        self.count = 0


class Sched:
    ENGS = ("pe", "act", "dve", "pool", "sp")

    def __init__(self, nc):
        self.nc = nc
        self.ops = {e: [] for e in self.ENGS}
        self.nops = {e: 0 for e in self.ENGS}
        self.esems = {e: [] for e in self.ENGS}
        self.seen = {e: {} for e in self.ENGS}
        self._semctx = []
        self.groups = []

    def new_sem(self, name):
        ctx = self.nc.semaphore(name)
        s = ctx.__enter__()
        self._semctx.append(ctx)
        return s

    def group(self, name):
        g = DmaGroup(self, name)
        self.groups.append(g)
        return g

    def _eng_ticket(self, eng):
        n = self.nops[eng]
        ep, v = divmod(n, EPOCH)
        while len(self.esems[eng]) <= ep:
            self.esems[eng].append(self.new_sem(f"e_{eng}_{len(self.esems[eng])}"))
        self.nops[eng] = n + 1
        return (self.esems[eng][ep], v + 1, eng)

    def op(self, eng, fn, reads=(), writes=(), dma=None):
        deps = []
        reads = [r for r in reads if r is not None]
        writes = [w for w in writes if w is not None]
        ex = [r for r in reads if r.excl]
        if ex:
            reads = [r for r in reads if not r.excl]
            writes = list(writes) + [r for r in ex if r not in writes]
        for r in reads:
            if r.last_w is not None:
                deps.append(r.last_w)
        for w in writes:
            if w.last_w is not None:
                deps.append(w.last_w)
            deps.extend(w.readers)
        seen = self.seen[eng]
        best = {}
        for (sem, val, src) in deps:
            if src == eng and eng == "pe":
                continue
            k = id(sem)
            if seen.get(k, 0) >= val:
                continue
            if k not in best or best[k][1] < val:
                best[k] = (sem, val)
        for k, (sem, val) in best.items():
            seen[k] = val
        waits = list(best.values())
        if dma is not None:
            dma.count += 16
            ticket = (dma.sem, dma.count, "dma")
            inc = (dma.sem, 16)
        else:
            ticket = self._eng_ticket(eng)
            inc = (ticket[0], 1)
        self.ops[eng].append((waits, fn, inc))
        for r in reads:
            r.readers.append(ticket)
        for w in writes:
            w.last_w = ticket
            w.readers = []
        return ticket

    def barrier(self):
        waits = []
        for e in self.ENGS:
            n = self.nops[e]
            if n > 0:
                ep, v = divmod(n - 1, EPOCH)
                waits.append((self.esems[e][ep], v + 1))
        for g in self.groups:
            if g.count > 0:
                waits.append((g.sem, g.count))
        for e in self.ENGS:
            t = self._eng_ticket(e)
            self.ops[e].append((list(waits), lambda eng: eng.nop(), (t[0], 1)))
            for sem, val in waits:
                k = id(sem)
                if self.seen[e].get(k, 0) < val:
                    self.seen[e][k] = val

    def emit(self):
        nc = self.nc
        fin = [(g.sem, g.count) for g in self.groups if g.count > 0]
        self.ops["sp"].append((fin, lambda e: e.nop(), None))
        engmap = {"pe": "tensor", "act": "scalar", "dve": "vector", "pool": "gpsimd", "sp": "sync"}
        with nc.Block() as block:
            for e in self.ENGS:
                ops = self.ops[e]

                def body(eng, ops=ops):
                    for waits, fn, inc in ops:
                        for sem, val in waits:
                            eng.wait_ge(sem, val)
                        ins = fn(eng)
                        if inc is not None:
                            ins.then_inc(inc[0], inc[1])
                getattr(block, engmap[e])(body)


N_WT = 49
import os as _os
SB_SLOTS = int(_os.environ.get('SB_SLOTS', '4'))
STG_SB4 = int(_os.environ.get('STG_SB4', '2'))
SB4_DUMMY = int(_os.environ.get('SB4_DUMMY', '1'))
SB_DUMMY = int(_os.environ.get('SB_DUMMY', '1'))
CONV_DUMMY = int(_os.environ.get('CONV_DUMMY', '0'))
BAND_DUMMY = int(_os.environ.get('BAND_DUMMY', '0'))
BAND_DUMMY_N = int(_os.environ.get('BAND_DUMMY_N', '512'))
SB_DUMMY_N = int(_os.environ.get('SB_DUMMY_N', '512'))
STG_SB = int(_os.environ.get('STG_SB', '1'))
STG_BAND = int(_os.environ.get('STG_BAND', '0'))
STG_SBS = int(_os.environ.get('STG_SBS', '0'))
DEBUG_STOP = None


class StopBuild(Exception):
    pass


DUMP = [None]
MARKS = []


def stop(k):
    if DEBUG_STOP == k:
        if DUMP[0] is not None:
            DUMP[0]()
        raise StopBuild()


def build_program():
    nc = bass.Bass("TRN2", target_bir_lowering=False, dynamic_dma_scratch_size=512)
    S = Sched(nc)

    def din(name, shape, dt=F32):
        return nc.dram_tensor(name, list(shape), dt, kind="ExternalInput").ap()

    def dout(name, shape, dt=F32):
        return nc.dram_tensor(name, list(shape), dt, kind="ExternalOutput").ap()

    xp = din("xp", [2, 2048, 1024])
    xs = din("xs", [64, 1024])
    csk = din("csk", [4, 2048, 512])
    csv = din("csv", [4, 2048, 512])
    cconv = din("cconv", [4, 30, 512])
    cbk = din("cbk", [4, 512, 1024])
    cbv = din("cbv", [4, 512, 1024])
    norm_mix = din("norm_mix", [2, 1024])
    norm_ffn = din("norm_ffn", [2, 1024])
    norm_final = din("norm_final", [1024])
    w_in_ab = din("w_in_ab", [1024, 2560])
    w_out_ab = din("w_out_ab", [1024, 1024])
    dw_w = din("dw_w", [31, 512])
    dw_b = din("dw_b", [512])
    ln_g = din("ln_g", [512])
    ln_b = din("ln_b", [512])
    w_in_c = din("w_in_c", [1024, 3072])
    w_out_c = din("w_out_c", [1024, 1024])
    bfull = din("bfull", [16, 128, 640])
    w_up = din("w_up", [2, 1024, 4096])
    w_down = din("w_down", [2, 4096, 1024])

    yp = dout("yp", [2, 2048, 1024])
    ys = dout("ys", [64, 1024])
    o_sbk_p = dout("o_sbk_p", [2, 2048, 512])
    o_sbv_p = dout("o_sbv_p", [2, 2048, 512])
    o_conv_p = dout("o_conv_p", [2, 30, 512])
    o_bk_p = dout("o_bk_p", [2, 512, 1024])
    o_bv_p = dout("o_bv_p", [2, 512, 1024])
    o_sbk_s = dout("o_sbk_s", [64, 512])
    o_sbv_s = dout("o_sbv_s", [64, 512])
    o_conv_s = dout("o_conv_s", [4, 30, 512])
    o_bk_s = dout("o_bk_s", [64, 1024])
    o_bv_s = dout("o_bv_s", [64, 1024])

    wsc = nc.dram_tensor("wsc", [N_WT, 128, 4096], BF16, kind="Internal").ap()

    cur = [1024]

    def alloc(name, shape, dt, at=None):
        esz = 4 if dt == F32 else 2
        n = 1
        for d in shape[1:]:
            n *= d
        nbytes = (n * esz + 31) // 32 * 32
        if at is None:
            at = cur[0]
            cur[0] += nbytes
        assert at + nbytes <= 229376, (name, at, nbytes)
        return nc.alloc_sbuf_tensor_at(name, list(shape), dt, offset=at)

    wring = [alloc(f"wring{i}", [128, 4096], BF16) for i in range(3)]
    xtok = alloc("xtok", [128, 4, 1024], F32)
    kT0 = alloc("kT0", [128, 4, 2048], BF16)
    v0 = alloc("v0", [128, 17, 512], BF16)
    KT1_BASE = cur[0]
    kT1 = [alloc(f"kT1_{i}", [128, 8, 512], BF16) for i in range(2)]
    v1 = [alloc(f"v1_{i}", [128, 4, 1024], BF16) for i in range(2)]
    v1s = alloc("v1s", [16, 1024], BF16)
    assert cur[0] >= KT1_BASE + 32768
    BB_BASE = cur[0]
    Bb = alloc("Bb", [128, 16, 640], BF16)
    uext = alloc("uext", [128, 4, 542], BF16)
    Dring = [alloc(f"D{i}", [128, 31, 128], BF16) for i in range(2)]
    identb = alloc("identb", [128, 128], BF16)
    identf = alloc("identf", [128, 128], F32)
    Tst = alloc("Tst", [128, 128], BF16)
    onesb = alloc("onesb", [128, 128], BF16)
    Umat = alloc("Umat", [128, 128], BF16)
    zerosb = alloc("zerosb", [128, 512], BF16)
    maskU = alloc("maskU", [128, 128], BF16)
    maskS = alloc("maskS", [16, 128], BF16)
    gT = alloc("gT", [128, 4, 8], F32)
    gfin = alloc("gfin", [128, 1024], F32)
    dwb_bc = alloc("dwb_bc", [128, 512], F32)
    lng_bc = alloc("lng_bc", [128, 512], F32)
    lnb_bc = alloc("lnb_bc", [128, 512], F32)
    dwT = alloc("dwT", [128, 4, 31], F32)
    stat = alloc("stat", [128, 64], F32)
    rs16 = alloc("rs16", [128, 32], F32)
    ARENA = cur[0]
    ARENA_SZ = 229376 - ARENA

    def ar(name, shape, dt, off):
        return alloc(name, shape, dt, at=ARENA + off)

    hT = ar("hT", [128, 8, 512], BF16, 0)
    qT0 = ar("qT0", [128, 4, 512], BF16, 8192)
    sg = ar("sg", [128, 4, 512], F32, 12288)
    aoT = ar("aoT", [128, 8, 512], BF16, 20480)
    cT = ar("cT", [128, 4, 512], BF16, 28672)
    SBT = [(ar("sbEA0", [128, 512], F32, 32768), ar("sbL20", [128, 512], F32, 34816),
            ar("sbLb0", [128, 512], BF16, 36864), ar("sbWb0", [128, 512], BF16, 37888)),
           (ar("sbEA1", [128, 512], F32, 12288), ar("sbL21", [128, 512], F32, 14336),
            ar("sbLb1", [128, 512], BF16, 16384), ar("sbWb1", [128, 512], BF16, 17408)),
           (ar("sbEA2", [128, 512], F32, 0), ar("sbL22", [128, 512], F32, 2048),
            ar("sbLb2", [128, 512], BF16, 4096), ar("sbWb2", [128, 512], BF16, 5120))]
    SBT.append((ar("sbEA3", [128, 512], F32, 40960), ar("sbL23", [128, 512], F32, 43008),
                ar("sbLb3", [128, 512], BF16, 45056), ar("sbWb3", [128, 512], BF16, 46080)))
    Eb = ar("Eb", [128, 512], F32, 32768)
    junk = ar("junk", [128, 1024], BF16, 32768)
    L2b = ar("L2b", [128, 512], F32, 34816)
    Ab = ar("Ab", [128, 512], F32, 36864)
    Lbb = ar("Lbb", [128, 512], BF16, 38912)
    Wbb = ar("Wbb", [128, 512], BF16, 39936)
    kst = ar("kst", [128, 512], F32, 40960)
    vst = ar("vst", [128, 512], F32, 43008)
    stg = ar("stg", [128, 1024], F32, 45056)
    ysb = ar("ysb", [128, 512], F32, 49152)
    cst = ar("cst", [32, 512], F32, 49152)
    ctm = ar("ctm", [128, 512], BF16, 51200)
    ysb2 = ar("ysb2", [128, 512], F32, 45056)
    ctm2 = ar("ctm2", [128, 512], BF16, 45056 + 2048)
    hn = ar("hn", [128, 1024], BF16, 52224)
    hn2 = ar("hn2", [128, 1024], BF16, 61440)
    utail = ar("utail", [128, 4, 64], F32, 54272)
    uexts = ar("uexts", [128, 4, 4, 46], BF16, 55296)
    kTs = ar("kTs", [128, 4, 64], BF16, 56832)
    L0_END = 57344
    hidT = ar("hidT", [128, 32, 512], BF16, 8192)
    rl = [ar(f"rl{i}", [128, 512], F32, 40960 + 2048 * i) for i in range(2)]
    qT1 = ar("qT1", [128, 8, 512], BF16, 8192)
    oT = ar("oT", [128, 8, 512], BF16, 16384)
    Sp = [ar(f"Sp{i}", [128, 640], F32, 24576 + 2560 * i) for i in range(2)]
    Pe = [ar(f"Pe{i}", [128, 640], BF16, 29696 + 1280 * i) for i in range(2)]
    PT = [ar(f"PT{i}", [128, 5, 128], BF16, 32256 + 1280 * i) for i in range(2)]
    osb = ar("osb", [128, 1024], BF16, 34816)
    kst1 = ar("kst1", [128, 1024], F32, 36864)
    vst1 = ar("vst1", [128, 1024], F32, 40960)
    kT1s = ar("kT1s", [128, 8, 64], BF16, 54272)
    assert 63488 <= ARENA_SZ, (ARENA, ARENA_SZ)
    kT0b = alloc("kT0b", [128, 4, 2048], BF16, at=KT1_BASE)
    v0b = alloc("v0b", [128, 17, 512], BF16, at=KT1_BASE + 16384)
    cvf = [alloc("cvf0", [128, 4096], F32, at=KT1_BASE), alloc("cvf1", [128, 4096], F32, at=KT1_BASE + 16384)]
    cvb = [alloc("cvb0", [128, 4096], BF16, at=BB_BASE), alloc("cvb1", [128, 4096], BF16, at=BB_BASE + 8192)]
    vnew0 = ar("vnew0", [16, 4, 512], BF16, 57344)
    vnew1 = ar("vnew1", [16, 4, 1024], BF16, 55296)

    ps = nc.alloc_psum_tensor("ps", [128, 8, 512], F32)
    PB = [Res(f"psb{i}", excl=True) for i in range(8)]

    def psf(b):
        return ps[:, b, :]

    def psb16(b):
        return ps[:, b, :].bitcast(BF16)

    R = {}

    def res(name):
        if name not in R:
            R[name] = Res(name)
        return R[name]


    dcount = [0]

    def dma(out, in_, reads=(), writes=(), grp=None, slow=False):
        if grp is None:
            grp = S.group(f"d{dcount[0]}")
            dcount[0] += 1
        if slow:
            f = lambda e: e.dma_start(out=out, in_=in_, allow_slow_non_contiguous=True)
        else:
            f = lambda e: e.dma_start(out=out, in_=in_)
        return op("sp", f, reads=reads, writes=writes, dma=grp)

    G = {}

    def grp(name):
        if name not in G:
            G[name] = S.group(name)
        return G[name]

    def setup_consts():
        op("pool", lambda e: e.memset(identb[:], 1.0), writes=[res("identb")])
        op("pool", lambda e: e.affine_select(out=identb[:], in_=identb[:], pattern=[[-1, 128]], compare_op=ALU.is_equal,
                                             fill=0.0, base=0, channel_multiplier=1), writes=[res("identb")])
        op("pool", lambda e: e.memset(identf[:], 1.0), writes=[res("identf")])
        op("pool", lambda e: e.affine_select(out=identf[:], in_=identf[:], pattern=[[-1, 128]], compare_op=ALU.is_equal,
                                             fill=0.0, base=0, channel_multiplier=1), writes=[res("identf")])
        op("pool", lambda e: e.memset(Tst[:], 1.0), writes=[res("Tst")])
        op("pool", lambda e: e.affine_select(out=Tst[:], in_=Tst[:], pattern=[[-1, 128]], compare_op=ALU.is_gt,
                                             fill=0.0, base=0, channel_multiplier=1), writes=[res("Tst")])
        op("pool", lambda e: e.memset(maskU[:], 1.0), writes=[res("maskU")])
        op("pool", lambda e: e.affine_select(out=maskU[:], in_=maskU[:], pattern=[[1, 128]], compare_op=ALU.is_gt,
                                             fill=0.0, base=0, channel_multiplier=-1), writes=[res("maskU")])
        op("pool", lambda e: e.memset(maskS[:], 1.0), writes=[res("maskS")])
        op("pool", lambda e: e.affine_select(out=maskS[:], in_=maskS[:], pattern=[[0, 8], [1, 16]], compare_op=ALU.is_gt,
                                             fill=0.0, base=0, channel_multiplier=-1), writes=[res("maskS")])
        op("pool", lambda e: e.memset(onesb[:], 1.0), writes=[res("onesb")])
        op("pool", lambda e: e.memset(Umat[:], 1.0), writes=[res("Umat")])
        op("pool", lambda e: e.affine_select(out=Umat[:], in_=Umat[:], pattern=[[1, 128]], compare_op=ALU.is_ge,
                                             fill=0.0, base=0, channel_multiplier=-1), writes=[res("Umat")])
        op("pool", lambda e: e.memset(zerosb[:], 0.0), writes=[res("zerosb")])
        op("pool", lambda e: e.memset(uext[:], 0.0), writes=[res("uext")])
        for n, src in enumerate([norm_mix[0], norm_ffn[0], norm_mix[1], norm_ffn[1]]):
            dma(gT[:, n, :], src.rearrange("(c p) -> p c", p=128), writes=[res("gT")], grp=grp("gT"), slow=True)
        dma(gfin[:], norm_final.partition_broadcast(128), writes=[res("gfin")], grp=grp("gfin"))
        dma(dwb_bc[:], dw_b.partition_broadcast(128), writes=[res("cb")], grp=grp("cb"))
        dma(lng_bc[:], ln_g.partition_broadcast(128), writes=[res("cb")], grp=grp("cb"))
        dma(lnb_bc[:], ln_b.partition_broadcast(128), writes=[res("cb")], grp=grp("cb"))
        dma(stg[0:31, 0:512], dw_w, writes=[res("stg")], grp=grp("stg"))
        for c in range(4):
            op("pe", lambda e, c=c: e.transpose(out=ps[:, 0, c * 32:c * 32 + 31], in_=stg[0:31, c * 128:(c + 1) * 128],
                                                identity=identf[0:31, 0:31]),
               reads=[res("stg"), res("identf")], writes=[PB[0]])
        op("dve", lambda e: e.tensor_copy(out=dwT[:], in_=ps[:, 0, 0:128].rearrange("p (c i) -> p c i", c=4)[:, :, 0:31]),
           reads=[PB[0]], writes=[res("dwT")])
    def setup_bias():
        for h in range(16):
            dma(stg[:, 0:640], bfull[h], writes=[res("stg")], grp=grp("stg"))
            eng = "act" if h % 2 == 0 else "dve"
            if eng == "act":
                op("act", lambda e, h=h: e.copy(out=Bb[:, h, :], in_=stg[:, 0:640]), reads=[res("stg")], writes=[res("Bb")])
            else:
                op("dve", lambda e, h=h: e.tensor_copy(out=Bb[:, h, :], in_=stg[:, 0:640]), reads=[res("stg")], writes=[res("Bb")])
        op("pool", lambda e: e.memset(Bb[0:64, :, 576:640], NEG), writes=[res("Bb")])
        op("pool", lambda e: e.memset(Bb[64:128, :, 0:64], NEG), writes=[res("Bb")])

    def wtile_specs():
        sp = []

        def colt(w, c0):
            return (w[:, c0:c0 + 512].rearrange("(k p) c -> p k c", p=128), 128, 8)
        for c0 in (2048, 1536, 0, 512, 1024):
            sp.append(colt(w_in_ab, c0))
        for o in range(2):
            sp.append((w_out_ab[0:512, o * 512:(o + 1) * 512].rearrange("(h p) c -> p h c", p=64), 64, 8))
            sp.append((w_out_ab[512:1024, o * 512:(o + 1) * 512].rearrange("(k p) c -> p k c", p=128), 128, 4))
        for l in range(2):
            if l == 1:
                for c0 in range(0, 3072, 512):
                    sp.append(colt(w_in_c, c0))
                for o in range(2):
                    sp.append(colt(w_out_c, o * 512))
            for j in range(8):
                sp.append(colt(w_up[l], j * 512))
            for o in range(2):
                for g in range(4):
                    sp.append((w_down[l][g * 1024:(g + 1) * 1024, o * 512:(o + 1) * 512].rearrange("(k p) c -> p k c", p=128), 128, 8))
        assert len(sp) == N_WT
        return sp

    WSPEC = wtile_specs()

    def convert_gen():
        def load(n):
            src, P, K = WSPEC[n]
            i = n % 2
            fv = cvf[i][0:P, 0:K * 512].rearrange("p (k c) -> p k c", k=K)
            dma(fv, src, writes=[res(f"cvf{i}")], grp=grp(f"cvf{i}"))

        def cast_store(n):
            src, P, K = WSPEC[n]
            i = n % 2
            eng = ("act", "dve", "pool")[n % 3]
            if eng == "act":
                op("act", lambda e: e.copy(out=cvb[i][0:P, 0:K * 512], in_=cvf[i][0:P, 0:K * 512]),
                   reads=[res(f"cvf{i}")], writes=[res(f"cvb{i}")])
            else:
                op(eng, lambda e: e.tensor_copy(out=cvb[i][0:P, 0:K * 512], in_=cvf[i][0:P, 0:K * 512]),
                   reads=[res(f"cvf{i}")], writes=[res(f"cvb{i}")])
            dma(wsc[n, 0:P, 0:K * 512], cvb[i][0:P, 0:K * 512], reads=[res(f"cvb{i}")], writes=[res(f"wsc{n}")],
                grp=grp(f"cvb{i}"))
        load(0)
        for n in range(N_WT):
            if n + 1 < N_WT:
                load(n + 1)
            cast_store(n)
            yield

    BG = {"gen": None, "busy": False, "every": 45, "cnt": 0}

    def bg_advance(k=1):
        if BG["gen"] is None or BG["busy"]:
            return
        BG["busy"] = True
        try:
            for _ in range(k):
                next(BG["gen"])
        except StopIteration:
            BG["gen"] = None
        BG["busy"] = False

    def bg_flush():
        while BG["gen"] is not None:
            bg_advance(1)

    _raw_op = S.op

    def op(eng, fn, reads=(), writes=(), dma=None):
        t = _raw_op(eng, fn, reads=reads, writes=writes, dma=dma)
        if BG["gen"] is not None and not BG["busy"]:
            BG["cnt"] += 1
            if BG["cnt"] >= BG["every"]:
                BG["cnt"] = 0
                bg_advance(1)
        return t

    class WStream:
        def __init__(self):
            self.issued = 0
            self.taken = 0

        def _issue(self):
            n = self.issued
            t = n % N_WT
            slot = n % 3
            _, P, K = WSPEC[t]
            dma(wring[slot][0:P, 0:K * 512], wsc[t, 0:P, 0:K * 512], reads=[res(f"wsc{t}")],
                writes=[res(f"wring{slot}")], grp=grp(f"wring{slot}"))
            if P == 64:
                dma(wring[slot][64:128, 0:K * 512], wsc[t, 0:P, 0:K * 512], reads=[res(f"wsc{t}")],
                    writes=[res(f"wring{slot}")], grp=grp(f"wring{slot}"))
            self.issued += 1

        def next(self, total):
            while self.issued < min(self.taken + 2, total):
                self._issue()
            n = self.taken
            self.taken += 1
            slot = n % 3
            _, P, K = WSPEC[n % N_WT]
            P = 128 if P == 64 else P
            return wring[slot][0:P, 0:K * 512].rearrange("p (k c) -> p k c", k=K), res(f"wring{slot}")

    WS = WStream()
    TOTAL_TILES = 9 * N_WT

    def wnext():
        return WS.next(TOTAL_TILES)

    bank_rr = [0]

    BANKSEL = [None]

    def bank4():
        sel = BANKSEL[0] or (0, 1, 2, 3)
        b = sel[bank_rr[0] % len(sel)]
        bank_rr[0] += 1
        return b

    def evac_copy(eng, out, in_, reads, writes, scale=None):
        if eng == "act":
            if scale is None:
                op("act", lambda e: e.copy(out=out, in_=in_), reads=reads, writes=writes)
            else:
                op("act", lambda e: e.activation(out=out, in_=in_, func=AF.Copy, scale=scale), reads=reads, writes=writes)
        else:
            if scale is None:
                op(eng, lambda e: e.tensor_copy(out=out, in_=in_), reads=reads, writes=writes)
            else:
                op(eng, lambda e: e.tensor_scalar(out=out, in0=in_, scalar1=scale, scalar2=None, op0=ALU.mult),
                   reads=reads, writes=writes)

    def rmsnorm(P, NT, gidx, final=False):
        for t in range(NT):
            op("act", lambda e, t=t: e.activation(out=junk[0:P, :], in_=xtok[0:P, t, :], func=AF.Square,
                                                  accum_out=stat[0:P, t:t + 1]),
               reads=[res("xtok")], writes=[res(f"nss{t}")])
        op("act", lambda e: e.activation(out=stat[0:P, 8:8 + NT], in_=stat[0:P, 0:NT], func=AF.Sqrt,
                                         scale=1.0 / 1024.0, bias=RMS_EPS),
           reads=[res(f"nss{t}") for t in range(NT)], writes=[res("nsd")])
        op("dve", lambda e: e.reciprocal(out=stat[0:P, 16:16 + NT], in_=stat[0:P, 8:8 + NT]),
           reads=[res("nsd")], writes=[res("nrs")])
        if final:
            for t in range(NT):
                op("dve", lambda e, t=t: e.scalar_tensor_tensor(out=xtok[0:P, t, :], in0=xtok[0:P, t, :],
                                                                scalar=stat[0:P, 16 + t:17 + t], in1=gfin[0:P, :],
                                                                op0=ALU.mult, op1=ALU.mult),
                   reads=[res("nrs"), res("gfin")], writes=[res("xtok")])
            return
        hbuf = [hn, hn2]

        def scale(t):
            op("dve", lambda e: e.tensor_scalar(out=hbuf[t % 2][0:P, :], in0=xtok[0:P, t, :], scalar1=stat[0:P, 16 + t:17 + t],
                                                scalar2=None, op0=ALU.mult),
               reads=[res("xtok"), res("nrs")], writes=[res(f"hn{t % 2}")])

        def tr_evac(t):
            bk = 4 + (t % 2)
            for kc in range(8):
                op("pe", lambda e, kc=kc: e.transpose(out=psb16(bk)[:, kc * P:(kc + 1) * P],
                                                      in_=hbuf[t % 2][0:P, kc * 128:(kc + 1) * 128], identity=identb[0:P, 0:P]),
                   reads=[res(f"hn{t % 2}"), res("identb")], writes=[PB[bk]])
            op("dve", lambda e: e.tensor_tensor(
                out=hT[:, :, t * P:(t + 1) * P],
                in0=psb16(bk)[:, 0:8 * P].rearrange("p (k j) -> p k j", k=8),
                in1=gT[:, gidx, :].unsqueeze(2).broadcast_to([128, 8, P]), op=ALU.mult),
               reads=[PB[bk], res("gT")], writes=[res("hT")])
        scale(0)
        if NT > 1:
            scale(1)
        for t in range(NT):
            tr_evac(t)
            if t + 2 < NT:
                scale(t + 2)

    def proj_fm(wt, wres, col0, nchunks, N, evac):
        for c in range(nchunks):
            b = bank4()
            for kc in range(8):
                op("pe", lambda e, c=c, kc=kc, b=b: e.matmul(ps[:, b, 0:N], lhsT=wt[:, kc, col0 + c * 128:col0 + (c + 1) * 128],
                                                             rhs=hT[:, kc, 0:N], start=(kc == 0), stop=(kc == 7)),
                   reads=[wres, res("hT")], writes=[PB[b]])
            evac(c, b)

    def proj_fm_g(wt, wres, col0, nchunks, N, evac):
        for c in range(nchunks):
            b = bank4()
            for kc in range(8):
                op("pe", lambda e, c=c, kc=kc, b=b: e.matmul(ps[:, b, 0:N], lhsT=wt[:, kc, col0 + c * 128:col0 + (c + 1) * 128],
                                                             rhs=hT[:, kc, 0:N], start=(kc == 0), stop=(kc == 7)),
                   reads=[wres, res("hT")], writes=[PB[b]])
            evac(c, b)
            yield

    def proj_tm_g(wt, wres, tiles, evac):
        for ti, (lo, P) in enumerate(tiles):
            b = bank4()
            for kc in range(8):
                op("pe", lambda e, kc=kc, b=b, lo=lo, P=P: e.matmul(ps[0:P, b, :], lhsT=hT[:, kc, lo:lo + P], rhs=wt[:, kc, :],
                                                                    start=(kc == 0), stop=(kc == 7)),
                   reads=[wres, res("hT")], writes=[PB[b]])
            evac(ti, lo, P, b)
            yield

    def proj_tm(wt, wres, tiles, evac):
        for ti, (lo, P) in enumerate(tiles):
            b = bank4()
            for kc in range(8):
                op("pe", lambda e, kc=kc, b=b, lo=lo, P=P: e.matmul(ps[0:P, b, :], lhsT=hT[:, kc, lo:lo + P], rhs=wt[:, kc, :],
                                                                    start=(kc == 0), stop=(kc == 7)),
                   reads=[wres, res("hT")], writes=[PB[b]])
            evac(ti, lo, P, b)

    def run_pool(factories, nslots, stagger=0, filler=None):
        pending = list(factories)
        active = {}
        rnd = 0
        while pending or active:
            for slot in range(nslots):
                if slot not in active and pending and rnd >= slot * stagger:
                    active[slot] = pending.pop(0)(slot)
            for slot in list(active):
                try:
                    next(active[slot])
                except StopIteration:
                    del active[slot]
            rnd += 1
            if filler is not None and active:
                filler(rnd)

    def sb_banks(slot, n_, three):
        if three == 4:
            return slot % 2, 2 + slot, 6 + slot % 2, (0, 0, 64, 64)[slot]
        if three:
            return slot, 3 + slot, (6, 7, 6)[slot], (0, 0, 64)[slot]
        return slot, 4 + slot, 6 + slot, 0

    def sb_head(slot, steps, evac, split=False, three=False):
        EA, L2t, Lbt, Wbt = SBT[slot]
        if slot == 3:
            rEA, rL2, rLb, rWb = res("kst"), res("vst"), res("stg"), res("stg")
        else:
            rEA, rL2, rLb, rWb = [res(f"sb{n}{slot}") for n in ("EA", "L2", "Lb", "Wb")]
        _, Xb, Ob, opo = sb_banks(slot, 0, three)
        op("pe", lambda e: e.matmul(ps[:, Xb, :], lhsT=zerosb[:, 0:128], rhs=zerosb[:, 0:512], start=True, stop=True),
           reads=[res("zerosb")], writes=[PB[Xb]])
        op("pe", lambda e: e.matmul(ps[opo:opo + 64, Ob, :], lhsT=zerosb[:, 0:64], rhs=zerosb[:, 0:512], start=True, stop=True),
           reads=[res("zerosb")], writes=[PB[Ob]])
        yield
        for n_, stp in enumerate(steps):
            yield from sb_one_step(slot, n_, stp, split, EA, L2t, Lbt, Wbt, rEA, rL2, rLb, rWb, Xb, Ob, opo, three)
        evac(Ob, opo)
        yield

    def sb_one_step(slot, n_, stp, split, EA, L2t, Lbt, Wbt, rEA, rL2, rLb, rWb, Xb, Ob, opo, three):
        if True:
            KK, zparts, c0, N, mask, vparts, kres, vres = stp
            if split:
                zb0 = 2 * slot
                zv = ps[0:KK, zb0:zb0 + 2, 0:64]
                zres = [PB[zb0], PB[zb0 + 1]]
                v3 = lambda ap: ap.rearrange("p (a b) -> p a b", a=2)
            else:
                zb0 = sb_banks(slot, n_, three)[0]
                zv = ps[0:KK, zb0, c0:N]
                zres = [PB[zb0]]
                v3 = lambda ap: ap
            for (l, r, zsel, a_, b_) in zparts:
                bk = zb0 + zsel
                op("pe", lambda e, l=l, r=r, bk=bk, a_=a_, b_=b_: e.matmul(ps[0:KK, bk, a_:b_], lhsT=l, rhs=r, start=True, stop=True),
                   reads=list(kres) + [res("qT0")], writes=[PB[bk]])
            yield
            op("act", lambda e: e.activation(out=v3(EA[0:KK, c0:N]), in_=zv, func=AF.Exp, scale=-1.0), reads=zres, writes=[rEA])
            yield
            op("act", lambda e: e.activation(out=L2t[0:KK, c0:N], in_=EA[0:KK, c0:N], func=AF.Ln, bias=1.0), reads=[rEA], writes=[rL2])
            yield
            op("dve", lambda e: e.tensor_tensor(out=v3(Lbt[0:KK, c0:N]), in0=v3(L2t[0:KK, c0:N]), in1=zv, op=ALU.add),
               reads=[rL2] + zres, writes=[rLb])
            yield
            if mask is not None:
                m_ap, m0, m1 = mask
                op("pool", lambda e: e.tensor_tensor(out=Lbt[0:KK, m0:m1], in0=Lbt[0:KK, m0:m1], in1=m_ap, op=ALU.mult),
                   reads=[res("maskU"), res("maskS")], writes=[rLb])
                yield
            op("pe", lambda e: e.matmul(ps[0:KK, Xb, c0:N], lhsT=Tst[0:KK, 0:KK], rhs=Lbt[0:KK, c0:N], start=False, stop=True,
                                        skip_group_check=True),
               reads=[res("Tst"), rLb], writes=[PB[Xb]])
            yield
            op("dve", lambda e: e.tensor_tensor(out=EA[0:KK, c0:N], in0=L2t[0:KK, c0:N], in1=ps[0:KK, Xb, c0:N], op=ALU.add),
               reads=[rL2, PB[Xb]], writes=[rEA])
            yield
            op("act", lambda e: e.activation(out=Wbt[0:KK, c0:N], in_=EA[0:KK, c0:N], func=AF.Exp, scale=-1.0), reads=[rEA], writes=[rWb])
            yield
            if mask is not None:
                m_ap, m0, m1 = mask
                op("pool", lambda e: e.tensor_tensor(out=Wbt[0:KK, m0:m1], in0=Wbt[0:KK, m0:m1], in1=m_ap, op=ALU.mult),
                   reads=[res("maskU"), res("maskS")], writes=[rWb])
                yield
            for (vl, a_, b_) in vparts:
                op("pe", lambda e, vl=vl, a_=a_, b_=b_: e.matmul(ps[opo:opo + 64, Ob, a_:b_], lhsT=vl, rhs=Wbt[0:KK, a_:b_], start=False, stop=True,
                                                                 skip_group_check=True),
                   reads=list(vres) + [rWb], writes=[PB[Ob]])
            op("pe", lambda e: e.matmul(ps[:, Xb, c0:N], lhsT=Umat[0:KK, :], rhs=Lbt[0:KK, c0:N], start=False, stop=True,
                                        skip_group_check=True),
               reads=[res("Umat"), rLb], writes=[PB[Xb]])
            if three == 4:
                for _ in range(SB4_DUMMY):
                    op("pe", lambda e: e.matmul(ps[:, Xb, 0:512], lhsT=zerosb[:, 0:128], rhs=zerosb[:, 0:512], start=False, stop=True,
                                                skip_group_check=True),
                       reads=[res("zerosb")], writes=[PB[Xb]])
            yield

    def conv_gen(tiles):
        for c in range(4):
            D = Dring[c % 2]
            dres = res(f"D{c % 2}")
            op("pool", lambda e, c=c, D=D: e.tensor_tensor(out=D[:], in0=identb[:].unsqueeze(1).broadcast_to([128, 31, 128]),
                                                           in1=dwT[:, c, :].unsqueeze(2).broadcast_to([128, 31, 128]), op=ALU.mult),
               reads=[res("identb"), res("dwT")], writes=[dres])
            for ti, (ufn, P, lo) in enumerate(tiles):
                for i in range(31):
                    op("pe", lambda e, ti=ti, ufn=ufn, P=P, c=c, i=i, D=D: e.matmul(
                        ps[0:P, ti, c * 128:(c + 1) * 128], lhsT=ufn(c, i, P), rhs=D[:, i, :], start=(i == 0), stop=(i == 30)),
                       reads=[dres, res("uext")], writes=[PB[ti]])
                yield

    def conv_module(tiles):
        nt = len(tiles)
        for c in range(4):
            D = Dring[c % 2]
            dres = res(f"D{c % 2}")
            op("pool", lambda e, c=c, D=D: e.tensor_tensor(out=D[:], in0=identb[:].unsqueeze(1).broadcast_to([128, 31, 128]),
                                                           in1=dwT[:, c, :].unsqueeze(2).broadcast_to([128, 31, 128]), op=ALU.mult),
               reads=[res("identb"), res("dwT")], writes=[dres])
            for ti, (ufn, P, lo) in enumerate(tiles):
                for i in range(31):
                    op("pe", lambda e, ti=ti, ufn=ufn, P=P, c=c, i=i, D=D: e.matmul(
                        ps[0:P, ti, c * 128:(c + 1) * 128], lhsT=ufn(c, i, P), rhs=D[:, i, :], start=(i == 0), stop=(i == 30)),
                       reads=[dres, res("uext")], writes=[PB[ti]])
                    if CONV_DUMMY and P == 128 and i % CONV_DUMMY == CONV_DUMMY - 1:
                        op("pe", lambda e: e.matmul(ps[:, 6, :], lhsT=zerosb[:, 0:128], rhs=zerosb[:, 0:512], start=True, stop=True,
                                                    skip_group_check=True), reads=[res("zerosb")], writes=[PB[6]])
        for ti, (ufn, P, lo) in enumerate(tiles):
            conv_epilogue(ti, P, lo)

    def conv_epilogue(ti, P, lo):
        if ti % 2 == 0:
            yb, cb_, ry, rc = ysb, ctm, res("ysb"), res("ctm")
        else:
            yb, cb_, ry, rc = ysb2, ctm2, res("stg"), res("stg")
        so = 32 + 16 * (ti % 2)
        rst = res(f"cstat{ti % 2}")
        op("dve", lambda e: e.tensor_tensor(out=yb[0:P, :], in0=ps[0:P, ti, :], in1=dwb_bc[0:P, :], op=ALU.add),
           reads=[PB[ti], res("cb")], writes=[ry])
        op("dve", lambda e: e.bn_stats(out=stat[0:P, so:so + 6], in_=yb[0:P, :]), reads=[ry], writes=[rst])
        op("dve", lambda e: e.bn_aggr(out=stat[0:P, so + 6:so + 8], in_=stat[0:P, so:so + 6]), reads=[rst], writes=[rst])
        op("act", lambda e: e.activation(out=stat[0:P, so + 8:so + 9], in_=stat[0:P, so + 7:so + 8], func=AF.Sqrt, scale=1.0, bias=LN_EPS),
           reads=[rst], writes=[rst])
        op("dve", lambda e: e.reciprocal(out=stat[0:P, so + 9:so + 10], in_=stat[0:P, so + 8:so + 9]), reads=[rst], writes=[rst])
        op("dve", lambda e: e.tensor_scalar(out=yb[0:P, :], in0=yb[0:P, :], scalar1=stat[0:P, so + 6:so + 7], scalar2=stat[0:P, so + 9:so + 10],
                                            op0=ALU.subtract, op1=ALU.mult),
           reads=[rst], writes=[ry])
        op("pool", lambda e: e.tensor_tensor(out=yb[0:P, :], in0=yb[0:P, :], in1=lng_bc[0:P, :], op=ALU.mult),
           reads=[res("cb")], writes=[ry])
        op("pool", lambda e: e.tensor_tensor(out=yb[0:P, :], in0=yb[0:P, :], in1=lnb_bc[0:P, :], op=ALU.add),
           reads=[res("cb")], writes=[ry])
        if rc is ry:
            op("act", lambda e: e.activation(out=cb_[0:P, :], in_=yb[0:P, :], func=AF.Silu), reads=[], writes=[ry])
        else:
            op("act", lambda e: e.activation(out=cb_[0:P, :], in_=yb[0:P, :], func=AF.Silu), reads=[ry], writes=[rc])
        b = 4 + (ti % 2)
        for c in range(4):
            op("pe", lambda e, c=c: e.transpose(out=psb16(b)[:, c * P:(c + 1) * P], in_=cb_[0:P, c * 128:(c + 1) * 128],
                                                identity=identb[0:P, 0:P]),
               reads=[rc, res("identb")], writes=[PB[b]])
        op("act", lambda e: e.copy(out=cT[:, :, lo:lo + P], in_=psb16(b)[:, 0:4 * P].rearrange("p (c j) -> p c j", c=4)),
           reads=[PB[b]], writes=[res("cT")])

    def split512(a, e_):
        out = []
        while a < e_:
            bnd = (a // 512 + 1) * 512
            x = min(e_, bnd)
            out.append((a, x))
            a = x
        return out

    def band_head(slot, h, P, qcol, segs, groups, kmin, W):
        c, po = h // 2, 64 * (h % 2)
        sb_ = 2 * slot
        started = set()
        for (kfn, n, dst, kr) in segs:
            for (a, e_) in split512(dst, dst + n):
                bb = sb_ + a // 512
                first = bb not in started
                started.add(bb)
                op("pe", lambda e, a=a, e_=e_, bb=bb, kfn=kfn, dst=dst, first=first: e.matmul(
                    ps[0:P, bb, a % 512:(a % 512) + (e_ - a)], lhsT=qT1[po:po + 64, c, qcol:qcol + P],
                    rhs=kfn(c, po)[:, a - dst:e_ - dst], start=first, stop=True, skip_group_check=True),
                   reads=[res("qT1"), kr], writes=[PB[bb]])
        yield
        for (a, e_) in split512(kmin, W):
            bb = sb_ + a // 512
            op("pe", lambda e, a=a, e_=e_, bb=bb: e.matmul(ps[0:P, bb, a % 512:(a % 512) + (e_ - a)], lhsT=identb[0:P, 0:P],
                                                           rhs=Bb[0:P, h, a:e_], start=False, stop=True, skip_group_check=True),
               reads=[res("identb"), res("Bb")], writes=[PB[bb]])
        yield
        spv = ps[0:P, sb_:sb_ + 2, :].rearrange("p a b -> p (a b)")
        rneg = res(f"negm{slot}")
        op("dve", lambda e: e.tensor_reduce(out=stat[0:P, 48 + slot:49 + slot], in_=spv[:, kmin:W], axis=AX.X, op=ALU.max, negate=True),
           reads=[PB[sb_], PB[sb_ + 1]], writes=[rneg])
        yield
        op("act", lambda e: e.activation(out=Pe[slot][0:P, kmin:W], in_=spv[:, kmin:W], func=AF.Exp, bias=stat[0:P, 48 + slot:49 + slot],
                                         scale=1.0, accum_out=rs16[0:P, h:h + 1]),
           reads=[PB[sb_], PB[sb_ + 1], rneg], writes=[res(f"Pe{slot}"), res(f"rs{h}")])
        yield
        tb = 4 + slot
        ng = len(groups)
        for gi, (g0, KK, vfn, vr) in enumerate(groups):
            op("pe", lambda e, gi=gi, g0=g0, KK=KK: e.transpose(out=psb16(tb)[0:KK, gi * 128:gi * 128 + P],
                                                               in_=Pe[slot][0:P, g0:g0 + KK], identity=identb[0:P, 0:P]),
               reads=[res(f"Pe{slot}"), res("identb")], writes=[PB[tb]])
        yield
        eng = "act" if h % 2 == 0 else "dve"
        kkmax = max(g[1] for g in groups)
        if all(g[1] == kkmax for g in groups):
            evac_copy(eng, PT[slot][0:kkmax, 0:ng, 0:P], psb16(tb)[0:kkmax, 0:ng * 128].rearrange("p (g j) -> p g j", g=ng)[:, :, 0:P],
                      reads=[PB[tb]], writes=[res(f"PT{slot}")])
        else:
            evac_copy(eng, PT[slot][0:128, 0:ng - 1, 0:P], psb16(tb)[0:128, 0:(ng - 1) * 128].rearrange("p (g j) -> p g j", g=ng - 1)[:, :, 0:P],
                      reads=[PB[tb]], writes=[res(f"PT{slot}")])
            kl = groups[-1][1]
            evac_copy(eng, PT[slot][0:kl, ng - 1, 0:P], psb16(tb)[0:kl, (ng - 1) * 128:(ng - 1) * 128 + P],
                      reads=[PB[tb]], writes=[res(f"PT{slot}")])
        yield
        ob = 6 + h // 8
        for gi, (g0, KK, vfn, vr) in enumerate(groups):
            op("pe", lambda e, gi=gi, KK=KK, vfn=vfn: e.matmul(
                ps[0:P, ob, (h % 8) * 64:(h % 8 + 1) * 64], lhsT=PT[slot][0:KK, gi, 0:P], rhs=vfn(h),
                start=(gi == 0), stop=(gi == ng - 1)),
               reads=[res(f"PT{slot}"), vr], writes=[PB[ob]])
        yield

    def band_qtile(P, qcol, segs, groups, kmin, W, osb_done_cb, par):
        Ob = 6
        state = {"done": set(), "started8": False, "norm": [False, False]}

        def normalize(hb):
            state["norm"][hb] = True
            op("dve", lambda e: e.reciprocal(out=rs16[0:P, 16 + hb * 8:24 + hb * 8], in_=rs16[0:P, hb * 8:hb * 8 + 8]),
               reads=[res(f"rs{h}") for h in range(hb * 8, hb * 8 + 8)], writes=[res(f"rinv{hb}")])
            op("dve", lambda e: e.tensor_tensor(
                out=osb[0:P, hb * 512:(hb + 1) * 512].rearrange("p (h d) -> p h d", h=8),
                in0=ps[0:P, Ob + hb, :].rearrange("p (h d) -> p h d", h=8),
                in1=rs16[0:P, 16 + hb * 8:24 + hb * 8].unsqueeze(2).broadcast_to([P, 8, 64]), op=ALU.mult),
               reads=[PB[Ob + hb], res(f"rinv{hb}")], writes=[res(f"osb{hb}")])

        def head_gen(slot, h):
            if h >= 8:
                state["started8"] = True
            yield from band_head(slot, h, P, qcol, segs, groups, kmin, W)
            state["done"].add(h)
            if not state["norm"][0] and all(x in state["done"] for x in range(8)):
                normalize(0)

        def filler(rnd):
            if not BAND_DUMMY or P < 128:
                return
            if not state["started8"]:
                bk = Ob + 1
            elif state["norm"][0]:
                bk = Ob
            else:
                return
            for _ in range(BAND_DUMMY):
                op("pe", lambda e: e.matmul(ps[:, bk, 0:BAND_DUMMY_N], lhsT=zerosb[:, 0:128], rhs=zerosb[:, 0:BAND_DUMMY_N],
                                            start=True, stop=True, skip_group_check=True),
                   reads=[res("zerosb")], writes=[PB[bk]])
        run_pool([(lambda slot, h=h: head_gen(slot, h)) for h in range(16)], 2, stagger=STG_BAND, filler=filler)
        if not state["norm"][0]:
            normalize(0)
        normalize(1)
        tb = 4 + par
        for kc in range(8):
            op("pe", lambda e, kc=kc, tb=tb: e.transpose(out=psb16(tb)[:, kc * P:(kc + 1) * P], in_=osb[0:P, kc * 128:(kc + 1) * 128],
                                                         identity=identb[0:P, 0:P]),
               reads=[res(f"osb{kc // 4}"), res("identb")], writes=[PB[tb]])
        op("act", lambda e, tb=tb: e.copy(out=oT[:, :, qcol:qcol + P], in_=psb16(tb)[:, 0:8 * P].rearrange("p (k j) -> p k j", k=8)),
           reads=[PB[tb]], writes=[res("oT")])

    def mlp(l, P, NT):
        N = P * NT
        rmsnorm(P, NT, 1 + 2 * l)
        for j in range(8):
            wu, wr = wnext()

            def ev(c, b, j=j):
                i2 = (j * 4 + c) % 2
                op("act", lambda e: e.activation(out=rl[i2][:, 0:N], in_=ps[:, b, 0:N], func=AF.Relu),
                   reads=[PB[b]], writes=[res(("kst", "vst")[i2])])
                op("dve", lambda e: e.tensor_tensor(out=hidT[:, j * 4 + c, 0:N], in0=rl[i2][:, 0:N], in1=rl[i2][:, 0:N], op=ALU.mult),
                   reads=[res(("kst", "vst")[i2])], writes=[res("hidT")])
            proj_fm(wu, wr, 0, 4, N, ev)
        for o in range(2):
            for g in range(4):
                wd, wr = wnext()
                for t in range(NT):
                    for j in range(8):
                        op("pe", lambda e, t=t, j=j, g=g, wd=wd: e.matmul(ps[0:P, t, :], lhsT=hidT[:, g * 8 + j, t * P:(t + 1) * P], rhs=wd[:, j, :],
                                                                          start=(g == 0 and j == 0), stop=(g == 3 and j == 7)),
                           reads=[wr, res("hidT")], writes=[PB[t]])
            for t in range(NT):
                op("dve", lambda e, t=t, o=o: e.tensor_tensor(out=xtok[0:P, t, o * 512:(o + 1) * 512], in0=xtok[0:P, t, o * 512:(o + 1) * 512],
                                                              in1=ps[0:P, t, :], op=ALU.add),
                   reads=[PB[t]], writes=[res("xtok")])

    def out_proj_resid(P, NT, parts_fn):
        for o in range(2):
            parts = parts_fn(o)
            for t in range(NT):
                b = bank4()
                n = len(parts)
                for pi, (lf, rhs, rd) in enumerate(parts):
                    op("pe", lambda e, lf=lf, rhs=rhs, t=t, b=b, pi=pi: e.matmul(ps[0:P, b, :], lhsT=lf(t), rhs=rhs, start=(pi == 0), stop=(pi == n - 1)),
                       reads=rd, writes=[PB[b]])
                op("dve", lambda e, t=t, o=o, b=b: e.tensor_tensor(out=xtok[0:P, t, o * 512:(o + 1) * 512], in0=xtok[0:P, t, o * 512:(o + 1) * 512],
                                                                    in1=ps[0:P, b, :], op=ALU.add),
                   reads=[PB[b]], writes=[res("xtok")])

    FIRST_L1_DONE = [False]
    HI = {}

    def run_block(sample, s=0, i=0):
        def st(k):
            MARKS.append((("S" if sample else f"P{s}{i}"), k, S.nops["pe"]))
            stop(k + (100 if sample else 0))
        st(0)
        if sample:
            P, NT, N = 64, 1, 64
            kvt = [(16 * b, 16) for b in range(4)]
            dma(xtok[0:64, 0, :], xs, writes=[res("xtok")], grp=grp("xtok"))
        else:
            P, NT, N = 128, 4, 512
            kvt = [(128 * t, 128) for t in range(4)]
            t0 = 512 * i
            dma(xtok[:, :, :], xp[s, t0:t0 + 512, :].rearrange("(t p) f -> p t f", p=128), writes=[res("xtok")], grp=grp("xtok"))
            if i == 0:
                op("pool", lambda e: e.memset(uext[:, :, 0:30], 0.0), writes=[res("uext")])

        if not sample:
            DUMP[0] = lambda: dma(yp[s, t0:t0 + 512, :].rearrange("(t p) f -> p t f", p=128), xtok[:, :, :], reads=[res("xtok")], grp=grp("ystore"))
        rmsnorm(P, NT, 0)
        wt, wr = wnext()
        proj_fm(wt, wr, 0, 4, N, lambda c, b: op("act", lambda e: e.activation(out=sg[:, c, 0:N], in_=ps[:, b, 0:N], func=AF.Sigmoid),
                                                 reads=[PB[b]], writes=[res("sg")]))
        wt, wr = wnext()

        def ev_a(c, b):
            if sample:
                for q in range(4):
                    op("dve", lambda e, q=q: e.tensor_tensor(out=uexts[:, q, c, 30:46], in0=ps[:, b, 16 * q:16 * q + 16], in1=sg[:, c, 16 * q:16 * q + 16], op=ALU.mult),
                       reads=[PB[b], res("sg")], writes=[res("uext")])
                op("dve", lambda e: e.tensor_tensor(out=utail[:, c, 0:64], in0=ps[:, b, 0:64], in1=sg[:, c, 0:64], op=ALU.mult),
                   reads=[PB[b], res("sg")], writes=[res("utail")])
            else:
                op("dve", lambda e: e.tensor_tensor(out=uext[:, c, 30:542], in0=ps[:, b, 0:512], in1=sg[:, c, 0:512], op=ALU.mult),
                   reads=[PB[b], res("sg")], writes=[res("uext")])
                if i == 3:
                    op("dve", lambda e: e.tensor_tensor(out=utail[:, c, 0:32], in0=ps[:, b, 480:512], in1=sg[:, c, 480:512], op=ALU.mult),
                       reads=[PB[b], res("sg")], writes=[res("utail")])
        proj_fm(wt, wr, 0, 4, N, ev_a)

        def qkv_gen():
            wt, wr = wnext()
            yield from proj_fm_g(wt, wr, 0, 4, N, lambda c, b: evac_copy("act", qT0[:, c, 0:N], ps[:, b, 0:N], [PB[b]], [res("qT0")], scale=0.125))
            wt, wr = wnext()
            if sample:
                yield from proj_fm_g(wt, wr, 0, 4, N, lambda c, b: evac_copy("dve", kTs[:, c, 0:N], ps[:, b, 0:N], [PB[b]], [res("kT0")]))
            else:
                yield from proj_fm_g(wt, wr, 0, 4, N, lambda c, b: evac_copy("dve", kT0[:, c, t0:t0 + N], ps[:, b, 0:N], [PB[b]], [res("kT0")]))

            def ev_k(ti, lo, Pk, b):
                evac_copy("act", kst[0:Pk, :], ps[0:Pk, b, :], [PB[b]], [res("kst")])
                dst = o_sbk_s[lo:lo + Pk, :] if sample else o_sbk_p[s, t0 + lo:t0 + lo + Pk, :]
                dma(dst, kst[0:Pk, :], reads=[res("kst")], grp=grp("kst"))
            yield from proj_tm_g(wt, wr, kvt, ev_k)
            wt, wr = wnext()

            def ev_v(ti, lo, Pk, b):
                evac_copy("act", vst[0:Pk, :], ps[0:Pk, b, :], [PB[b]], [res("vst")])
                if sample:
                    pass
                else:
                    evac_copy("dve", v0[:, 4 * i + ti, :], ps[:, b, :], [PB[b]], [res("v0")])
                dst = o_sbv_s[lo:lo + Pk, :] if sample else o_sbv_p[s, t0 + lo:t0 + lo + Pk, :]
                dma(dst, vst[0:Pk, :], reads=[res("vst")], grp=grp("vst"))
                if sample:
                    evac_copy("dve", vnew0[0:16, ti, :], ps[0:16, b, :], [PB[b]], [res("vnew")])
            yield from proj_tm_g(wt, wr, kvt, ev_v)

        st(2)
        if sample:
            for q in range(4):
                dma(stg[0:30, 0:512], cconv[q], writes=[res("stg")], grp=grp("stg"))
                for c in range(4):
                    op("pe", lambda e, c=c: e.transpose(out=ps[:, 3, c * 32:c * 32 + 30], in_=stg[0:30, c * 128:(c + 1) * 128], identity=identf[0:30, 0:30]),
                       reads=[res("stg"), res("identf")], writes=[PB[3]])
                op("dve", lambda e, q=q: e.tensor_copy(out=uexts[:, q, :, 0:30], in_=ps[:, 3, 0:128].rearrange("p (c j) -> p c j", c=4)[:, :, 0:30]),
                   reads=[PB[3]], writes=[res("uext")])
                for c in range(4):
                    op("pe", lambda e, c=c, q=q: e.transpose(out=ps[0:16, 2, c * 128:(c + 1) * 128], in_=utail[:, c, 16 * q:16 * q + 16], identity=identf[:, :]),
                       reads=[res("utail"), res("identf")], writes=[PB[2]])
                op("act", lambda e: e.copy(out=cst[0:16, :], in_=ps[0:16, 2, :]), reads=[PB[2]], writes=[res("ysb")])
                dma(o_conv_s[q, 14:30, :], cst[0:16, :], reads=[res("ysb")], grp=grp("cst"))
                dma(o_conv_s[q, 0:14, :], cconv[q, 16:30, :], grp=grp("cst2"))
        elif i == 3:
            for c in range(4):
                op("pe", lambda e, c=c: e.transpose(out=ps[0:32, 2, c * 128:(c + 1) * 128], in_=utail[:, c, 0:32], identity=identf[:, :]),
                   reads=[res("utail"), res("identf")], writes=[PB[2]])
            op("act", lambda e: e.copy(out=cst[0:32, :], in_=ps[0:32, 2, :]), reads=[PB[2]], writes=[res("ysb")])
            dma(o_conv_p[s, :, :], cst[2:32, :], reads=[res("ysb")], grp=grp("cst"))

        if sample:
            ctiles = [((lambda c, ii, Pq, q=q: uexts[:, q, c, ii:ii + 16]), 16, 16 * q) for q in range(4)]
        else:
            ctiles = [((lambda c, ii, Pq, t=t: uext[:, c, t * 128 + ii:t * 128 + ii + 128]), 128, 128 * t) for t in range(4)]
        BANKSEL[0] = (4, 5, 6, 7)
        run_pool([lambda slot: conv_gen(ctiles), lambda slot: qkv_gen()], 2)
        BANKSEL[0] = None
        if not sample:
            op("pool", lambda e: e.tensor_copy(out=uext[:, :, 0:30], in_=uext[:, :, 512:542]), reads=[], writes=[res("uext")])
        for ti, (ufn, Pc, lo) in enumerate(ctiles):
            conv_epilogue(ti, Pc, lo)

        st(3)
        HI.clear()
        if sample:
            CS = [(kT0, v0, [res("kT0")], [res("v0")]),
                  (kT0b, v0b, [res("kT1_0"), res("kT1_1")], [res("v1_0"), res("v1_1"), res("v1s")])]
            KSTG = [(stg[:, 0:512], res("stg"), grp("stg")), (kst[:, :], res("kst"), grp("kst"))]
            VSTG = [(stg[:, 512:1024], res("stgv"), grp("stgv")), (vst[:, :], res("vst"), grp("vst"))]

            def sb_load(q):
                kc_, vc_, kr_, vr_ = CS[q % 2]
                for kt in range(16):
                    kb, krs, kg = KSTG[kt % 2]
                    vb, vrs, vg = VSTG[kt % 2]
                    tbk = 2 + kt % 2
                    dma(kb, csk[q, kt * 128:(kt + 1) * 128, :], writes=[krs], grp=kg)
                    for c in range(4):
                        op("pe", lambda e, c=c, kb=kb, tbk=tbk: e.transpose(out=ps[:, tbk, c * 128:(c + 1) * 128], in_=kb[:, c * 128:(c + 1) * 128], identity=identf[:, :]),
                           reads=[krs, res("identf")], writes=[PB[tbk]])
                    evac_copy("act" if kt % 2 == 0 else "dve", kc_[:, :, kt * 128:(kt + 1) * 128],
                              ps[:, tbk, :].rearrange("p (c j) -> p c j", c=4), [PB[tbk]], kr_)
                    dma(vb, csv[q, kt * 128:(kt + 1) * 128, :], writes=[vrs], grp=vg)
                    op("pool", lambda e, kt=kt, vb=vb, vc_=vc_: e.tensor_copy(out=vc_[:, kt, :], in_=vb), reads=[vrs], writes=vr_)
                op("pool", lambda e, vc_=vc_: e.tensor_copy(out=vc_[0:16, 16, :], in_=vnew0[0:16, q, :]), reads=[res("vnew")], writes=vr_)

            def sb_run(q):
                kc_, vc_, kr_, vr_ = CS[q % 2]
                facts_s = []
                for par_ in range(2):
                    steps = []
                    for kt in range(16, -1, -1):
                        KK = 16 if kt == 16 else 128
                        zp, vp = [], []
                        for j_ in range(4):
                            h = 2 * j_ + par_
                            c, po = h // 2, 64 * (h % 2)
                            if kt == 16:
                                l = kTs[po:po + 64, c, 16 * q:16 * q + 16]
                            else:
                                l = kc_[po:po + 64, c, kt * 128:(kt + 1) * 128]
                            zp.append((l, qT0[po:po + 64, c, 16 * q:16 * q + 16], 0, j_ * 16, j_ * 16 + 16))
                            vp.append((vc_[0:KK, kt, h * 64:(h + 1) * 64], j_ * 16, j_ * 16 + 16))
                        mask = (maskS[:, 0:64], 0, 64) if kt == 16 else None
                        steps.append((KK, zp, 0, 64, mask, vp, kr_ + [res("kT0")], vr_))

                    def evac_s(Ob, opo, q=q, par_=par_):
                        op("act", lambda e: e.copy(
                            out=aoT[0:64, :, 16 * q:16 * q + 16].rearrange("p (j two) t -> p two j t", two=2)[:, par_, :, :],
                            in_=ps[0:64, Ob, 0:64].rearrange("p (j t) -> p j t", j=4)),
                           reads=[PB[Ob]], writes=[res("aoT")])
                    facts_s.append(lambda slot, steps=steps, evac_s=evac_s: sb_head(slot, steps, evac_s))
                run_pool(facts_s, 2, stagger=STG_SBS)
            sb_load(0)
            for q in range(4):
                if q + 1 < 4:
                    sb_load(q + 1)
                sb_run(q)
        else:
            facts = []
            for h in range(8):
                c, po = h // 2, 64 * (h % 2)
                steps = []
                for kt in range(4 * i + 3, -1, -1):
                    o_ = kt - 4 * i
                    c0 = 128 * o_ if o_ >= 0 else 0
                    mask = (maskU[:, :], c0, c0 + 128) if o_ >= 0 else None
                    zp = [(kT0[po:po + 64, c, kt * 128:(kt + 1) * 128], qT0[po:po + 64, c, c0:512], 0, c0, 512)]
                    vp = [(v0[:, kt, h * 64:(h + 1) * 64], c0, 512)]
                    steps.append((128, zp, c0, 512, mask, vp, [res("kT0")], [res("v0")]))

                def evac_p(Ob, opo, h=h):
                    HI[h] = opo
                    op("act", lambda e: e.copy(out=aoT[opo:opo + 64, h, 0:512], in_=ps[opo:opo + 64, Ob, :]), reads=[PB[Ob]], writes=[res("aoT")])
                facts.append(lambda slot, steps=steps, evac_p=evac_p: sb_head(slot, steps, evac_p, three=(4 if SB_SLOTS == 4 else True)))
            def sb_filler(rnd):
                for _ in range(SB_DUMMY):
                    op("pe", lambda e: e.matmul(ps[64:128, 7, 0:SB_DUMMY_N], lhsT=zerosb[:, 0:64], rhs=zerosb[:, 0:SB_DUMMY_N],
                                                start=True, stop=True, skip_group_check=True),
                       reads=[res("zerosb")], writes=[PB[7]])
            if SB_SLOTS == 4:
                run_pool(facts, 4, stagger=STG_SB4)
            else:
                run_pool(facts, 3, stagger=STG_SB, filler=sb_filler if SB_DUMMY else None)

        st(4)
        def parts_ab(o):
            wa, ra = wnext()
            wc, rc = wnext()
            pr_lo, pr_hi, pr_c = [], [], []
            for h in range(8):
                po_ = HI.get(h, 0)
                ent = ((lambda t, h=h, po_=po_: aoT[po_:po_ + 64, h, t * P:(t + 1) * P]), wa[po_:po_ + 64, h, :], [ra, res("aoT")])
                (pr_hi if po_ else pr_lo).append(ent)
            for c in range(4):
                pr_c.append(((lambda t, c=c: cT[:, c, t * P:(t + 1) * P]), wc[:, c, :], [rc, res("cT")]))
            return pr_lo + pr_c + pr_hi
        out_proj_resid(P, NT, parts_ab)
        st(45)
        mlp(0, P, NT)
        st(5)

        if not FIRST_L1_DONE[0]:
            FIRST_L1_DONE[0] = True
            bg_flush()
            S.barrier()
            setup_bias()
        rmsnorm(P, NT, 2)
        own = i % 2 if not sample else 1
        prev = 1 - own
        need_kv = sample or i == 3
        for half in range(2):
            wt, wr = wnext()
            proj_fm(wt, wr, 0, 4, N, lambda c, b, half=half: evac_copy("act", qT1[:, half * 4 + c, 0:N], ps[:, b, 0:N], [PB[b]], [res("qT1")], scale=0.125))
        for half in range(2):
            wt, wr = wnext()
            if sample:
                proj_fm(wt, wr, 0, 4, N, lambda c, b, half=half: evac_copy("dve", kT1s[:, half * 4 + c, 0:N], ps[:, b, 0:N], [PB[b]], [res("kT1s")]))
            else:
                proj_fm(wt, wr, 0, 4, N, lambda c, b, half=half: evac_copy("dve", kT1[own][:, half * 4 + c, 0:N], ps[:, b, 0:N], [PB[b]], [res(f"kT1_{own}")]))
            if need_kv:
                def ev_k1(ti, lo, Pk, b, half=half):
                    evac_copy("act", kst1[0:Pk, 0:512], ps[0:Pk, b, :], [PB[b]], [res("kst1")])
                    dst = o_bk_s[lo:lo + Pk, half * 512:(half + 1) * 512] if sample else o_bk_p[s, lo:lo + Pk, half * 512:(half + 1) * 512]
                    dma(dst, kst1[0:Pk, 0:512], reads=[res("kst1")], grp=grp("kst1"))
                proj_tm(wt, wr, kvt, ev_k1)
        for half in range(2):
            wt, wr = wnext()

            def ev_v1(ti, lo, Pk, b, half=half):
                if sample:
                    evac_copy("dve", vnew1[0:16, ti, half * 512:(half + 1) * 512], ps[0:16, b, :], [PB[b]], [res("vnew")])
                else:
                    evac_copy("dve", v1[own][:, ti, half * 512:(half + 1) * 512], ps[:, b, :], [PB[b]], [res(f"v1_{own}")])
                if need_kv:
                    evac_copy("act", vst1[0:Pk, 0:512], ps[0:Pk, b, :], [PB[b]], [res("vst1")])
                    dst = o_bv_s[lo:lo + Pk, half * 512:(half + 1) * 512] if sample else o_bv_p[s, lo:lo + Pk, half * 512:(half + 1) * 512]
                    dma(dst, vst1[0:Pk, 0:512], reads=[res("vst1")], grp=grp("vst1"))
            proj_tm(wt, wr, kvt, ev_v1)

        st(6)
        if sample:
            def band_load(q):
                kb_, vb_ = kT1[q % 2], v1[q % 2]
                rk, rv = res(f"kT1_{q % 2}"), res(f"v1_{q % 2}")
                for kt in range(4):
                    dma(stg[:, :], cbk[q, kt * 128:(kt + 1) * 128, :], writes=[res("stg"), res("stgv")], grp=grp("stg"))
                    for hh in range(2):
                        for c in range(4):
                            op("pe", lambda e, c=c, hh=hh: e.transpose(out=ps[:, 3, c * 128:(c + 1) * 128], in_=stg[:, (hh * 4 + c) * 128:(hh * 4 + c + 1) * 128], identity=identf[:, :]),
                               reads=[res("stg"), res("stgv"), res("identf")], writes=[PB[3]])
                        evac_copy("act" if hh == 0 else "dve", kb_[:, hh * 4:hh * 4 + 4, kt * 128:(kt + 1) * 128],
                                  ps[:, 3, :].rearrange("p (c j) -> p c j", c=4), [PB[3]], [rk])
                    dma(kst1[:, :], cbv[q, kt * 128:(kt + 1) * 128, :], writes=[res("kst1")], grp=grp("kst1"))
                    op("pool", lambda e, kt=kt: e.tensor_copy(out=vb_[:, kt, :], in_=kst1[:, :]), reads=[res("kst1")], writes=[rv])

            def band_run(q):
                kb_, vb_ = kT1[q % 2], v1[q % 2]
                rk, rv = res(f"kT1_{q % 2}"), res(f"v1_{q % 2}")
                op("pool", lambda e: e.tensor_copy(out=v1s[0:16, :], in_=vnew1[0:16, q, :]), reads=[res("vnew")], writes=[res("v1s")])
                segs = [((lambda c, po: kb_[po:po + 64, c, 0:512]), 512, 0, rk),
                        ((lambda c, po: kT1s[po:po + 64, c, 16 * q:16 * q + 16]), 16, 512, res("kT1s"))]
                groups = [(128 * g, 128, (lambda h, g=g: vb_[:, g, h * 64:(h + 1) * 64]), rv) for g in range(4)]
                groups.append((512, 16, (lambda h: v1s[0:16, h * 64:(h + 1) * 64]), res("v1s")))
                band_qtile(16, 16 * q, segs, groups, 0, 528, None, q % 2)
            band_load(0)
            for q in range(4):
                if q + 1 < 4:
                    band_load(q + 1)
                band_run(q)
        else:
            for r in range(4):
                segs = []
                if i > 0:
                    segs.append(((lambda c, po, r=r: kT1[prev][po:po + 64, c, 128 * r:512]), 512 - 128 * r, 0, res(f"kT1_{prev}")))
                segs.append(((lambda c, po, r=r: kT1[own][po:po + 64, c, 0:128 * (r + 1)]), 128 * (r + 1), 512 - 128 * r, res(f"kT1_{own}")))
                kmin = 0 if i > 0 else 512 - 128 * r
                groups = []
                for g in range(5):
                    if 128 * g < kmin:
                        continue
                    tt = r + g
                    if tt < 4:
                        groups.append((128 * g, 128, (lambda h, tt=tt: v1[prev][:, tt, h * 64:(h + 1) * 64]), res(f"v1_{prev}")))
                    else:
                        groups.append((128 * g, 128, (lambda h, tt=tt: v1[own][:, tt - 4, h * 64:(h + 1) * 64]), res(f"v1_{own}")))
                band_qtile(128, 128 * r, segs, groups, kmin, 640, None, r % 2)

        st(7)
        def parts_c(o):
            wo, ro = wnext()
            return [((lambda t, kc=kc: oT[:, kc, t * P:(t + 1) * P]), wo[:, kc, :], [ro, res("oT")]) for kc in range(8)]
        out_proj_resid(P, NT, parts_c)
        st(75)
        mlp(1, P, NT)
        DUMP[0] = None

        rmsnorm(P, NT, None, final=True)
        st(99)
        if sample:
            dma(ys, xtok[0:64, 0, :], reads=[res("xtok")], grp=grp("ystore"))
        else:
            dma(yp[s, t0:t0 + 512, :].rearrange("(t p) f -> p t f", p=128), xtok[:, :, :], reads=[res("xtok")], grp=grp("ystore"))

    import os
    try:
        if not os.environ.get("NO_CONSTS"):
            setup_consts()
        BG["gen"] = convert_gen()
        bg_advance(9)
        stop(1)
        for s in range(2):
            if os.environ.get("NO_PROMPT"):
                bg_flush()
                S.barrier()
                break
            for i in range(4):
                run_block(False, s, i)
                stop(8 + 4 * s + i)
        run_block(True)
    except StopBuild:
        pass
    S.emit()
    return nc, S


_CACHE = {}


def _get_program():
    if "nc" not in _CACHE:
        _CACHE["nc"] = build_program()[0]
    return _CACHE["nc"]


def kernel(x_prompt, x_sample, cache_sb_k, cache_sb_v, cache_conv, cache_band_k, cache_band_v,
           norm_mix, norm_ffn, norm_final, w_in_ab, w_out_ab, dw_w, dw_b, conv_ln_g, conv_ln_b,
           w_in_c, w_out_c, rel_bias, w_up, w_down):
    f = lambda a: np.ascontiguousarray(np.asarray(a, dtype=np.float32))
    x_prompt, x_sample = f(x_prompt), f(x_sample)
    q = np.arange(128)[:, None]
    k = np.arange(640)[None, :]
    idx = np.clip(q + 512 - k, -128, 128) + 128
    bfull = f(np.asarray(rel_bias, dtype=np.float32)[0][:, idx])
    shared = {
        "norm_mix": f(norm_mix), "norm_ffn": f(norm_ffn), "norm_final": f(norm_final),
        "w_in_ab": f(w_in_ab)[0], "w_out_ab": f(w_out_ab)[0], "dw_w": f(dw_w)[0], "dw_b": f(dw_b)[0],
        "ln_g": f(conv_ln_g)[0], "ln_b": f(conv_ln_b)[0], "w_in_c": f(w_in_c)[0], "w_out_c": f(w_out_c)[0],
        "bfull": bfull, "w_up": f(w_up), "w_down": f(w_down),
    }
    csk, csv = f(cache_sb_k)[0], f(cache_sb_v)[0]
    cconv, cbk, cbv = f(cache_conv)[0], f(cache_band_k)[0], f(cache_band_v)[0]
    in_maps = []
    for c in range(NCORES):
        m = dict(shared)
        m["xp"] = x_prompt[2 * c:2 * c + 2]
        m["xs"] = x_sample[4 * c:4 * c + 4].reshape(64, 1024)
        m["csk"] = csk[4 * c:4 * c + 4].reshape(4, 2048, 512)
        m["csv"] = csv[4 * c:4 * c + 4].reshape(4, 2048, 512)
        m["cconv"] = cconv[4 * c:4 * c + 4]
        m["cbk"] = cbk[4 * c:4 * c + 4].reshape(4, 512, 1024)
        m["cbv"] = cbv[4 * c:4 * c + 4].reshape(4, 512, 1024)
        in_maps.append({k_: np.ascontiguousarray(v) for k_, v in m.items()})
    nc = _get_program()
    res = run_bass_kernel_spmd(nc, in_maps, core_ids=list(range(NCORES)))
    R = res.results
    cat = lambda key: np.concatenate([np.asarray(r[key], dtype=np.float32) for r in R], axis=0)
    y_prompt = cat("yp")
    y_sample = cat("ys").reshape(32, 16, 1024)
    sbk_p = cat("o_sbk_p").reshape(1, 16, 2048, 8, 64)
    sbv_p = cat("o_sbv_p").reshape(1, 16, 2048, 8, 64)
    conv_p = cat("o_conv_p").reshape(1, 16, 30, 512)
    bk_p = cat("o_bk_p").reshape(1, 16, 512, 16, 64)
    bv_p = cat("o_bv_p").reshape(1, 16, 512, 16, 64)
    sbk_s = cat("o_sbk_s").reshape(1, 32, 16, 8, 64)
    sbv_s = cat("o_sbv_s").reshape(1, 32, 16, 8, 64)
    conv_s = cat("o_conv_s").reshape(1, 32, 30, 512)
    bk_s = cat("o_bk_s").reshape(1, 32, 16, 16, 64)
    bv_s = cat("o_bv_s").reshape(1, 32, 16, 16, 64)
    return (y_prompt, y_sample, sbk_p, sbv_p, conv_p, bk_p, bv_p, sbk_s, sbv_s, conv_s, bk_s, bv_s)
```

```python
import numpy as np
import concourse.bass as bass
import concourse.mybir as mybir
from concourse.bass_utils import run_bass_kernel_spmd

F32 = mybir.dt.float32
BF16 = mybir.dt.bfloat16
AF = mybir.ActivationFunctionType
ALU = mybir.AluOpType
AX = mybir.AxisListType

EPOCH = 3000
NCORES = 8
RMS_EPS = 1e-6
LN_EPS = 1e-5
NEG = -30000.0


class Res:
    __slots__ = ("name", "last_w", "readers", "excl")

    def __init__(self, name, excl=False):
        self.name = name
        self.excl = excl
        self.last_w = None
        self.readers = []


class DmaGroup:
    def __init__(self, sched, name):
        self.sem = sched.new_sem("dg_" + name)
        self.count = 0


class Sched:
    ENGS = ("pe", "act", "dve", "pool", "sp")

    def __init__(self, nc):
        self.nc = nc
        self.ops = {e: [] for e in self.ENGS}
        self.nops = {e: 0 for e in self.ENGS}
        self.esems = {e: [] for e in self.ENGS}
        self.seen = {e: {} for e in self.ENGS}
        self._semctx = []
        self.groups = []

    def new_sem(self, name):
        ctx = self.nc.semaphore(name)
        s = ctx.__enter__()
        self._semctx.append(ctx)
        return s

    def group(self, name):
        g = DmaGroup(self, name)
        self.groups.append(g)
        return g

    def _eng_ticket(self, eng):
        n = self.nops[eng]
        ep, v = divmod(n, EPOCH)
        while len(self.esems[eng]) <= ep:
            self.esems[eng].append(self.new_sem(f"e_{eng}_{len(self.esems[eng])}"))
        self.nops[eng] = n + 1
        return (self.esems[eng][ep], v + 1, eng)

    def op(self, eng, fn, reads=(), writes=(), dma=None):
        deps = []
        reads = [r for r in reads if r is not None]
        writes = [w for w in writes if w is not None]
        ex = [r for r in reads if r.excl]
        if ex:
            reads = [r for r in reads if not r.excl]
            writes = list(writes) + [r for r in ex if r not in writes]
        for r in reads:
            if r.last_w is not None:
                deps.append(r.last_w)
        for w in writes:
            if w.last_w is not None:
                deps.append(w.last_w)
            deps.extend(w.readers)
        seen = self.seen[eng]
        best = {}
        for (sem, val, src) in deps:
            if src == eng and eng == "pe":
                continue
            k = id(sem)
            if seen.get(k, 0) >= val:
                continue
            if k not in best or best[k][1] < val:
                best[k] = (sem, val)
        for k, (sem, val) in best.items():
            seen[k] = val
        waits = list(best.values())
        if dma is not None:
            dma.count += 16
            ticket = (dma.sem, dma.count, "dma")
            inc = (dma.sem, 16)
        else:
            ticket = self._eng_ticket(eng)
            inc = (ticket[0], 1)
        self.ops[eng].append((waits, fn, inc))
        for r in reads:
            r.readers.append(ticket)
        for w in writes:
            w.last_w = ticket
            w.readers = []
        return ticket

    def barrier(self):
        waits = []
        for e in self.ENGS:
            n = self.nops[e]
            if n > 0:
                ep, v = divmod(n - 1, EPOCH)
                waits.append((self.esems[e][ep], v + 1))
        for g in self.groups:
            if g.count > 0:
                waits.append((g.sem, g.count))
        for e in self.ENGS:
            t = self._eng_ticket(e)
            self.ops[e].append((list(waits), lambda eng: eng.nop(), (t[0], 1)))
            for sem, val in waits:
                k = id(sem)
                if self.seen[e].get(k, 0) < val:
                    self.seen[e][k] = val

    def emit(self):
        nc = self.nc
        fin = [(g.sem, g.count) for g in self.groups if g.count > 0]
        self.ops["sp"].append((fin, lambda e: e.nop(), None))
        engmap = {"pe": "tensor", "act": "scalar", "dve": "vector", "pool": "gpsimd", "sp": "sync"}
        with nc.Block() as block:
            for e in self.ENGS:
                ops = self.ops[e]

                def body(eng, ops=ops):
                    for waits, fn, inc in ops:
                        for sem, val in waits:
                            eng.wait_ge(sem, val)
                        ins = fn(eng)
                        if inc is not None:
                            ins.then_inc(inc[0], inc[1])
                getattr(block, engmap[e])(body)


N_WT = 49
import os as _os
SB_SLOTS = int(_os.environ.get('SB_SLOTS', '4'))
STG_SB4 = int(_os.environ.get('STG_SB4', '2'))
SB4_DUMMY = int(_os.environ.get('SB4_DUMMY', '1'))
SB_DUMMY = int(_os.environ.get('SB_DUMMY', '1'))
CONV_DUMMY = int(_os.environ.get('CONV_DUMMY', '0'))
BAND_DUMMY = int(_os.environ.get('BAND_DUMMY', '0'))
BAND_DUMMY_N = int(_os.environ.get('BAND_DUMMY_N', '512'))
SB_DUMMY_N = int(_os.environ.get('SB_DUMMY_N', '512'))
STG_SB = int(_os.environ.get('STG_SB', '1'))
STG_BAND = int(_os.environ.get('STG_BAND', '0'))
STG_SBS = int(_os.environ.get('STG_SBS', '0'))
DEBUG_STOP = None


class StopBuild(Exception):
    pass


DUMP = [None]
MARKS = []


def stop(k):
    if DEBUG_STOP == k:
        if DUMP[0] is not None:
            DUMP[0]()
        raise StopBuild()


def build_program():
    nc = bass.Bass("TRN2", target_bir_lowering=False, dynamic_dma_scratch_size=512)
    S = Sched(nc)

    def din(name, shape, dt=F32):
        return nc.dram_tensor(name, list(shape), dt, kind="ExternalInput").ap()

    def dout(name, shape, dt=F32):
        return nc.dram_tensor(name, list(shape), dt, kind="ExternalOutput").ap()

    xp = din("xp", [2, 2048, 1024])
    xs = din("xs", [64, 1024])
    csk = din("csk", [4, 2048, 512])
    csv = din("csv", [4, 2048, 512])
    cconv = din("cconv", [4, 30, 512])
    cbk = din("cbk", [4, 512, 1024])
    cbv = din("cbv", [4, 512, 1024])
    norm_mix = din("norm_mix", [2, 1024])
    norm_ffn = din("norm_ffn", [2, 1024])
    norm_final = din("norm_final", [1024])
    w_in_ab = din("w_in_ab", [1024, 2560])
    w_out_ab = din("w_out_ab", [1024, 1024])
    dw_w = din("dw_w", [31, 512])
    dw_b = din("dw_b", [512])
    ln_g = din("ln_g", [512])
    ln_b = din("ln_b", [512])
    w_in_c = din("w_in_c", [1024, 3072])
    w_out_c = din("w_out_c", [1024, 1024])
    bfull = din("bfull", [16, 128, 640])
    w_up = din("w_up", [2, 1024, 4096])
    w_down = din("w_down", [2, 4096, 1024])

    yp = dout("yp", [2, 2048, 1024])
    ys = dout("ys", [64, 1024])
    o_sbk_p = dout("o_sbk_p", [2, 2048, 512])
    o_sbv_p = dout("o_sbv_p", [2, 2048, 512])
    o_conv_p = dout("o_conv_p", [2, 30, 512])
    o_bk_p = dout("o_bk_p", [2, 512, 1024])
    o_bv_p = dout("o_bv_p", [2, 512, 1024])
    o_sbk_s = dout("o_sbk_s", [64, 512])
    o_sbv_s = dout("o_sbv_s", [64, 512])
    o_conv_s = dout("o_conv_s", [4, 30, 512])
    o_bk_s = dout("o_bk_s", [64, 1024])
    o_bv_s = dout("o_bv_s", [64, 1024])

    wsc = nc.dram_tensor("wsc", [N_WT, 128, 4096], BF16, kind="Internal").ap()

    cur = [1024]

    def alloc(name, shape, dt, at=None):
        esz = 4 if dt == F32 else 2
        n = 1
        for d in shape[1:]:
            n *= d
        nbytes = (n * esz + 31) // 32 * 32
        if at is None:
            at = cur[0]
            cur[0] += nbytes
        assert at + nbytes <= 229376, (name, at, nbytes)
        return nc.alloc_sbuf_tensor_at(name, list(shape), dt, offset=at)

    wring = [alloc(f"wring{i}", [128, 4096], BF16) for i in range(3)]
    xtok = alloc("xtok", [128, 4, 1024], F32)
    kT0 = alloc("kT0", [128, 4, 2048], BF16)
    v0 = alloc("v0", [128, 17, 512], BF16)
    KT1_BASE = cur[0]
    kT1 = [alloc(f"kT1_{i}", [128, 8, 512], BF16) for i in range(2)]
    v1 = [alloc(f"v1_{i}", [128, 4, 1024], BF16) for i in range(2)]
    v1s = alloc("v1s", [16, 1024], BF16)
    assert cur[0] >= KT1_BASE + 32768
    BB_BASE = cur[0]
    Bb = alloc("Bb", [128, 16, 640], BF16)
    uext = alloc("uext", [128, 4, 542], BF16)
    Dring = [alloc(f"D{i}", [128, 31, 128], BF16) for i in range(2)]
    identb = alloc("identb", [128, 128], BF16)
    identf = alloc("identf", [128, 128], F32)
    Tst = alloc("Tst", [128, 128], BF16)
    onesb = alloc("onesb", [128, 128], BF16)
    Umat = alloc("Umat", [128, 128], BF16)
    zerosb = alloc("zerosb", [128, 512], BF16)
    maskU = alloc("maskU", [128, 128], BF16)
    maskS = alloc("maskS", [16, 128], BF16)
    gT = alloc("gT", [128, 4, 8], F32)
    gfin = alloc("gfin", [128, 1024], F32)
    dwb_bc = alloc("dwb_bc", [128, 512], F32)
    lng_bc = alloc("lng_bc", [128, 512], F32)
    lnb_bc = alloc("lnb_bc", [128, 512], F32)
    dwT = alloc("dwT", [128, 4, 31], F32)
    stat = alloc("stat", [128, 64], F32)
    rs16 = alloc("rs16", [128, 32], F32)
    ARENA = cur[0]
    ARENA_SZ = 229376 - ARENA

    def ar(name, shape, dt, off):
        return alloc(name, shape, dt, at=ARENA + off)

    hT = ar("hT", [128, 8, 512], BF16, 0)
    qT0 = ar("qT0", [128, 4, 512], BF16, 8192)
    sg = ar("sg", [128, 4, 512], F32, 12288)
    aoT = ar("aoT", [128, 8, 512], BF16, 20480)
    cT = ar("cT", [128, 4, 512], BF16, 28672)
    SBT = [(ar("sbEA0", [128, 512], F32, 32768), ar("sbL20", [128, 512], F32, 34816),
            ar("sbLb0", [128, 512], BF16, 36864), ar("sbWb0", [128, 512], BF16, 37888)),
           (ar("sbEA1", [128, 512], F32, 12288), ar("sbL21", [128, 512], F32, 14336),
            ar("sbLb1", [128, 512], BF16, 16384), ar("sbWb1", [128, 512], BF16, 17408)),
           (ar("sbEA2", [128, 512], F32, 0), ar("sbL22", [128, 512], F32, 2048),
            ar("sbLb2", [128, 512], BF16, 4096), ar("sbWb2", [128, 512], BF16, 5120))]
    SBT.append((ar("sbEA3", [128, 512], F32, 40960), ar("sbL23", [128, 512], F32, 43008),
                ar("sbLb3", [128, 512], BF16, 45056), ar("sbWb3", [128, 512], BF16, 46080)))
    Eb = ar("Eb", [128, 512], F32, 32768)
    junk = ar("junk", [128, 1024], BF16, 32768)
    L2b = ar("L2b", [128, 512], F32, 34816)
    Ab = ar("Ab", [128, 512], F32, 36864)
    Lbb = ar("Lbb", [128, 512], BF16, 38912)
    Wbb = ar("Wbb", [128, 512], BF16, 39936)
    kst = ar("kst", [128, 512], F32, 40960)
    vst = ar("vst", [128, 512], F32, 43008)
    stg = ar("stg", [128, 1024], F32, 45056)
    ysb = ar("ysb", [128, 512], F32, 49152)
    cst = ar("cst", [32, 512], F32, 49152)
    ctm = ar("ctm", [128, 512], BF16, 51200)
    ysb2 = ar("ysb2", [128, 512], F32, 45056)
    ctm2 = ar("ctm2", [128, 512], BF16, 45056 + 2048)
    hn = ar("hn", [128, 1024], BF16, 52224)
    hn2 = ar("hn2", [128, 1024], BF16, 61440)
    utail = ar("utail", [128, 4, 64], F32, 54272)
    uexts = ar("uexts", [128, 4, 4, 46], BF16, 55296)
    kTs = ar("kTs", [128, 4, 64], BF16, 56832)
    L0_END = 57344
    hidT = ar("hidT", [128, 32, 512], BF16, 8192)
    rl = [ar(f"rl{i}", [128, 512], F32, 40960 + 2048 * i) for i in range(2)]
    qT1 = ar("qT1", [128, 8, 512], BF16, 8192)
    oT = ar("oT", [128, 8, 512], BF16, 16384)
    Sp = [ar(f"Sp{i}", [128, 640], F32, 24576 + 2560 * i) for i in range(2)]
    Pe = [ar(f"Pe{i}", [128, 640], BF16, 29696 + 1280 * i) for i in range(2)]
    PT = [ar(f"PT{i}", [128, 5, 128], BF16, 32256 + 1280 * i) for i in range(2)]
    osb = ar("osb", [128, 1024], BF16, 34816)
    kst1 = ar("kst1", [128, 1024], F32, 36864)
    vst1 = ar("vst1", [128, 1024], F32, 40960)
    kT1s = ar("kT1s", [128, 8, 64], BF16, 54272)
    assert 63488 <= ARENA_SZ, (ARENA, ARENA_SZ)
    kT0b = alloc("kT0b", [128, 4, 2048], BF16, at=KT1_BASE)
    v0b = alloc("v0b", [128, 17, 512], BF16, at=KT1_BASE + 16384)
    cvf = [alloc("cvf0", [128, 4096], F32, at=KT1_BASE), alloc("cvf1", [128, 4096], F32, at=KT1_BASE + 16384)]
    cvb = [alloc("cvb0", [128, 4096], BF16, at=BB_BASE), alloc("cvb1", [128, 4096], BF16, at=BB_BASE + 8192)]
    vnew0 = ar("vnew0", [16, 4, 512], BF16, 57344)
    vnew1 = ar("vnew1", [16, 4, 1024], BF16, 55296)

    ps = nc.alloc_psum_tensor("ps", [128, 8, 512], F32)
    PB = [Res(f"psb{i}", excl=True) for i in range(8)]

    def psf(b):
        return ps[:, b, :]

    def psb16(b):
        return ps[:, b, :].bitcast(BF16)

    R = {}

    def res(name):
        if name not in R:
            R[name] = Res(name)
        return R[name]


    dcount = [0]

    def dma(out, in_, reads=(), writes=(), grp=None, slow=False):
        if grp is None:
            grp = S.group(f"d{dcount[0]}")
            dcount[0] += 1
        if slow:
            f = lambda e: e.dma_start(out=out, in_=in_, allow_slow_non_contiguous=True)
        else:
            f = lambda e: e.dma_start(out=out, in_=in_)
        return op("sp", f, reads=reads, writes=writes, dma=grp)

    G = {}

    def grp(name):
        if name not in G:
            G[name] = S.group(name)
        return G[name]

    def setup_consts():
        op("pool", lambda e: e.memset(identb[:], 1.0), writes=[res("identb")])
        op("pool", lambda e: e.affine_select(out=identb[:], in_=identb[:], pattern=[[-1, 128]], compare_op=ALU.is_equal,
                                             fill=0.0, base=0, channel_multiplier=1), writes=[res("identb")])
        op("pool", lambda e: e.memset(identf[:], 1.0), writes=[res("identf")])
        op("pool", lambda e: e.affine_select(out=identf[:], in_=identf[:], pattern=[[-1, 128]], compare_op=ALU.is_equal,
                                             fill=0.0, base=0, channel_multiplier=1), writes=[res("identf")])
        op("pool", lambda e: e.memset(Tst[:], 1.0), writes=[res("Tst")])
        op("pool", lambda e: e.affine_select(out=Tst[:], in_=Tst[:], pattern=[[-1, 128]], compare_op=ALU.is_gt,
                                             fill=0.0, base=0, channel_multiplier=1), writes=[res("Tst")])
        op("pool", lambda e: e.memset(maskU[:], 1.0), writes=[res("maskU")])
        op("pool", lambda e: e.affine_select(out=maskU[:], in_=maskU[:], pattern=[[1, 128]], compare_op=ALU.is_gt,
                                             fill=0.0, base=0, channel_multiplier=-1), writes=[res("maskU")])
        op("pool", lambda e: e.memset(maskS[:], 1.0), writes=[res("maskS")])
        op("pool", lambda e: e.affine_select(out=maskS[:], in_=maskS[:], pattern=[[0, 8], [1, 16]], compare_op=ALU.is_gt,
                                             fill=0.0, base=0, channel_multiplier=-1), writes=[res("maskS")])
        op("pool", lambda e: e.memset(onesb[:], 1.0), writes=[res("onesb")])
        op("pool", lambda e: e.memset(Umat[:], 1.0), writes=[res("Umat")])
        op("pool", lambda e: e.affine_select(out=Umat[:], in_=Umat[:], pattern=[[1, 128]], compare_op=ALU.is_ge,
                                             fill=0.0, base=0, channel_multiplier=-1), writes=[res("Umat")])
        op("pool", lambda e: e.memset(zerosb[:], 0.0), writes=[res("zerosb")])
        op("pool", lambda e: e.memset(uext[:], 0.0), writes=[res("uext")])
        for n, src in enumerate([norm_mix[0], norm_ffn[0], norm_mix[1], norm_ffn[1]]):
            dma(gT[:, n, :], src.rearrange("(c p) -> p c", p=128), writes=[res("gT")], grp=grp("gT"), slow=True)
        dma(gfin[:], norm_final.partition_broadcast(128), writes=[res("gfin")], grp=grp("gfin"))
        dma(dwb_bc[:], dw_b.partition_broadcast(128), writes=[res("cb")], grp=grp("cb"))
        dma(lng_bc[:], ln_g.partition_broadcast(128), writes=[res("cb")], grp=grp("cb"))
        dma(lnb_bc[:], ln_b.partition_broadcast(128), writes=[res("cb")], grp=grp("cb"))
        dma(stg[0:31, 0:512], dw_w, writes=[res("stg")], grp=grp("stg"))
        for c in range(4):
            op("pe", lambda e, c=c: e.transpose(out=ps[:, 0, c * 32:c * 32 + 31], in_=stg[0:31, c * 128:(c + 1) * 128],
                                                identity=identf[0:31, 0:31]),
               reads=[res("stg"), res("identf")], writes=[PB[0]])
        op("dve", lambda e: e.tensor_copy(out=dwT[:], in_=ps[:, 0, 0:128].rearrange("p (c i) -> p c i", c=4)[:, :, 0:31]),
           reads=[PB[0]], writes=[res("dwT")])
    def setup_bias():
        for h in range(16):
            dma(stg[:, 0:640], bfull[h], writes=[res("stg")], grp=grp("stg"))
            eng = "act" if h % 2 == 0 else "dve"
            if eng == "act":
                op("act", lambda e, h=h: e.copy(out=Bb[:, h, :], in_=stg[:, 0:640]), reads=[res("stg")], writes=[res("Bb")])
            else:
                op("dve", lambda e, h=h: e.tensor_copy(out=Bb[:, h, :], in_=stg[:, 0:640]), reads=[res("stg")], writes=[res("Bb")])
        op("pool", lambda e: e.memset(Bb[0:64, :, 576:640], NEG), writes=[res("Bb")])
        op("pool", lambda e: e.memset(Bb[64:128, :, 0:64], NEG), writes=[res("Bb")])

    def wtile_specs():
        sp = []

        def colt(w, c0):
            return (w[:, c0:c0 + 512].rearrange("(k p) c -> p k c", p=128), 128, 8)
        for c0 in (2048, 1536, 0, 512, 1024):
            sp.append(colt(w_in_ab, c0))
        for o in range(2):
            sp.append((w_out_ab[0:512, o * 512:(o + 1) * 512].rearrange("(h p) c -> p h c", p=64), 64, 8))
            sp.append((w_out_ab[512:1024, o * 512:(o + 1) * 512].rearrange("(k p) c -> p k c", p=128), 128, 4))
        for l in range(2):
            if l == 1:
                for c0 in range(0, 3072, 512):
                    sp.append(colt(w_in_c, c0))
                for o in range(2):
                    sp.append(colt(w_out_c, o * 512))
            for j in range(8):
                sp.append(colt(w_up[l], j * 512))
            for o in range(2):
                for g in range(4):
                    sp.append((w_down[l][g * 1024:(g + 1) * 1024, o * 512:(o + 1) * 512].rearrange("(k p) c -> p k c", p=128), 128, 8))
        assert len(sp) == N_WT
        return sp

    WSPEC = wtile_specs()

    def convert_gen():
        def load(n):
            src, P, K = WSPEC[n]
            i = n % 2
            fv = cvf[i][0:P, 0:K * 512].rearrange("p (k c) -> p k c", k=K)
            dma(fv, src, writes=[res(f"cvf{i}")], grp=grp(f"cvf{i}"))

        def cast_store(n):
            src, P, K = WSPEC[n]
            i = n % 2
            eng = ("act", "dve", "pool")[n % 3]
            if eng == "act":
                op("act", lambda e: e.copy(out=cvb[i][0:P, 0:K * 512], in_=cvf[i][0:P, 0:K * 512]),
                   reads=[res(f"cvf{i}")], writes=[res(f"cvb{i}")])
            else:
                op(eng, lambda e: e.tensor_copy(out=cvb[i][0:P, 0:K * 512], in_=cvf[i][0:P, 0:K * 512]),
                   reads=[res(f"cvf{i}")], writes=[res(f"cvb{i}")])
            dma(wsc[n, 0:P, 0:K * 512], cvb[i][0:P, 0:K * 512], reads=[res(f"cvb{i}")], writes=[res(f"wsc{n}")],
                grp=grp(f"cvb{i}"))
        load(0)
        for n in range(N_WT):
            if n + 1 < N_WT:
                load(n + 1)
            cast_store(n)
            yield

    BG = {"gen": None, "busy": False, "every": 45, "cnt": 0}

    def bg_advance(k=1):
        if BG["gen"] is None or BG["busy"]:
            return
        BG["busy"] = True
        try:
            for _ in range(k):
                next(BG["gen"])
        except StopIteration:
            BG["gen"] = None
        BG["busy"] = False

    def bg_flush():
        while BG["gen"] is not None:
            bg_advance(1)

    _raw_op = S.op

    def op(eng, fn, reads=(), writes=(), dma=None):
        t = _raw_op(eng, fn, reads=reads, writes=writes, dma=dma)
        if BG["gen"] is not None and not BG["busy"]:
            BG["cnt"] += 1
            if BG["cnt"] >= BG["every"]:
                BG["cnt"] = 0
                bg_advance(1)
        return t

    class WStream:
        def __init__(self):
            self.issued = 0
            self.taken = 0

        def _issue(self):
            n = self.issued
            t = n % N_WT
            slot = n % 3
            _, P, K = WSPEC[t]
            dma(wring[slot][0:P, 0:K * 512], wsc[t, 0:P, 0:K * 512], reads=[res(f"wsc{t}")],
                writes=[res(f"wring{slot}")], grp=grp(f"wring{slot}"))
            if P == 64:
                dma(wring[slot][64:128, 0:K * 512], wsc[t, 0:P, 0:K * 512], reads=[res(f"wsc{t}")],
                    writes=[res(f"wring{slot}")], grp=grp(f"wring{slot}"))
            self.issued += 1

        def next(self, total):
            while self.issued < min(self.taken + 2, total):
                self._issue()
            n = self.taken
            self.taken += 1
            slot = n % 3
            _, P, K = WSPEC[n % N_WT]
            P = 128 if P == 64 else P
            return wring[slot][0:P, 0:K * 512].rearrange("p (k c) -> p k c", k=K), res(f"wring{slot}")

    WS = WStream()
    TOTAL_TILES = 9 * N_WT

    def wnext():
        return WS.next(TOTAL_TILES)

    bank_rr = [0]

    BANKSEL = [None]

    def bank4():
        sel = BANKSEL[0] or (0, 1, 2, 3)
        b = sel[bank_rr[0] % len(sel)]
        bank_rr[0] += 1
        return b

    def evac_copy(eng, out, in_, reads, writes, scale=None):
        if eng == "act":
            if scale is None:
                op("act", lambda e: e.copy(out=out, in_=in_), reads=reads, writes=writes)
            else:
                op("act", lambda e: e.activation(out=out, in_=in_, func=AF.Copy, scale=scale), reads=reads, writes=writes)
        else:
            if scale is None:
                op(eng, lambda e: e.tensor_copy(out=out, in_=in_), reads=reads, writes=writes)
            else:
                op(eng, lambda e: e.tensor_scalar(out=out, in0=in_, scalar1=scale, scalar2=None, op0=ALU.mult),
                   reads=reads, writes=writes)

    def rmsnorm(P, NT, gidx, final=False):
        for t in range(NT):
            op("act", lambda e, t=t: e.activation(out=junk[0:P, :], in_=xtok[0:P, t, :], func=AF.Square,
                                                  accum_out=stat[0:P, t:t + 1]),
               reads=[res("xtok")], writes=[res(f"nss{t}"), res("junk")])
        op("act", lambda e: e.activation(out=stat[0:P, 8:8 + NT], in_=stat[0:P, 0:NT], func=AF.Sqrt,
                                         scale=1.0 / 1024.0, bias=RMS_EPS),
           reads=[res(f"nss{t}") for t in range(NT)], writes=[res("nsd")])
        op("dve", lambda e: e.reciprocal(out=stat[0:P, 16:16 + NT], in_=stat[0:P, 8:8 + NT]),
           reads=[res("nsd")], writes=[res("nrs")])
        if final:
            for t in range(NT):
                op("dve", lambda e, t=t: e.scalar_tensor_tensor(out=xtok[0:P, t, :], in0=xtok[0:P, t, :],
                                                                scalar=stat[0:P, 16 + t:17 + t], in1=gfin[0:P, :],
                                                                op0=ALU.mult, op1=ALU.mult),
                   reads=[res("nrs"), res("gfin")], writes=[res("xtok")])
            return
        hbuf = [hn, hn2]

        def scale(t):
            op("dve", lambda e: e.tensor_scalar(out=hbuf[t % 2][0:P, :], in0=xtok[0:P, t, :], scalar1=stat[0:P, 16 + t:17 + t],
                                                scalar2=None, op0=ALU.mult),
               reads=[res("xtok"), res("nrs")], writes=[res(f"hn{t % 2}")])

        def tr_evac(t):
            bk = 4 + (t % 2)
            for kc in range(8):
                op("pe", lambda e, kc=kc: e.transpose(out=psb16(bk)[:, kc * P:(kc + 1) * P],
                                                      in_=hbuf[t % 2][0:P, kc * 128:(kc + 1) * 128], identity=identb[0:P, 0:P]),
                   reads=[res(f"hn{t % 2}"), res("identb")], writes=[PB[bk]])
            op("dve", lambda e: e.tensor_tensor(
                out=hT[:, :, t * P:(t + 1) * P],
                in0=psb16(bk)[:, 0:8 * P].rearrange("p (k j) -> p k j", k=8),
                in1=gT[:, gidx, :].unsqueeze(2).broadcast_to([128, 8, P]), op=ALU.mult),
               reads=[PB[bk], res("gT")], writes=[res("hT")])
        scale(0)
        if NT > 1:
            scale(1)
        for t in range(NT):
            tr_evac(t)
            if t + 2 < NT:
                scale(t + 2)

    def proj_fm(wt, wres, col0, nchunks, N, evac):
        for c in range(nchunks):
            b = bank4()
            for kc in range(8):
                op("pe", lambda e, c=c, kc=kc, b=b: e.matmul(ps[:, b, 0:N], lhsT=wt[:, kc, col0 + c * 128:col0 + (c + 1) * 128],
                                                             rhs=hT[:, kc, 0:N], start=(kc == 0), stop=(kc == 7)),
                   reads=[wres, res("hT")], writes=[PB[b]])
            evac(c, b)

    def proj_fm_g(wt, wres, col0, nchunks, N, evac):
        for c in range(nchunks):
            b = bank4()
            for kc in range(8):
                op("pe", lambda e, c=c, kc=kc, b=b: e.matmul(ps[:, b, 0:N], lhsT=wt[:, kc, col0 + c * 128:col0 + (c + 1) * 128],
                                                             rhs=hT[:, kc, 0:N], start=(kc == 0), stop=(kc == 7)),
                   reads=[wres, res("hT")], writes=[PB[b]])
            evac(c, b)
            yield

    def proj_tm_g(wt, wres, tiles, evac):
        for ti, (lo, P) in enumerate(tiles):
            b = bank4()
            for kc in range(8):
                op("pe", lambda e, kc=kc, b=b, lo=lo, P=P: e.matmul(ps[0:P, b, :], lhsT=hT[:, kc, lo:lo + P], rhs=wt[:, kc, :],
                                                                    start=(kc == 0), stop=(kc == 7)),
                   reads=[wres, res("hT")], writes=[PB[b]])
            evac(ti, lo, P, b)
            yield

    def proj_tm(wt, wres, tiles, evac):
        for ti, (lo, P) in enumerate(tiles):
            b = bank4()
            for kc in range(8):
                op("pe", lambda e, kc=kc, b=b, lo=lo, P=P: e.matmul(ps[0:P, b, :], lhsT=hT[:, kc, lo:lo + P], rhs=wt[:, kc, :],
                                                                    start=(kc == 0), stop=(kc == 7)),
                   reads=[wres, res("hT")], writes=[PB[b]])
            evac(ti, lo, P, b)

    def run_pool(factories, nslots, stagger=0, filler=None):
        pending = list(factories)
        active = {}
        rnd = 0
        while pending or active:
            for slot in range(nslots):
                if slot not in active and pending and rnd >= slot * stagger:
                    active[slot] = pending.pop(0)(slot)
            for slot in list(active):
                try:
                    next(active[slot])
                except StopIteration:
                    del active[slot]
            rnd += 1
            if filler is not None and active:
                filler(rnd)

    def sb_banks(slot, n_, three):
        if three == 4:
            return slot % 2, 2 + slot, 6 + slot % 2, (0, 0, 64, 64)[slot]
        if three:
            return slot, 3 + slot, (6, 7, 6)[slot], (0, 0, 64)[slot]
        return slot, 4 + slot, 6 + slot, 0

    def sb_head(slot, steps, evac, split=False, three=False):
        EA, L2t, Lbt, Wbt = SBT[slot]
        if slot == 3:
            rEA, rL2, rLb, rWb = res("kst"), res("vst"), res("stg"), res("stg")
        else:
            rEA, rL2, rLb, rWb = [res(f"sb{n}{slot}") for n in ("EA", "L2", "Lb", "Wb")]
        _, Xb, Ob, opo = sb_banks(slot, 0, three)
        op("pe", lambda e: e.matmul(ps[:, Xb, :], lhsT=zerosb[:, 0:128], rhs=zerosb[:, 0:512], start=True, stop=True),
           reads=[res("zerosb")], writes=[PB[Xb]])
        op("pe", lambda e: e.matmul(ps[opo:opo + 64, Ob, :], lhsT=zerosb[:, 0:64], rhs=zerosb[:, 0:512], start=True, stop=True),
           reads=[res("zerosb")], writes=[PB[Ob]])
        yield
        for n_, stp in enumerate(steps):
            yield from sb_one_step(slot, n_, stp, split, EA, L2t, Lbt, Wbt, rEA, rL2, rLb, rWb, Xb, Ob, opo, three)
        evac(Ob, opo)
        yield

    def sb_one_step(slot, n_, stp, split, EA, L2t, Lbt, Wbt, rEA, rL2, rLb, rWb, Xb, Ob, opo, three):
        if True:
            KK, zparts, c0, N, mask, vparts, kres, vres = stp
            if split:
                zb0 = 2 * slot
                zv = ps[0:KK, zb0:zb0 + 2, 0:64]
                zres = [PB[zb0], PB[zb0 + 1]]
                v3 = lambda ap: ap.rearrange("p (a b) -> p a b", a=2)
            else:
                zb0 = sb_banks(slot, n_, three)[0]
                zv = ps[0:KK, zb0, c0:N]
                zres = [PB[zb0]]
                v3 = lambda ap: ap
            for (l, r, zsel, a_, b_) in zparts:
                bk = zb0 + zsel
                op("pe", lambda e, l=l, r=r, bk=bk, a_=a_, b_=b_: e.matmul(ps[0:KK, bk, a_:b_], lhsT=l, rhs=r, start=True, stop=True),
                   reads=list(kres) + [res("qT0")], writes=[PB[bk]])
            yield
            op("act", lambda e: e.activation(out=v3(EA[0:KK, c0:N]), in_=zv, func=AF.Exp, scale=-1.0), reads=zres, writes=[rEA])
            yield
            op("act", lambda e: e.activation(out=L2t[0:KK, c0:N], in_=EA[0:KK, c0:N], func=AF.Ln, bias=1.0), reads=[rEA], writes=[rL2])
            yield
            op("dve", lambda e: e.tensor_tensor(out=v3(Lbt[0:KK, c0:N]), in0=v3(L2t[0:KK, c0:N]), in1=zv, op=ALU.add),
               reads=[rL2] + zres, writes=[rLb])
            yield
            if mask is not None:
                m_ap, m0, m1 = mask
                op("pool", lambda e: e.tensor_tensor(out=Lbt[0:KK, m0:m1], in0=Lbt[0:KK, m0:m1], in1=m_ap, op=ALU.mult),
                   reads=[res("maskU"), res("maskS")], writes=[rLb])
                yield
            op("pe", lambda e: e.matmul(ps[0:KK, Xb, c0:N], lhsT=Tst[0:KK, 0:KK], rhs=Lbt[0:KK, c0:N], start=False, stop=True,
                                        skip_group_check=True),
               reads=[res("Tst"), rLb], writes=[PB[Xb]])
            yield
            op("dve", lambda e: e.tensor_tensor(out=EA[0:KK, c0:N], in0=L2t[0:KK, c0:N], in1=ps[0:KK, Xb, c0:N], op=ALU.add),
               reads=[rL2, PB[Xb]], writes=[rEA])
            yield
            op("act", lambda e: e.activation(out=Wbt[0:KK, c0:N], in_=EA[0:KK, c0:N], func=AF.Exp, scale=-1.0), reads=[rEA], writes=[rWb])
            yield
            if mask is not None:
                m_ap, m0, m1 = mask
                op("pool", lambda e: e.tensor_tensor(out=Wbt[0:KK, m0:m1], in0=Wbt[0:KK, m0:m1], in1=m_ap, op=ALU.mult),
                   reads=[res("maskU"), res("maskS")], writes=[rWb])
                yield
            for (vl, a_, b_) in vparts:
                op("pe", lambda e, vl=vl, a_=a_, b_=b_: e.matmul(ps[opo:opo + 64, Ob, a_:b_], lhsT=vl, rhs=Wbt[0:KK, a_:b_], start=False, stop=True,
                                                                 skip_group_check=True),
                   reads=list(vres) + [rWb], writes=[PB[Ob]])
            op("pe", lambda e: e.matmul(ps[:, Xb, c0:N], lhsT=Umat[0:KK, :], rhs=Lbt[0:KK, c0:N], start=False, stop=True,
                                        skip_group_check=True),
               reads=[res("Umat"), rLb], writes=[PB[Xb]])
            if three == 4:
                for _ in range(SB4_DUMMY):
                    op("pe", lambda e: e.matmul(ps[:, Xb, 0:512], lhsT=zerosb[:, 0:128], rhs=zerosb[:, 0:512], start=False, stop=True,
                                                skip_group_check=True),
                       reads=[res("zerosb")], writes=[PB[Xb]])
            yield

    def conv_gen(tiles):
        for c in range(4):
            D = Dring[c % 2]
            dres = res(f"D{c % 2}")
            op("pool", lambda e, c=c, D=D: e.tensor_tensor(out=D[:], in0=identb[:].unsqueeze(1).broadcast_to([128, 31, 128]),
                                                           in1=dwT[:, c, :].unsqueeze(2).broadcast_to([128, 31, 128]), op=ALU.mult),
               reads=[res("identb"), res("dwT")], writes=[dres])
            for ti, (ufn, P, lo) in enumerate(tiles):
                for i in range(31):
                    op("pe", lambda e, ti=ti, ufn=ufn, P=P, c=c, i=i, D=D: e.matmul(
                        ps[0:P, ti, c * 128:(c + 1) * 128], lhsT=ufn(c, i, P), rhs=D[:, i, :], start=(i == 0), stop=(i == 30)),
                       reads=[dres, res("uext")], writes=[PB[ti]])
                yield

    def conv_module(tiles):
        nt = len(tiles)
        for c in range(4):
            D = Dring[c % 2]
            dres = res(f"D{c % 2}")
            op("pool", lambda e, c=c, D=D: e.tensor_tensor(out=D[:], in0=identb[:].unsqueeze(1).broadcast_to([128, 31, 128]),
                                                           in1=dwT[:, c, :].unsqueeze(2).broadcast_to([128, 31, 128]), op=ALU.mult),
               reads=[res("identb"), res("dwT")], writes=[dres])
            for ti, (ufn, P, lo) in enumerate(tiles):
                for i in range(31):
                    op("pe", lambda e, ti=ti, ufn=ufn, P=P, c=c, i=i, D=D: e.matmul(
                        ps[0:P, ti, c * 128:(c + 1) * 128], lhsT=ufn(c, i, P), rhs=D[:, i, :], start=(i == 0), stop=(i == 30)),
                       reads=[dres, res("uext")], writes=[PB[ti]])
                    if CONV_DUMMY and P == 128 and i % CONV_DUMMY == CONV_DUMMY - 1:
                        op("pe", lambda e: e.matmul(ps[:, 6, :], lhsT=zerosb[:, 0:128], rhs=zerosb[:, 0:512], start=True, stop=True,
                                                    skip_group_check=True), reads=[res("zerosb")], writes=[PB[6]])
        for ti, (ufn, P, lo) in enumerate(tiles):
            conv_epilogue(ti, P, lo)

    def conv_epilogue(ti, P, lo):
        if ti % 2 == 0:
            yb, cb_, ry, rc = ysb, ctm, res("ysb"), res("ctm")
        else:
            yb, cb_, ry, rc = ysb2, ctm2, res("stg"), res("stg")
        so = 32 + 16 * (ti % 2)
        rst = res(f"cstat{ti % 2}")
        op("dve", lambda e: e.tensor_tensor(out=yb[0:P, :], in0=ps[0:P, ti, :], in1=dwb_bc[0:P, :], op=ALU.add),
           reads=[PB[ti], res("cb")], writes=[ry])
        op("dve", lambda e: e.bn_stats(out=stat[0:P, so:so + 6], in_=yb[0:P, :]), reads=[ry], writes=[rst])
        op("dve", lambda e: e.bn_aggr(out=stat[0:P, so + 6:so + 8], in_=stat[0:P, so:so + 6]), reads=[rst], writes=[rst])
        op("act", lambda e: e.activation(out=stat[0:P, so + 8:so + 9], in_=stat[0:P, so + 7:so + 8], func=AF.Sqrt, scale=1.0, bias=LN_EPS),
           reads=[rst], writes=[rst])
        op("dve", lambda e: e.reciprocal(out=stat[0:P, so + 9:so + 10], in_=stat[0:P, so + 8:so + 9]), reads=[rst], writes=[rst])
        op("dve", lambda e: e.tensor_scalar(out=yb[0:P, :], in0=yb[0:P, :], scalar1=stat[0:P, so + 6:so + 7], scalar2=stat[0:P, so + 9:so + 10],
                                            op0=ALU.subtract, op1=ALU.mult),
           reads=[rst], writes=[ry])
        op("pool", lambda e: e.tensor_tensor(out=yb[0:P, :], in0=yb[0:P, :], in1=lng_bc[0:P, :], op=ALU.mult),
           reads=[res("cb")], writes=[ry])
        op("pool", lambda e: e.tensor_tensor(out=yb[0:P, :], in0=yb[0:P, :], in1=lnb_bc[0:P, :], op=ALU.add),
           reads=[res("cb")], writes=[ry])
        if rc is ry:
            op("act", lambda e: e.activation(out=cb_[0:P, :], in_=yb[0:P, :], func=AF.Silu), reads=[], writes=[ry])
        else:
            op("act", lambda e: e.activation(out=cb_[0:P, :], in_=yb[0:P, :], func=AF.Silu), reads=[ry], writes=[rc])
        b = 4 + (ti % 2)
        for c in range(4):
            op("pe", lambda e, c=c: e.transpose(out=psb16(b)[:, c * P:(c + 1) * P], in_=cb_[0:P, c * 128:(c + 1) * 128],
                                                identity=identb[0:P, 0:P]),
               reads=[rc, res("identb")], writes=[PB[b]])
        op("act", lambda e: e.copy(out=cT[:, :, lo:lo + P], in_=psb16(b)[:, 0:4 * P].rearrange("p (c j) -> p c j", c=4)),
           reads=[PB[b]], writes=[res("cT")])

    def split512(a, e_):
        out = []
        while a < e_:
            bnd = (a // 512 + 1) * 512
            x = min(e_, bnd)
            out.append((a, x))
            a = x
        return out

    def band_head(slot, h, P, qcol, segs, groups, kmin, W):
        c, po = h // 2, 64 * (h % 2)
        sb_ = 2 * slot
        started = set()
        for (kfn, n, dst, kr) in segs:
            for (a, e_) in split512(dst, dst + n):
                bb = sb_ + a // 512
                first = bb not in started
                started.add(bb)
                op("pe", lambda e, a=a, e_=e_, bb=bb, kfn=kfn, dst=dst, first=first: e.matmul(
                    ps[0:P, bb, a % 512:(a % 512) + (e_ - a)], lhsT=qT1[po:po + 64, c, qcol:qcol + P],
                    rhs=kfn(c, po)[:, a - dst:e_ - dst], start=first, stop=True, skip_group_check=True),
                   reads=[res("qT1"), kr], writes=[PB[bb]])
        yield
        for (a, e_) in split512(kmin, W):
            bb = sb_ + a // 512
            op("pe", lambda e, a=a, e_=e_, bb=bb: e.matmul(ps[0:P, bb, a % 512:(a % 512) + (e_ - a)], lhsT=identb[0:P, 0:P],
                                                           rhs=Bb[0:P, h, a:e_], start=False, stop=True, skip_group_check=True),
               reads=[res("identb"), res("Bb")], writes=[PB[bb]])
        yield
        spv = ps[0:P, sb_:sb_ + 2, :].rearrange("p a b -> p (a b)")
        rneg = res(f"negm{slot}")
        op("dve", lambda e: e.tensor_reduce(out=stat[0:P, 48 + slot:49 + slot], in_=spv[:, kmin:W], axis=AX.X, op=ALU.max, negate=True),
           reads=[PB[sb_], PB[sb_ + 1]], writes=[rneg])
        yield
        op("act", lambda e: e.activation(out=Pe[slot][0:P, kmin:W], in_=spv[:, kmin:W], func=AF.Exp, bias=stat[0:P, 48 + slot:49 + slot],
                                         scale=1.0, accum_out=rs16[0:P, h:h + 1]),
           reads=[PB[sb_], PB[sb_ + 1], rneg], writes=[res(f"Pe{slot}"), res(f"rs{h}")])
        yield
        tb = 4 + slot
        ng = len(groups)
        for gi, (g0, KK, vfn, vr) in enumerate(groups):
            op("pe", lambda e, gi=gi, g0=g0, KK=KK: e.transpose(out=psb16(tb)[0:KK, gi * 128:gi * 128 + P],
                                                               in_=Pe[slot][0:P, g0:g0 + KK], identity=identb[0:P, 0:P]),
               reads=[res(f"Pe{slot}"), res("identb")], writes=[PB[tb]])
        yield
        eng = "act" if h % 2 == 0 else "dve"
        kkmax = max(g[1] for g in groups)
        if all(g[1] == kkmax for g in groups):
            evac_copy(eng, PT[slot][0:kkmax, 0:ng, 0:P], psb16(tb)[0:kkmax, 0:ng * 128].rearrange("p (g j) -> p g j", g=ng)[:, :, 0:P],
                      reads=[PB[tb]], writes=[res(f"PT{slot}")])
        else:
            evac_copy(eng, PT[slot][0:128, 0:ng - 1, 0:P], psb16(tb)[0:128, 0:(ng - 1) * 128].rearrange("p (g j) -> p g j", g=ng - 1)[:, :, 0:P],
                      reads=[PB[tb]], writes=[res(f"PT{slot}")])
            kl = groups[-1][1]
            evac_copy(eng, PT[slot][0:kl, ng - 1, 0:P], psb16(tb)[0:kl, (ng - 1) * 128:(ng - 1) * 128 + P],
                      reads=[PB[tb]], writes=[res(f"PT{slot}")])
        yield
        ob = 6 + h // 8
        for gi, (g0, KK, vfn, vr) in enumerate(groups):
            op("pe", lambda e, gi=gi, KK=KK, vfn=vfn: e.matmul(
                ps[0:P, ob, (h % 8) * 64:(h % 8 + 1) * 64], lhsT=PT[slot][0:KK, gi, 0:P], rhs=vfn(h),
                start=(gi == 0), stop=(gi == ng - 1)),
               reads=[res(f"PT{slot}"), vr], writes=[PB[ob]])
        yield

    def band_qtile(P, qcol, segs, groups, kmin, W, osb_done_cb, par):
        Ob = 6
        state = {"done": set(), "started8": False, "norm": [False, False]}

        def normalize(hb):
            state["norm"][hb] = True
            op("dve", lambda e: e.reciprocal(out=rs16[0:P, 16 + hb * 8:24 + hb * 8], in_=rs16[0:P, hb * 8:hb * 8 + 8]),
               reads=[res(f"rs{h}") for h in range(hb * 8, hb * 8 + 8)], writes=[res(f"rinv{hb}")])
            op("dve", lambda e: e.tensor_tensor(
                out=osb[0:P, hb * 512:(hb + 1) * 512].rearrange("p (h d) -> p h d", h=8),
                in0=ps[0:P, Ob + hb, :].rearrange("p (h d) -> p h d", h=8),
                in1=rs16[0:P, 16 + hb * 8:24 + hb * 8].unsqueeze(2).broadcast_to([P, 8, 64]), op=ALU.mult),
               reads=[PB[Ob + hb], res(f"rinv{hb}")], writes=[res(f"osb{hb}")])

        def head_gen(slot, h):
            if h >= 8:
                state["started8"] = True
            yield from band_head(slot, h, P, qcol, segs, groups, kmin, W)
            state["done"].add(h)
            if not state["norm"][0] and all(x in state["done"] for x in range(8)):
                normalize(0)

        def filler(rnd):
            if not BAND_DUMMY or P < 128:
                return
            if not state["started8"]:
                bk = Ob + 1
            elif state["norm"][0]:
                bk = Ob
            else:
                return
            for _ in range(BAND_DUMMY):
                op("pe", lambda e: e.matmul(ps[:, bk, 0:BAND_DUMMY_N], lhsT=zerosb[:, 0:128], rhs=zerosb[:, 0:BAND_DUMMY_N],
                                            start=True, stop=True, skip_group_check=True),
                   reads=[res("zerosb")], writes=[PB[bk]])
        run_pool([(lambda slot, h=h: head_gen(slot, h)) for h in range(16)], 2, stagger=STG_BAND, filler=filler)
        if not state["norm"][0]:
            normalize(0)
        normalize(1)
        tb = 4 + par
        for kc in range(8):
            op("pe", lambda e, kc=kc, tb=tb: e.transpose(out=psb16(tb)[:, kc * P:(kc + 1) * P], in_=osb[0:P, kc * 128:(kc + 1) * 128],
                                                         identity=identb[0:P, 0:P]),
               reads=[res(f"osb{kc // 4}"), res("identb")], writes=[PB[tb]])
        op("act", lambda e, tb=tb: e.copy(out=oT[:, :, qcol:qcol + P], in_=psb16(tb)[:, 0:8 * P].rearrange("p (k j) -> p k j", k=8)),
           reads=[PB[tb]], writes=[res("oT")])

    def mlp(l, P, NT):
        N = P * NT
        rmsnorm(P, NT, 1 + 2 * l)
        for j in range(8):
            wu, wr = wnext()

            def ev(c, b, j=j):
                i2 = (j * 4 + c) % 2
                op("act", lambda e: e.activation(out=rl[i2][:, 0:N], in_=ps[:, b, 0:N], func=AF.Relu),
                   reads=[PB[b]], writes=[res(("kst", "vst")[i2]), res("vst1")])
                op("dve", lambda e: e.tensor_tensor(out=hidT[:, j * 4 + c, 0:N], in0=rl[i2][:, 0:N], in1=rl[i2][:, 0:N], op=ALU.mult),
                   reads=[res(("kst", "vst")[i2])], writes=[res("hidT"), res("kst1")])
            proj_fm(wu, wr, 0, 4, N, ev)
        for o in range(2):
            for g in range(4):
                wd, wr = wnext()
                for t in range(NT):
                    for j in range(8):
                        op("pe", lambda e, t=t, j=j, g=g, wd=wd: e.matmul(ps[0:P, t, :], lhsT=hidT[:, g * 8 + j, t * P:(t + 1) * P], rhs=wd[:, j, :],
                                                                          start=(g == 0 and j == 0), stop=(g == 3 and j == 7)),
                           reads=[wr, res("hidT")], writes=[PB[t]])
            for t in range(NT):
                op("dve", lambda e, t=t, o=o: e.tensor_tensor(out=xtok[0:P, t, o * 512:(o + 1) * 512], in0=xtok[0:P, t, o * 512:(o + 1) * 512],
                                                              in1=ps[0:P, t, :], op=ALU.add),
                   reads=[PB[t]], writes=[res("xtok")])

    def out_proj_resid(P, NT, parts_fn):
        for o in range(2):
            parts = parts_fn(o)
            for t in range(NT):
                b = bank4()
                n = len(parts)
                for pi, (lf, rhs, rd) in enumerate(parts):
                    op("pe", lambda e, lf=lf, rhs=rhs, t=t, b=b, pi=pi: e.matmul(ps[0:P, b, :], lhsT=lf(t), rhs=rhs, start=(pi == 0), stop=(pi == n - 1)),
                       reads=rd, writes=[PB[b]])
                op("dve", lambda e, t=t, o=o, b=b: e.tensor_tensor(out=xtok[0:P, t, o * 512:(o + 1) * 512], in0=xtok[0:P, t, o * 512:(o + 1) * 512],
                                                                    in1=ps[0:P, b, :], op=ALU.add),
                   reads=[PB[b]], writes=[res("xtok")])

    FIRST_L1_DONE = [False]
    HI = {}

    def run_block(sample, s=0, i=0):
        def st(k):
            MARKS.append((("S" if sample else f"P{s}{i}"), k, S.nops["pe"]))
            stop(k + (100 if sample else 0))
        st(0)
        if sample:
            P, NT, N = 64, 1, 64
            kvt = [(16 * b, 16) for b in range(4)]
            dma(xtok[0:64, 0, :], xs, writes=[res("xtok")], grp=grp("xtok"))
        else:
            P, NT, N = 128, 4, 512
            kvt = [(128 * t, 128) for t in range(4)]
            t0 = 512 * i
            dma(xtok[:, :, :], xp[s, t0:t0 + 512, :].rearrange("(t p) f -> p t f", p=128), writes=[res("xtok")], grp=grp("xtok"))
            if i == 0:
                op("pool", lambda e: e.memset(uext[:, :, 0:30], 0.0), writes=[res("uext")])

        if not sample:
            DUMP[0] = lambda: dma(yp[s, t0:t0 + 512, :].rearrange("(t p) f -> p t f", p=128), xtok[:, :, :], reads=[res("xtok")], grp=grp("ystore"))
        rmsnorm(P, NT, 0)
        wt, wr = wnext()
        proj_fm(wt, wr, 0, 4, N, lambda c, b: op("act", lambda e: e.activation(out=sg[:, c, 0:N], in_=ps[:, b, 0:N], func=AF.Sigmoid),
                                                 reads=[PB[b]], writes=[res("sg")]))
        wt, wr = wnext()

        def ev_a(c, b):
            if sample:
                for q in range(4):
                    op("dve", lambda e, q=q: e.tensor_tensor(out=uexts[:, q, c, 30:46], in0=ps[:, b, 16 * q:16 * q + 16], in1=sg[:, c, 16 * q:16 * q + 16], op=ALU.mult),
                       reads=[PB[b], res("sg")], writes=[res("uext")])
                op("dve", lambda e: e.tensor_tensor(out=utail[:, c, 0:64], in0=ps[:, b, 0:64], in1=sg[:, c, 0:64], op=ALU.mult),
                   reads=[PB[b], res("sg")], writes=[res("utail")])
            else:
                op("dve", lambda e: e.tensor_tensor(out=uext[:, c, 30:542], in0=ps[:, b, 0:512], in1=sg[:, c, 0:512], op=ALU.mult),
                   reads=[PB[b], res("sg")], writes=[res("uext")])
                if i == 3:
                    op("dve", lambda e: e.tensor_tensor(out=utail[:, c, 0:32], in0=ps[:, b, 480:512], in1=sg[:, c, 480:512], op=ALU.mult),
                       reads=[PB[b], res("sg")], writes=[res("utail")])
        proj_fm(wt, wr, 0, 4, N, ev_a)

        def qkv_gen():
            wt, wr = wnext()
            yield from proj_fm_g(wt, wr, 0, 4, N, lambda c, b: evac_copy("act", qT0[:, c, 0:N], ps[:, b, 0:N], [PB[b]], [res("qT0")], scale=0.125))
            wt, wr = wnext()
            if sample:
                yield from proj_fm_g(wt, wr, 0, 4, N, lambda c, b: evac_copy("dve", kTs[:, c, 0:N], ps[:, b, 0:N], [PB[b]], [res("kT0")]))
            else:
                yield from proj_fm_g(wt, wr, 0, 4, N, lambda c, b: evac_copy("dve", kT0[:, c, t0:t0 + N], ps[:, b, 0:N], [PB[b]], [res("kT0")]))

            def ev_k(ti, lo, Pk, b):
                evac_copy("act", kst[0:Pk, :], ps[0:Pk, b, :], [PB[b]], [res("kst")])
                dst = o_sbk_s[lo:lo + Pk, :] if sample else o_sbk_p[s, t0 + lo:t0 + lo + Pk, :]
                dma(dst, kst[0:Pk, :], reads=[res("kst")], grp=grp("kst"))
            yield from proj_tm_g(wt, wr, kvt, ev_k)
            wt, wr = wnext()

            def ev_v(ti, lo, Pk, b):
                evac_copy("act", vst[0:Pk, :], ps[0:Pk, b, :], [PB[b]], [res("vst")])
                if sample:
                    pass
                else:
                    evac_copy("dve", v0[:, 4 * i + ti, :], ps[:, b, :], [PB[b]], [res("v0")])
                dst = o_sbv_s[lo:lo + Pk, :] if sample else o_sbv_p[s, t0 + lo:t0 + lo + Pk, :]
                dma(dst, vst[0:Pk, :], reads=[res("vst")], grp=grp("vst"))
                if sample:
                    evac_copy("dve", vnew0[0:16, ti, :], ps[0:16, b, :], [PB[b]], [res("vnew")])
            yield from proj_tm_g(wt, wr, kvt, ev_v)

        st(2)
        if sample:
            for q in range(4):
                dma(stg[0:30, 0:512], cconv[q], writes=[res("stg")], grp=grp("stg"))
                for c in range(4):
                    op("pe", lambda e, c=c: e.transpose(out=ps[:, 3, c * 32:c * 32 + 30], in_=stg[0:30, c * 128:(c + 1) * 128], identity=identf[0:30, 0:30]),
                       reads=[res("stg"), res("identf")], writes=[PB[3]])
                op("dve", lambda e, q=q: e.tensor_copy(out=uexts[:, q, :, 0:30], in_=ps[:, 3, 0:128].rearrange("p (c j) -> p c j", c=4)[:, :, 0:30]),
                   reads=[PB[3]], writes=[res("uext")])
                for c in range(4):
                    op("pe", lambda e, c=c, q=q: e.transpose(out=ps[0:16, 2, c * 128:(c + 1) * 128], in_=utail[:, c, 16 * q:16 * q + 16], identity=identf[:, :]),
                       reads=[res("utail"), res("identf")], writes=[PB[2]])
                op("act", lambda e: e.copy(out=cst[0:16, :], in_=ps[0:16, 2, :]), reads=[PB[2]], writes=[res("ysb")])
                dma(o_conv_s[q, 14:30, :], cst[0:16, :], reads=[res("ysb")], grp=grp("cst"))
                dma(o_conv_s[q, 0:14, :], cconv[q, 16:30, :], grp=grp("cst2"))
        elif i == 3:
            for c in range(4):
                op("pe", lambda e, c=c: e.transpose(out=ps[0:32, 2, c * 128:(c + 1) * 128], in_=utail[:, c, 0:32], identity=identf[:, :]),
                   reads=[res("utail"), res("identf")], writes=[PB[2]])
            op("act", lambda e: e.copy(out=cst[0:32, :], in_=ps[0:32, 2, :]), reads=[PB[2]], writes=[res("ysb")])
            dma(o_conv_p[s, :, :], cst[2:32, :], reads=[res("ysb")], grp=grp("cst"))

        if sample:
            ctiles = [((lambda c, ii, Pq, q=q: uexts[:, q, c, ii:ii + 16]), 16, 16 * q) for q in range(4)]
        else:
            ctiles = [((lambda c, ii, Pq, t=t: uext[:, c, t * 128 + ii:t * 128 + ii + 128]), 128, 128 * t) for t in range(4)]
        BANKSEL[0] = (4, 5, 6, 7)
        run_pool([lambda slot: conv_gen(ctiles), lambda slot: qkv_gen()], 2)
        BANKSEL[0] = None
        if not sample:
            op("pool", lambda e: e.tensor_copy(out=uext[:, :, 0:30], in_=uext[:, :, 512:542]), reads=[], writes=[res("uext")])
        for ti, (ufn, Pc, lo) in enumerate(ctiles):
            conv_epilogue(ti, Pc, lo)

        st(3)
        HI.clear()
        if sample:
            CS = [(kT0, v0, [res("kT0")], [res("v0")]),
                  (kT0b, v0b, [res("kT1_0"), res("kT1_1")], [res("v1_0"), res("v1_1"), res("v1s")])]
            KSTG = [(stg[:, 0:512], res("stg"), grp("stg")), (kst[:, :], res("kst"), grp("kst"))]
            VSTG = [(stg[:, 512:1024], res("stgv"), grp("stgv")), (vst[:, :], res("vst"), grp("vst"))]

            def sb_load(q):
                kc_, vc_, kr_, vr_ = CS[q % 2]
                for kt in range(16):
                    kb, krs, kg = KSTG[kt % 2]
                    vb, vrs, vg = VSTG[kt % 2]
                    tbk = 2 + kt % 2
                    dma(kb, csk[q, kt * 128:(kt + 1) * 128, :], writes=[krs], grp=kg)
                    for c in range(4):
                        op("pe", lambda e, c=c, kb=kb, tbk=tbk: e.transpose(out=ps[:, tbk, c * 128:(c + 1) * 128], in_=kb[:, c * 128:(c + 1) * 128], identity=identf[:, :]),
                           reads=[krs, res("identf")], writes=[PB[tbk]])
                    evac_copy("act" if kt % 2 == 0 else "dve", kc_[:, :, kt * 128:(kt + 1) * 128],
                              ps[:, tbk, :].rearrange("p (c j) -> p c j", c=4), [PB[tbk]], kr_)
                    dma(vb, csv[q, kt * 128:(kt + 1) * 128, :], writes=[vrs], grp=vg)
                    op("pool", lambda e, kt=kt, vb=vb, vc_=vc_: e.tensor_copy(out=vc_[:, kt, :], in_=vb), reads=[vrs], writes=vr_)
                op("pool", lambda e, vc_=vc_: e.tensor_copy(out=vc_[0:16, 16, :], in_=vnew0[0:16, q, :]), reads=[res("vnew")], writes=vr_)

            def sb_run(q):
                kc_, vc_, kr_, vr_ = CS[q % 2]
                facts_s = []
                for par_ in range(2):
                    steps = []
                    for kt in range(16, -1, -1):
                        KK = 16 if kt == 16 else 128
                        zp, vp = [], []
                        for j_ in range(4):
                            h = 2 * j_ + par_
                            c, po = h // 2, 64 * (h % 2)
                            if kt == 16:
                                l = kTs[po:po + 64, c, 16 * q:16 * q + 16]
                            else:
                                l = kc_[po:po + 64, c, kt * 128:(kt + 1) * 128]
                            zp.append((l, qT0[po:po + 64, c, 16 * q:16 * q + 16], 0, j_ * 16, j_ * 16 + 16))
                            vp.append((vc_[0:KK, kt, h * 64:(h + 1) * 64], j_ * 16, j_ * 16 + 16))
                        mask = (maskS[:, 0:64], 0, 64) if kt == 16 else None
                        steps.append((KK, zp, 0, 64, mask, vp, kr_ + [res("kT0")], vr_))

                    def evac_s(Ob, opo, q=q, par_=par_):
                        op("act", lambda e: e.copy(
                            out=aoT[0:64, :, 16 * q:16 * q + 16].rearrange("p (j two) t -> p two j t", two=2)[:, par_, :, :],
                            in_=ps[0:64, Ob, 0:64].rearrange("p (j t) -> p j t", j=4)),
                           reads=[PB[Ob]], writes=[res("aoT")])
                    facts_s.append(lambda slot, steps=steps, evac_s=evac_s: sb_head(slot, steps, evac_s))
                run_pool(facts_s, 2, stagger=STG_SBS)
            sb_load(0)
            for q in range(4):
                if q + 1 < 4:
                    sb_load(q + 1)
                sb_run(q)
        else:
            facts = []
            for h in range(8):
                c, po = h // 2, 64 * (h % 2)
                steps = []
                for kt in range(4 * i + 3, -1, -1):
                    o_ = kt - 4 * i
                    c0 = 128 * o_ if o_ >= 0 else 0
                    mask = (maskU[:, :], c0, c0 + 128) if o_ >= 0 else None
                    zp = [(kT0[po:po + 64, c, kt * 128:(kt + 1) * 128], qT0[po:po + 64, c, c0:512], 0, c0, 512)]
                    vp = [(v0[:, kt, h * 64:(h + 1) * 64], c0, 512)]
                    steps.append((128, zp, c0, 512, mask, vp, [res("kT0")], [res("v0")]))

                def evac_p(Ob, opo, h=h):
                    HI[h] = opo
                    op("act", lambda e: e.copy(out=aoT[opo:opo + 64, h, 0:512], in_=ps[opo:opo + 64, Ob, :]), reads=[PB[Ob]], writes=[res("aoT")])
                facts.append(lambda slot, steps=steps, evac_p=evac_p: sb_head(slot, steps, evac_p, three=(4 if SB_SLOTS == 4 else True)))
            def sb_filler(rnd):
                for _ in range(SB_DUMMY):
                    op("pe", lambda e: e.matmul(ps[64:128, 7, 0:SB_DUMMY_N], lhsT=zerosb[:, 0:64], rhs=zerosb[:, 0:SB_DUMMY_N],
                                                start=True, stop=True, skip_group_check=True),
                       reads=[res("zerosb")], writes=[PB[7]])
            if SB_SLOTS == 4:
                run_pool(facts, 4, stagger=STG_SB4)
            else:
                run_pool(facts, 3, stagger=STG_SB, filler=sb_filler if SB_DUMMY else None)

        st(4)
        def parts_ab(o):
            wa, ra = wnext()
            wc, rc = wnext()
            pr_lo, pr_hi, pr_c = [], [], []
            for h in range(8):
                po_ = HI.get(h, 0)
                ent = ((lambda t, h=h, po_=po_: aoT[po_:po_ + 64, h, t * P:(t + 1) * P]), wa[po_:po_ + 64, h, :], [ra, res("aoT")])
                (pr_hi if po_ else pr_lo).append(ent)
            for c in range(4):
                pr_c.append(((lambda t, c=c: cT[:, c, t * P:(t + 1) * P]), wc[:, c, :], [rc, res("cT")]))
            return pr_lo + pr_c + pr_hi
        out_proj_resid(P, NT, parts_ab)
        st(45)
        mlp(0, P, NT)
        st(5)

        if not FIRST_L1_DONE[0]:
            FIRST_L1_DONE[0] = True
            bg_flush()
            S.barrier()
            setup_bias()
        rmsnorm(P, NT, 2)
        own = i % 2 if not sample else 1
        prev = 1 - own
        need_kv = sample or i == 3
        for half in range(2):
            wt, wr = wnext()
            proj_fm(wt, wr, 0, 4, N, lambda c, b, half=half: evac_copy("act", qT1[:, half * 4 + c, 0:N], ps[:, b, 0:N], [PB[b]], [res("qT1")], scale=0.125))
        for half in range(2):
            wt, wr = wnext()
            if sample:
                proj_fm(wt, wr, 0, 4, N, lambda c, b, half=half: evac_copy("dve", kT1s[:, half * 4 + c, 0:N], ps[:, b, 0:N], [PB[b]], [res("kT1s")]))
            else:
                proj_fm(wt, wr, 0, 4, N, lambda c, b, half=half: evac_copy("dve", kT1[own][:, half * 4 + c, 0:N], ps[:, b, 0:N], [PB[b]], [res(f"kT1_{own}")]))
            if need_kv:
                def ev_k1(ti, lo, Pk, b, half=half):
                    evac_copy("act", kst1[0:Pk, 0:512], ps[0:Pk, b, :], [PB[b]], [res("kst1"), res("hidT"), res("sbLb0"), res("sbWb0")])
                    dst = o_bk_s[lo:lo + Pk, half * 512:(half + 1) * 512] if sample else o_bk_p[s, lo:lo + Pk, half * 512:(half + 1) * 512]
                    dma(dst, kst1[0:Pk, 0:512], reads=[res("kst1")], grp=grp("kst1"))
                proj_tm(wt, wr, kvt, ev_k1)
        for half in range(2):
            wt, wr = wnext()

            def ev_v1(ti, lo, Pk, b, half=half):
                if sample:
                    evac_copy("dve", vnew1[0:16, ti, half * 512:(half + 1) * 512], ps[0:16, b, :], [PB[b]], [res("vnew")])
                else:
                    evac_copy("dve", v1[own][:, ti, half * 512:(half + 1) * 512], ps[:, b, :], [PB[b]], [res(f"v1_{own}")])
                if need_kv:
                    evac_copy("act", vst1[0:Pk, 0:512], ps[0:Pk, b, :], [PB[b]], [res("vst1"), res("kst"), res("vst")])
                    dst = o_bv_s[lo:lo + Pk, half * 512:(half + 1) * 512] if sample else o_bv_p[s, lo:lo + Pk, half * 512:(half + 1) * 512]
                    dma(dst, vst1[0:Pk, 0:512], reads=[res("vst1")], grp=grp("vst1"))
            proj_tm(wt, wr, kvt, ev_v1)

        st(6)
        if sample:
            def band_load(q):
                kb_, vb_ = kT1[q % 2], v1[q % 2]
                rk, rv = res(f"kT1_{q % 2}"), res(f"v1_{q % 2}")
                for kt in range(4):
                    dma(stg[:, :], cbk[q, kt * 128:(kt + 1) * 128, :], writes=[res("stg"), res("stgv")], grp=grp("stg"))
                    for hh in range(2):
                        for c in range(4):
                            op("pe", lambda e, c=c, hh=hh: e.transpose(out=ps[:, 3, c * 128:(c + 1) * 128], in_=stg[:, (hh * 4 + c) * 128:(hh * 4 + c + 1) * 128], identity=identf[:, :]),
                               reads=[res("stg"), res("stgv"), res("identf")], writes=[PB[3]])
                        evac_copy("act" if hh == 0 else "dve", kb_[:, hh * 4:hh * 4 + 4, kt * 128:(kt + 1) * 128],
                                  ps[:, 3, :].rearrange("p (c j) -> p c j", c=4), [PB[3]], [rk])
                    dma(kst1[:, :], cbv[q, kt * 128:(kt + 1) * 128, :], writes=[res("kst1"), res("hidT"), res("sbLb0"), res("sbWb0")], grp=grp("kst1"))
                    op("pool", lambda e, kt=kt: e.tensor_copy(out=vb_[:, kt, :], in_=kst1[:, :]), reads=[res("kst1")], writes=[rv])

            def band_run(q):
                kb_, vb_ = kT1[q % 2], v1[q % 2]
                rk, rv = res(f"kT1_{q % 2}"), res(f"v1_{q % 2}")
                op("pool", lambda e: e.tensor_copy(out=v1s[0:16, :], in_=vnew1[0:16, q, :]), reads=[res("vnew")], writes=[res("v1s")])
                segs = [((lambda c, po: kb_[po:po + 64, c, 0:512]), 512, 0, rk),
                        ((lambda c, po: kT1s[po:po + 64, c, 16 * q:16 * q + 16]), 16, 512, res("kT1s"))]
                groups = [(128 * g, 128, (lambda h, g=g: vb_[:, g, h * 64:(h + 1) * 64]), rv) for g in range(4)]
                groups.append((512, 16, (lambda h: v1s[0:16, h * 64:(h + 1) * 64]), res("v1s")))
                band_qtile(16, 16 * q, segs, groups, 0, 528, None, q % 2)
            band_load(0)
            for q in range(4):
                if q + 1 < 4:
                    band_load(q + 1)
                band_run(q)
        else:
            for r in range(4):
                segs = []
                if i > 0:
                    segs.append(((lambda c, po, r=r: kT1[prev][po:po + 64, c, 128 * r:512]), 512 - 128 * r, 0, res(f"kT1_{prev}")))
                segs.append(((lambda c, po, r=r: kT1[own][po:po + 64, c, 0:128 * (r + 1)]), 128 * (r + 1), 512 - 128 * r, res(f"kT1_{own}")))
                kmin = 0 if i > 0 else 512 - 128 * r
                groups = []
                for g in range(5):
                    if 128 * g < kmin:
                        continue
                    tt = r + g
                    if tt < 4:
                        groups.append((128 * g, 128, (lambda h, tt=tt: v1[prev][:, tt, h * 64:(h + 1) * 64]), res(f"v1_{prev}")))
                    else:
                        groups.append((128 * g, 128, (lambda h, tt=tt: v1[own][:, tt - 4, h * 64:(h + 1) * 64]), res(f"v1_{own}")))
                band_qtile(128, 128 * r, segs, groups, kmin, 640, None, r % 2)

        st(7)
        def parts_c(o):
            wo, ro = wnext()
            return [((lambda t, kc=kc: oT[:, kc, t * P:(t + 1) * P]), wo[:, kc, :], [ro, res("oT")]) for kc in range(8)]
        out_proj_resid(P, NT, parts_c)
        st(75)
        mlp(1, P, NT)
        DUMP[0] = None

        rmsnorm(P, NT, None, final=True)
        st(99)
        if sample:
            dma(ys, xtok[0:64, 0, :], reads=[res("xtok")], grp=grp("ystore"))
        else:
            dma(yp[s, t0:t0 + 512, :].rearrange("(t p) f -> p t f", p=128), xtok[:, :, :], reads=[res("xtok")], grp=grp("ystore"))

    import os
    try:
        if not os.environ.get("NO_CONSTS"):
            setup_consts()
        BG["gen"] = convert_gen()
        bg_advance(9)
        stop(1)
        for s in range(2):
            if os.environ.get("NO_PROMPT"):
                bg_flush()
                S.barrier()
                break
            for i in range(4):
                run_block(False, s, i)
                stop(8 + 4 * s + i)
        run_block(True)
    except StopBuild:
        pass
    S.emit()
    return nc, S


_CACHE = {}


def _get_program():
    if "nc" not in _CACHE:
        _CACHE["nc"] = build_program()[0]
    return _CACHE["nc"]


def kernel(x_prompt, x_sample, cache_sb_k, cache_sb_v, cache_conv, cache_band_k, cache_band_v,
           norm_mix, norm_ffn, norm_final, w_in_ab, w_out_ab, dw_w, dw_b, conv_ln_g, conv_ln_b,
           w_in_c, w_out_c, rel_bias, w_up, w_down):
    f = lambda a: np.ascontiguousarray(np.asarray(a, dtype=np.float32))
    x_prompt, x_sample = f(x_prompt), f(x_sample)
    q = np.arange(128)[:, None]
    k = np.arange(640)[None, :]
    idx = np.clip(q + 512 - k, -128, 128) + 128
    bfull = f(np.asarray(rel_bias, dtype=np.float32)[0][:, idx])
    shared = {
        "norm_mix": f(norm_mix), "norm_ffn": f(norm_ffn), "norm_final": f(norm_final),
        "w_in_ab": f(w_in_ab)[0], "w_out_ab": f(w_out_ab)[0], "dw_w": f(dw_w)[0], "dw_b": f(dw_b)[0],
        "ln_g": f(conv_ln_g)[0], "ln_b": f(conv_ln_b)[0], "w_in_c": f(w_in_c)[0], "w_out_c": f(w_out_c)[0],
        "bfull": bfull, "w_up": f(w_up), "w_down": f(w_down),
    }
    csk, csv = f(cache_sb_k)[0], f(cache_sb_v)[0]
    cconv, cbk, cbv = f(cache_conv)[0], f(cache_band_k)[0], f(cache_band_v)[0]
    in_maps = []
    for c in range(NCORES):
        m = dict(shared)
        m["xp"] = x_prompt[2 * c:2 * c + 2]
        m["xs"] = x_sample[4 * c:4 * c + 4].reshape(64, 1024)
        m["csk"] = csk[4 * c:4 * c + 4].reshape(4, 2048, 512)
        m["csv"] = csv[4 * c:4 * c + 4].reshape(4, 2048, 512)
        m["cconv"] = cconv[4 * c:4 * c + 4]
        m["cbk"] = cbk[4 * c:4 * c + 4].reshape(4, 512, 1024)
        m["cbv"] = cbv[4 * c:4 * c + 4].reshape(4, 512, 1024)
        in_maps.append({k_: np.ascontiguousarray(v) for k_, v in m.items()})
    nc = _get_program()
    res = run_bass_kernel_spmd(nc, in_maps, core_ids=list(range(NCORES)))
    R = res.results
    cat = lambda key: np.concatenate([np.asarray(r[key], dtype=np.float32) for r in R], axis=0)
    y_prompt = cat("yp")
    y_sample = cat("ys").reshape(32, 16, 1024)
    sbk_p = cat("o_sbk_p").reshape(1, 16, 2048, 8, 64)
    sbv_p = cat("o_sbv_p").reshape(1, 16, 2048, 8, 64)
    conv_p = cat("o_conv_p").reshape(1, 16, 30, 512)
    bk_p = cat("o_bk_p").reshape(1, 16, 512, 16, 64)
    bv_p = cat("o_bv_p").reshape(1, 16, 512, 16, 64)
    sbk_s = cat("o_sbk_s").reshape(1, 32, 16, 8, 64)
    sbv_s = cat("o_sbv_s").reshape(1, 32, 16, 8, 64)
    conv_s = cat("o_conv_s").reshape(1, 32, 30, 512)
    bk_s = cat("o_bk_s").reshape(1, 32, 16, 16, 64)
    bv_s = cat("o_bv_s").reshape(1, 32, 16, 16, 64)
    return (y_prompt, y_sample, sbk_p, sbv_p, conv_p, bk_p, bv_p, sbk_s, sbv_s, conv_s, bk_s, bv_s)
```

```python
import numpy as np
import concourse.bass as bass
import concourse.mybir as mybir
from concourse.bass_utils import run_bass_kernel_spmd

F32 = mybir.dt.float32
BF16 = mybir.dt.bfloat16
AF = mybir.ActivationFunctionType
ALU = mybir.AluOpType
AX = mybir.AxisListType

EPOCH = 3000
NCORES = 8
RMS_EPS = 1e-6
LN_EPS = 1e-5
NEG = -30000.0


class Res:
    __slots__ = ("name", "last_w", "readers", "excl")

    def __init__(self, name, excl=False):
        self.name = name
        self.excl = excl
        self.last_w = None
        self.readers = []


class DmaGroup:
    def __init__(self, sched, name):
        self.sem = sched.new_sem("dg_" + name)
        self.count = 0


class Sched:
    ENGS = ("pe", "act", "dve", "pool", "sp")

    def __init__(self, nc):
        self.nc = nc
        self.ops = {e: [] for e in self.ENGS}
        self.nops = {e: 0 for e in self.ENGS}
        self.esems = {e: [] for e in self.ENGS}
        self.seen = {e: {} for e in self.ENGS}
        self._semctx = []
        self.groups = []

    def new_sem(self, name):
        ctx = self.nc.semaphore(name)
        s = ctx.__enter__()
        self._semctx.append(ctx)
        return s

    def group(self, name):
        g = DmaGroup(self, name)
        self.groups.append(g)
        return g

    def _eng_ticket(self, eng):
        n = self.nops[eng]
        ep, v = divmod(n, EPOCH)
        while len(self.esems[eng]) <= ep:
            self.esems[eng].append(self.new_sem(f"e_{eng}_{len(self.esems[eng])}"))
        self.nops[eng] = n + 1
        return (self.esems[eng][ep], v + 1, eng)

    def op(self, eng, fn, reads=(), writes=(), dma=None):
        deps = []
        reads = [r for r in reads if r is not None]
        writes = [w for w in writes if w is not None]
        ex = [r for r in reads if r.excl]
        if ex:
            reads = [r for r in reads if not r.excl]
            writes = list(writes) + [r for r in ex if r not in writes]
        for r in reads:
            if r.last_w is not None:
                deps.append(r.last_w)
        for w in writes:
            if w.last_w is not None:
                deps.append(w.last_w)
            deps.extend(w.readers)
        seen = self.seen[eng]
        best = {}
        for (sem, val, src) in deps:
            if src == eng and eng == "pe":
                continue
            k = id(sem)
            if seen.get(k, 0) >= val:
                continue
            if k not in best or best[k][1] < val:
                best[k] = (sem, val)
        for k, (sem, val) in best.items():
            seen[k] = val
        waits = list(best.values())
        if dma is not None:
            dma.count += 16
            ticket = (dma.sem, dma.count, "dma")
            inc = (dma.sem, 16)
        else:
            ticket = self._eng_ticket(eng)
            inc = (ticket[0], 1)
        self.ops[eng].append((waits, fn, inc))
        for r in reads:
            r.readers.append(ticket)
        for w in writes:
            w.last_w = ticket
            w.readers = []
        return ticket

    def barrier(self):
        waits = []
        for e in self.ENGS:
            n = self.nops[e]
            if n > 0:
                ep, v = divmod(n - 1, EPOCH)
                waits.append((self.esems[e][ep], v + 1))
        for g in self.groups:
            if g.count > 0:
                waits.append((g.sem, g.count))
        for e in self.ENGS:
            t = self._eng_ticket(e)
            self.ops[e].append((list(waits), lambda eng: eng.nop(), (t[0], 1)))
            for sem, val in waits:
                k = id(sem)
                if self.seen[e].get(k, 0) < val:
                    self.seen[e][k] = val

    def emit(self):
        nc = self.nc
        fin = [(g.sem, g.count) for g in self.groups if g.count > 0]
        self.ops["sp"].append((fin, lambda e: e.nop(), None))
        engmap = {"pe": "tensor", "act": "scalar", "dve": "vector", "pool": "gpsimd", "sp": "sync"}
        with nc.Block() as block:
            for e in self.ENGS:
                ops = self.ops[e]

                def body(eng, ops=ops):
                    for waits, fn, inc in ops:
                        for sem, val in waits:
                            eng.wait_ge(sem, val)
                        ins = fn(eng)
                        if inc is not None:
                            ins.then_inc(inc[0], inc[1])
                getattr(block, engmap[e])(body)


N_WT = 49
import os as _os
SB_SLOTS = int(_os.environ.get('SB_SLOTS', '4'))
STG_SB4 = int(_os.environ.get('STG_SB4', '2'))
SB4_DUMMY = int(_os.environ.get('SB4_DUMMY', '1'))
SB_DUMMY = int(_os.environ.get('SB_DUMMY', '1'))
CONV_DUMMY = int(_os.environ.get('CONV_DUMMY', '0'))
BAND_DUMMY = int(_os.environ.get('BAND_DUMMY', '0'))
BAND_DUMMY_N = int(_os.environ.get('BAND_DUMMY_N', '512'))
SB_DUMMY_N = int(_os.environ.get('SB_DUMMY_N', '512'))
STG_SB = int(_os.environ.get('STG_SB', '1'))
STG_BAND = int(_os.environ.get('STG_BAND', '0'))
STG_SBS = int(_os.environ.get('STG_SBS', '0'))
DEBUG_STOP = None


class StopBuild(Exception):
    pass


DUMP = [None]
MARKS = []


def stop(k):
    if DEBUG_STOP == k:
        if DUMP[0] is not None:
            DUMP[0]()
        raise StopBuild()


def build_program():
    nc = bass.Bass("TRN2", target_bir_lowering=False, dynamic_dma_scratch_size=512)
    S = Sched(nc)

    def din(name, shape, dt=F32):
        return nc.dram_tensor(name, list(shape), dt, kind="ExternalInput").ap()

    def dout(name, shape, dt=F32):
        return nc.dram_tensor(name, list(shape), dt, kind="ExternalOutput").ap()

    xp = din("xp", [2, 2048, 1024])
    xs = din("xs", [64, 1024])
    csk = din("csk", [4, 2048, 512])
    csv = din("csv", [4, 2048, 512])
    cconv = din("cconv", [4, 30, 512])
    cbk = din("cbk", [4, 512, 1024])
    cbv = din("cbv", [4, 512, 1024])
    norm_mix = din("norm_mix", [2, 1024])
    norm_ffn = din("norm_ffn", [2, 1024])
    norm_final = din("norm_final", [1024])
    w_in_ab = din("w_in_ab", [1024, 2560])
    w_out_ab = din("w_out_ab", [1024, 1024])
    dw_w = din("dw_w", [31, 512])
    dw_b = din("dw_b", [512])
    ln_g = din("ln_g", [512])
    ln_b = din("ln_b", [512])
    w_in_c = din("w_in_c", [1024, 3072])
    w_out_c = din("w_out_c", [1024, 1024])
    bfull = din("bfull", [16, 128, 640])
    w_up = din("w_up", [2, 1024, 4096])
    w_down = din("w_down", [2, 4096, 1024])

    yp = dout("yp", [2, 2048, 1024])
    ys = dout("ys", [64, 1024])
    o_sbk_p = dout("o_sbk_p", [2, 2048, 512])
    o_sbv_p = dout("o_sbv_p", [2, 2048, 512])
    o_conv_p = dout("o_conv_p", [2, 30, 512])
    o_bk_p = dout("o_bk_p", [2, 512, 1024])
    o_bv_p = dout("o_bv_p", [2, 512, 1024])
    o_sbk_s = dout("o_sbk_s", [64, 512])
    o_sbv_s = dout("o_sbv_s", [64, 512])
    o_conv_s = dout("o_conv_s", [4, 30, 512])
    o_bk_s = dout("o_bk_s", [64, 1024])
    o_bv_s = dout("o_bv_s", [64, 1024])

    wsc = nc.dram_tensor("wsc", [N_WT, 128, 4096], BF16, kind="Internal").ap()

    cur = [1024]

    def alloc(name, shape, dt, at=None):
        esz = 4 if dt == F32 else 2
        n = 1
        for d in shape[1:]:
            n *= d
        nbytes = (n * esz + 31) // 32 * 32
        if at is None:
            at = cur[0]
            cur[0] += nbytes
        assert at + nbytes <= 229376, (name, at, nbytes)
        return nc.alloc_sbuf_tensor_at(name, list(shape), dt, offset=at)

    wring = [alloc(f"wring{i}", [128, 4096], BF16) for i in range(3)]
    xtok = alloc("xtok", [128, 4, 1024], F32)
    kT0 = alloc("kT0", [128, 4, 2048], BF16)
    v0 = alloc("v0", [128, 17, 512], BF16)
    KT1_BASE = cur[0]
    kT1 = [alloc(f"kT1_{i}", [128, 8, 512], BF16) for i in range(2)]
    v1 = [alloc(f"v1_{i}", [128, 4, 1024], BF16) for i in range(2)]
    v1s = alloc("v1s", [16, 1024], BF16)
    assert cur[0] >= KT1_BASE + 32768
    BB_BASE = cur[0]
    Bb = alloc("Bb", [128, 16, 640], BF16)
    uext = alloc("uext", [128, 4, 542], BF16)
    Dring = [alloc(f"D{i}", [128, 31, 128], BF16) for i in range(2)]
    identb = alloc("identb", [128, 128], BF16)
    identf = alloc("identf", [128, 128], F32)
    Tst = alloc("Tst", [128, 128], BF16)
    onesb = alloc("onesb", [128, 128], BF16)
    Umat = alloc("Umat", [128, 128], BF16)
    zerosb = alloc("zerosb", [128, 512], BF16)
    maskU = alloc("maskU", [128, 128], BF16)
    maskS = alloc("maskS", [16, 128], BF16)
    gT = alloc("gT", [128, 4, 8], F32)
    gfin = alloc("gfin", [128, 1024], F32)
    dwb_bc = alloc("dwb_bc", [128, 512], F32)
    lng_bc = alloc("lng_bc", [128, 512], F32)
    lnb_bc = alloc("lnb_bc", [128, 512], F32)
    dwT = alloc("dwT", [128, 4, 31], F32)
    stat = alloc("stat", [128, 64], F32)
    rs16 = alloc("rs16", [128, 32], F32)
    ARENA = cur[0]
    ARENA_SZ = 229376 - ARENA

    def ar(name, shape, dt, off):
        return alloc(name, shape, dt, at=ARENA + off)

    hT = ar("hT", [128, 8, 512], BF16, 0)
    qT0 = ar("qT0", [128, 4, 512], BF16, 8192)
    sg = ar("sg", [128, 4, 512], F32, 12288)
    aoT = ar("aoT", [128, 8, 512], BF16, 20480)
    cT = ar("cT", [128, 4, 512], BF16, 28672)
    SBT = [(ar("sbEA0", [128, 512], F32, 32768), ar("sbL20", [128, 512], F32, 34816),
            ar("sbLb0", [128, 512], BF16, 36864), ar("sbWb0", [128, 512], BF16, 37888)),
           (ar("sbEA1", [128, 512], F32, 12288), ar("sbL21", [128, 512], F32, 14336),
            ar("sbLb1", [128, 512], BF16, 16384), ar("sbWb1", [128, 512], BF16, 17408)),
           (ar("sbEA2", [128, 512], F32, 0), ar("sbL22", [128, 512], F32, 2048),
            ar("sbLb2", [128, 512], BF16, 4096), ar("sbWb2", [128, 512], BF16, 5120))]
    SBT.append((ar("sbEA3", [128, 512], F32, 40960), ar("sbL23", [128, 512], F32, 43008),
                ar("sbLb3", [128, 512], BF16, 45056), ar("sbWb3", [128, 512], BF16, 46080)))
    Eb = ar("Eb", [128, 512], F32, 32768)
    junk = ar("junk", [128, 1024], BF16, 32768)
    L2b = ar("L2b", [128, 512], F32, 34816)
    Ab = ar("Ab", [128, 512], F32, 36864)
    Lbb = ar("Lbb", [128, 512], BF16, 38912)
    Wbb = ar("Wbb", [128, 512], BF16, 39936)
    kst = ar("kst", [128, 512], F32, 40960)
    vst = ar("vst", [128, 512], F32, 43008)
    stg = ar("stg", [128, 1024], F32, 45056)
    ysb = ar("ysb", [128, 512], F32, 49152)
    cst = ar("cst", [32, 512], F32, 49152)
    ctm = ar("ctm", [128, 512], BF16, 51200)
    ysb2 = ar("ysb2", [128, 512], F32, 45056)
    ctm2 = ar("ctm2", [128, 512], BF16, 45056 + 2048)
    hn = ar("hn", [128, 1024], BF16, 52224)
    hn2 = ar("hn2", [128, 1024], BF16, 61440)
    utail = ar("utail", [128, 4, 64], F32, 54272)
    uexts = ar("uexts", [128, 4, 4, 46], BF16, 55296)
    kTs = ar("kTs", [128, 4, 64], BF16, 56832)
    L0_END = 57344
    hidT = ar("hidT", [128, 32, 512], BF16, 8192)
    rl = [ar(f"rl{i}", [128, 512], F32, 40960 + 2048 * i) for i in range(2)]
    qT1 = ar("qT1", [128, 8, 512], BF16, 8192)
    oT = ar("oT", [128, 8, 512], BF16, 16384)
    Sp = [ar(f"Sp{i}", [128, 640], F32, 24576 + 2560 * i) for i in range(2)]
    Pe = [ar(f"Pe{i}", [128, 640], BF16, 29696 + 1280 * i) for i in range(2)]
    PT = [ar(f"PT{i}", [128, 5, 128], BF16, 32256 + 1280 * i) for i in range(2)]
    osb = ar("osb", [128, 1024], BF16, 34816)
    kst1 = ar("kst1", [128, 1024], F32, 36864)
    vst1 = ar("vst1", [128, 1024], F32, 40960)
    kT1s = ar("kT1s", [128, 8, 64], BF16, 54272)
    assert 63488 <= ARENA_SZ, (ARENA, ARENA_SZ)
    kT0b = alloc("kT0b", [128, 4, 2048], BF16, at=KT1_BASE)
    v0b = alloc("v0b", [128, 17, 512], BF16, at=KT1_BASE + 16384)
    cvf = [alloc("cvf0", [128, 4096], F32, at=KT1_BASE), alloc("cvf1", [128, 4096], F32, at=KT1_BASE + 16384)]
    cvb = [alloc("cvb0", [128, 4096], BF16, at=BB_BASE), alloc("cvb1", [128, 4096], BF16, at=BB_BASE + 8192)]
    vnew0 = ar("vnew0", [16, 4, 512], BF16, 57344)
    vnew1 = ar("vnew1", [16, 4, 1024], BF16, 55296)

    ps = nc.alloc_psum_tensor("ps", [128, 8, 512], F32)
    PB = [Res(f"psb{i}", excl=True) for i in range(8)]

    def psf(b):
        return ps[:, b, :]

    def psb16(b):
        return ps[:, b, :].bitcast(BF16)

    R = {}

    def res(name):
        if name not in R:
            R[name] = Res(name)
        return R[name]


    dcount = [0]

    def dma(out, in_, reads=(), writes=(), grp=None, slow=False):
        if grp is None:
            grp = S.group(f"d{dcount[0]}")
            dcount[0] += 1
        if slow:
            f = lambda e: e.dma_start(out=out, in_=in_, allow_slow_non_contiguous=True)
        else:
            f = lambda e: e.dma_start(out=out, in_=in_)
        return op("sp", f, reads=reads, writes=writes, dma=grp)

    G = {}

    def grp(name):
        if name not in G:
            G[name] = S.group(name)
        return G[name]

    def setup_consts():
        op("pool", lambda e: e.memset(identb[:], 1.0), writes=[res("identb")])
        op("pool", lambda e: e.affine_select(out=identb[:], in_=identb[:], pattern=[[-1, 128]], compare_op=ALU.is_equal,
                                             fill=0.0, base=0, channel_multiplier=1), writes=[res("identb")])
        op("pool", lambda e: e.memset(identf[:], 1.0), writes=[res("identf")])
        op("pool", lambda e: e.affine_select(out=identf[:], in_=identf[:], pattern=[[-1, 128]], compare_op=ALU.is_equal,
                                             fill=0.0, base=0, channel_multiplier=1), writes=[res("identf")])
        op("pool", lambda e: e.memset(Tst[:], 1.0), writes=[res("Tst")])
        op("pool", lambda e: e.affine_select(out=Tst[:], in_=Tst[:], pattern=[[-1, 128]], compare_op=ALU.is_gt,
                                             fill=0.0, base=0, channel_multiplier=1), writes=[res("Tst")])
        op("pool", lambda e: e.memset(maskU[:], 1.0), writes=[res("maskU")])
        op("pool", lambda e: e.affine_select(out=maskU[:], in_=maskU[:], pattern=[[1, 128]], compare_op=ALU.is_gt,
                                             fill=0.0, base=0, channel_multiplier=-1), writes=[res("maskU")])
        op("pool", lambda e: e.memset(maskS[:], 1.0), writes=[res("maskS")])
        op("pool", lambda e: e.affine_select(out=maskS[:], in_=maskS[:], pattern=[[0, 8], [1, 16]], compare_op=ALU.is_gt,
                                             fill=0.0, base=0, channel_multiplier=-1), writes=[res("maskS")])
        op("pool", lambda e: e.memset(onesb[:], 1.0), writes=[res("onesb")])
        op("pool", lambda e: e.memset(Umat[:], 1.0), writes=[res("Umat")])
        op("pool", lambda e: e.affine_select(out=Umat[:], in_=Umat[:], pattern=[[1, 128]], compare_op=ALU.is_ge,
                                             fill=0.0, base=0, channel_multiplier=-1), writes=[res("Umat")])
        op("pool", lambda e: e.memset(zerosb[:], 0.0), writes=[res("zerosb")])
        op("pool", lambda e: e.memset(uext[:], 0.0), writes=[res("uext")])
        for n, src in enumerate([norm_mix[0], norm_ffn[0], norm_mix[1], norm_ffn[1]]):
            dma(gT[:, n, :], src.rearrange("(c p) -> p c", p=128), writes=[res("gT")], grp=grp("gT"), slow=True)
        dma(gfin[:], norm_final.partition_broadcast(128), writes=[res("gfin")], grp=grp("gfin"))
        dma(dwb_bc[:], dw_b.partition_broadcast(128), writes=[res("cb")], grp=grp("cb"))
        dma(lng_bc[:], ln_g.partition_broadcast(128), writes=[res("cb")], grp=grp("cb"))
        dma(lnb_bc[:], ln_b.partition_broadcast(128), writes=[res("cb")], grp=grp("cb"))
        dma(stg[0:31, 0:512], dw_w, writes=[res("stg")], grp=grp("stg"))
        for c in range(4):
            op("pe", lambda e, c=c: e.transpose(out=ps[:, 0, c * 32:c * 32 + 31], in_=stg[0:31, c * 128:(c + 1) * 128],
                                                identity=identf[0:31, 0:31]),
               reads=[res("stg"), res("identf")], writes=[PB[0]])
        op("dve", lambda e: e.tensor_copy(out=dwT[:], in_=ps[:, 0, 0:128].rearrange("p (c i) -> p c i", c=4)[:, :, 0:31]),
           reads=[PB[0]], writes=[res("dwT")])
    def setup_bias():
        for h in range(16):
            dma(stg[:, 0:640], bfull[h], writes=[res("stg")], grp=grp("stg"))
            eng = "act" if h % 2 == 0 else "dve"
            if eng == "act":
                op("act", lambda e, h=h: e.copy(out=Bb[:, h, :], in_=stg[:, 0:640]), reads=[res("stg")], writes=[res("Bb")])
            else:
                op("dve", lambda e, h=h: e.tensor_copy(out=Bb[:, h, :], in_=stg[:, 0:640]), reads=[res("stg")], writes=[res("Bb")])
        op("pool", lambda e: e.memset(Bb[0:64, :, 576:640], NEG), writes=[res("Bb")])
        op("pool", lambda e: e.memset(Bb[64:128, :, 0:64], NEG), writes=[res("Bb")])

    def wtile_specs():
        sp = []

        def colt(w, c0):
            return (w[:, c0:c0 + 512].rearrange("(k p) c -> p k c", p=128), 128, 8)
        for c0 in (2048, 1536, 0, 512, 1024):
            sp.append(colt(w_in_ab, c0))
        for o in range(2):
            sp.append((w_out_ab[0:512, o * 512:(o + 1) * 512].rearrange("(h p) c -> p h c", p=64), 64, 8))
            sp.append((w_out_ab[512:1024, o * 512:(o + 1) * 512].rearrange("(k p) c -> p k c", p=128), 128, 4))
        for l in range(2):
            if l == 1:
                for c0 in range(0, 3072, 512):
                    sp.append(colt(w_in_c, c0))
                for o in range(2):
                    sp.append(colt(w_out_c, o * 512))
            for j in range(8):
                sp.append(colt(w_up[l], j * 512))
            for o in range(2):
                for g in range(4):
                    sp.append((w_down[l][g * 1024:(g + 1) * 1024, o * 512:(o + 1) * 512].rearrange("(k p) c -> p k c", p=128), 128, 8))
        assert len(sp) == N_WT
        return sp

    WSPEC = wtile_specs()

    def convert_gen():
        def load(n):
            src, P, K = WSPEC[n]
            i = n % 2
            fv = cvf[i][0:P, 0:K * 512].rearrange("p (k c) -> p k c", k=K)
            dma(fv, src, writes=[res(f"cvf{i}")], grp=grp(f"cvf{i}"))

        def cast_store(n):
            src, P, K = WSPEC[n]
            i = n % 2
            eng = ("act", "dve", "pool")[n % 3]
            if eng == "act":
                op("act", lambda e: e.copy(out=cvb[i][0:P, 0:K * 512], in_=cvf[i][0:P, 0:K * 512]),
                   reads=[res(f"cvf{i}")], writes=[res(f"cvb{i}")])
            else:
                op(eng, lambda e: e.tensor_copy(out=cvb[i][0:P, 0:K * 512], in_=cvf[i][0:P, 0:K * 512]),
                   reads=[res(f"cvf{i}")], writes=[res(f"cvb{i}")])
            dma(wsc[n, 0:P, 0:K * 512], cvb[i][0:P, 0:K * 512], reads=[res(f"cvb{i}")], writes=[res(f"wsc{n}")],
                grp=grp(f"cvb{i}"))
        load(0)
        for n in range(N_WT):
            if n + 1 < N_WT:
                load(n + 1)
            cast_store(n)
            yield

    BG = {"gen": None, "busy": False, "every": 45, "cnt": 0}

    def bg_advance(k=1):
        if BG["gen"] is None or BG["busy"]:
            return
        BG["busy"] = True
        try:
            for _ in range(k):
                next(BG["gen"])
        except StopIteration:
            BG["gen"] = None
        BG["busy"] = False

    def bg_flush():
        while BG["gen"] is not None:
            bg_advance(1)

    _raw_op = S.op

    def op(eng, fn, reads=(), writes=(), dma=None):
        t = _raw_op(eng, fn, reads=reads, writes=writes, dma=dma)
        if BG["gen"] is not None and not BG["busy"]:
            BG["cnt"] += 1
            if BG["cnt"] >= BG["every"]:
                BG["cnt"] = 0
                bg_advance(1)
        return t

    class WStream:
        def __init__(self):
            self.issued = 0
            self.taken = 0

        def _issue(self):
            n = self.issued
            t = n % N_WT
            slot = n % 3
            _, P, K = WSPEC[t]
            dma(wring[slot][0:P, 0:K * 512], wsc[t, 0:P, 0:K * 512], reads=[res(f"wsc{t}")],
                writes=[res(f"wring{slot}")], grp=grp(f"wring{slot}"))
            if P == 64:
                dma(wring[slot][64:128, 0:K * 512], wsc[t, 0:P, 0:K * 512], reads=[res(f"wsc{t}")],
                    writes=[res(f"wring{slot}")], grp=grp(f"wring{slot}"))
            self.issued += 1

        def next(self, total):
            while self.issued < min(self.taken + 2, total):
                self._issue()
            n = self.taken
            self.taken += 1
            slot = n % 3
            _, P, K = WSPEC[n % N_WT]
            P = 128 if P == 64 else P
            return wring[slot][0:P, 0:K * 512].rearrange("p (k c) -> p k c", k=K), res(f"wring{slot}")

    WS = WStream()
    TOTAL_TILES = 9 * N_WT

    def wnext():
        return WS.next(TOTAL_TILES)

    bank_rr = [0]

    BANKSEL = [None]

    def bank4():
        sel = BANKSEL[0] or (0, 1, 2, 3)
        b = sel[bank_rr[0] % len(sel)]
        bank_rr[0] += 1
        return b

    def evac_copy(eng, out, in_, reads, writes, scale=None):
        if eng == "act":
            if scale is None:
                op("act", lambda e: e.copy(out=out, in_=in_), reads=reads, writes=writes)
            else:
                op("act", lambda e: e.activation(out=out, in_=in_, func=AF.Copy, scale=scale), reads=reads, writes=writes)
        else:
            if scale is None:
                op(eng, lambda e: e.tensor_copy(out=out, in_=in_), reads=reads, writes=writes)
            else:
                op(eng, lambda e: e.tensor_scalar(out=out, in0=in_, scalar1=scale, scalar2=None, op0=ALU.mult),
                   reads=reads, writes=writes)

    def rmsnorm(P, NT, gidx, final=False):
        for t in range(NT):
            op("act", lambda e, t=t: e.activation(out=junk[0:P, :], in_=xtok[0:P, t, :], func=AF.Square,
                                                  accum_out=stat[0:P, t:t + 1]),
               reads=[res("xtok")], writes=[res(f"nss{t}"), res("junk")])
        op("act", lambda e: e.activation(out=stat[0:P, 8:8 + NT], in_=stat[0:P, 0:NT], func=AF.Sqrt,
                                         scale=1.0 / 1024.0, bias=RMS_EPS),
           reads=[res(f"nss{t}") for t in range(NT)], writes=[res("nsd")])
        op("dve", lambda e: e.reciprocal(out=stat[0:P, 16:16 + NT], in_=stat[0:P, 8:8 + NT]),
           reads=[res("nsd")], writes=[res("nrs")])
        if final:
            for t in range(NT):
                op("dve", lambda e, t=t: e.scalar_tensor_tensor(out=xtok[0:P, t, :], in0=xtok[0:P, t, :],
                                                                scalar=stat[0:P, 16 + t:17 + t], in1=gfin[0:P, :],
                                                                op0=ALU.mult, op1=ALU.mult),
                   reads=[res("nrs"), res("gfin")], writes=[res("xtok")])
            return
        hbuf = [hn, hn2]

        def scale(t):
            op("dve", lambda e: e.tensor_scalar(out=hbuf[t % 2][0:P, :], in0=xtok[0:P, t, :], scalar1=stat[0:P, 16 + t:17 + t],
                                                scalar2=None, op0=ALU.mult),
               reads=[res("xtok"), res("nrs")], writes=[res(f"hn{t % 2}")])

        def tr_evac(t):
            bk = 4 + (t % 2)
            for kc in range(8):
                op("pe", lambda e, kc=kc: e.transpose(out=psb16(bk)[:, kc * P:(kc + 1) * P],
                                                      in_=hbuf[t % 2][0:P, kc * 128:(kc + 1) * 128], identity=identb[0:P, 0:P]),
                   reads=[res(f"hn{t % 2}"), res("identb")], writes=[PB[bk]])
            op("dve", lambda e: e.tensor_tensor(
                out=hT[:, :, t * P:(t + 1) * P],
                in0=psb16(bk)[:, 0:8 * P].rearrange("p (k j) -> p k j", k=8),
                in1=gT[:, gidx, :].unsqueeze(2).broadcast_to([128, 8, P]), op=ALU.mult),
               reads=[PB[bk], res("gT")], writes=[res("hT")])
        scale(0)
        if NT > 1:
            scale(1)
        for t in range(NT):
            tr_evac(t)
            if t + 2 < NT:
                scale(t + 2)

    def proj_fm(wt, wres, col0, nchunks, N, evac):
        for c in range(nchunks):
            b = bank4()
            for kc in range(8):
                op("pe", lambda e, c=c, kc=kc, b=b: e.matmul(ps[:, b, 0:N], lhsT=wt[:, kc, col0 + c * 128:col0 + (c + 1) * 128],
                                                             rhs=hT[:, kc, 0:N], start=(kc == 0), stop=(kc == 7)),
                   reads=[wres, res("hT")], writes=[PB[b]])
            evac(c, b)

    def proj_fm_g(wt, wres, col0, nchunks, N, evac):
        for c in range(nchunks):
            b = bank4()
            for kc in range(8):
                op("pe", lambda e, c=c, kc=kc, b=b: e.matmul(ps[:, b, 0:N], lhsT=wt[:, kc, col0 + c * 128:col0 + (c + 1) * 128],
                                                             rhs=hT[:, kc, 0:N], start=(kc == 0), stop=(kc == 7)),
                   reads=[wres, res("hT")], writes=[PB[b]])
            evac(c, b)
            yield

    def proj_tm_g(wt, wres, tiles, evac):
        for ti, (lo, P) in enumerate(tiles):
            b = bank4()
            for kc in range(8):
                op("pe", lambda e, kc=kc, b=b, lo=lo, P=P: e.matmul(ps[0:P, b, :], lhsT=hT[:, kc, lo:lo + P], rhs=wt[:, kc, :],
                                                                    start=(kc == 0), stop=(kc == 7)),
                   reads=[wres, res("hT")], writes=[PB[b]])
            evac(ti, lo, P, b)
            yield

    def proj_tm(wt, wres, tiles, evac):
        for ti, (lo, P) in enumerate(tiles):
            b = bank4()
            for kc in range(8):
                op("pe", lambda e, kc=kc, b=b, lo=lo, P=P: e.matmul(ps[0:P, b, :], lhsT=hT[:, kc, lo:lo + P], rhs=wt[:, kc, :],
                                                                    start=(kc == 0), stop=(kc == 7)),
                   reads=[wres, res("hT")], writes=[PB[b]])
            evac(ti, lo, P, b)

    def run_pool(factories, nslots, stagger=0, filler=None):
        pending = list(factories)
        active = {}
        rnd = 0
        while pending or active:
            for slot in range(nslots):
                if slot not in active and pending and rnd >= slot * stagger:
                    active[slot] = pending.pop(0)(slot)
            for slot in list(active):
                try:
                    next(active[slot])
                except StopIteration:
                    del active[slot]
            rnd += 1
            if filler is not None and active:
                filler(rnd)

    def sb_banks(slot, n_, three):
        if three == 4:
            return slot % 2, 2 + slot, 6 + slot % 2, (0, 0, 64, 64)[slot]
        if three:
            return slot, 3 + slot, (6, 7, 6)[slot], (0, 0, 64)[slot]
        return slot, 4 + slot, 6 + slot, 0

    def sb_head(slot, steps, evac, split=False, three=False):
        EA, L2t, Lbt, Wbt = SBT[slot]
        if slot == 3:
            rEA, rL2, rLb, rWb = res("kst"), res("vst"), res("stg"), res("stg")
        else:
            rEA, rL2, rLb, rWb = [res(f"sb{n}{slot}") for n in ("EA", "L2", "Lb", "Wb")]
        _, Xb, Ob, opo = sb_banks(slot, 0, three)
        op("pe", lambda e: e.matmul(ps[:, Xb, :], lhsT=zerosb[:, 0:128], rhs=zerosb[:, 0:512], start=True, stop=True),
           reads=[res("zerosb")], writes=[PB[Xb]])
        op("pe", lambda e: e.matmul(ps[opo:opo + 64, Ob, :], lhsT=zerosb[:, 0:64], rhs=zerosb[:, 0:512], start=True, stop=True),
           reads=[res("zerosb")], writes=[PB[Ob]])
        yield
        for n_, stp in enumerate(steps):
            yield from sb_one_step(slot, n_, stp, split, EA, L2t, Lbt, Wbt, rEA, rL2, rLb, rWb, Xb, Ob, opo, three)
        evac(Ob, opo)
        yield

    def sb_one_step(slot, n_, stp, split, EA, L2t, Lbt, Wbt, rEA, rL2, rLb, rWb, Xb, Ob, opo, three):
        if True:
            KK, zparts, c0, N, mask, vparts, kres, vres = stp
            if split:
                zb0 = 2 * slot
                zv = ps[0:KK, zb0:zb0 + 2, 0:64]
                zres = [PB[zb0], PB[zb0 + 1]]
                v3 = lambda ap: ap.rearrange("p (a b) -> p a b", a=2)
            else:
                zb0 = sb_banks(slot, n_, three)[0]
                zv = ps[0:KK, zb0, c0:N]
                zres = [PB[zb0]]
                v3 = lambda ap: ap
            for (l, r, zsel, a_, b_) in zparts:
                bk = zb0 + zsel
                op("pe", lambda e, l=l, r=r, bk=bk, a_=a_, b_=b_: e.matmul(ps[0:KK, bk, a_:b_], lhsT=l, rhs=r, start=True, stop=True),
                   reads=list(kres) + [res("qT0")], writes=[PB[bk]])
            yield
            op("act", lambda e: e.activation(out=v3(EA[0:KK, c0:N]), in_=zv, func=AF.Exp, scale=-1.0), reads=zres, writes=[rEA])
            yield
            op("act", lambda e: e.activation(out=L2t[0:KK, c0:N], in_=EA[0:KK, c0:N], func=AF.Ln, bias=1.0), reads=[rEA], writes=[rL2])
            yield
            op("dve", lambda e: e.tensor_tensor(out=v3(Lbt[0:KK, c0:N]), in0=v3(L2t[0:KK, c0:N]), in1=zv, op=ALU.add),
               reads=[rL2] + zres, writes=[rLb])
            yield
            if mask is not None:
                m_ap, m0, m1 = mask
                op("pool", lambda e: e.tensor_tensor(out=Lbt[0:KK, m0:m1], in0=Lbt[0:KK, m0:m1], in1=m_ap, op=ALU.mult),
                   reads=[res("maskU"), res("maskS")], writes=[rLb])
                yield
            op("pe", lambda e: e.matmul(ps[0:KK, Xb, c0:N], lhsT=Tst[0:KK, 0:KK], rhs=Lbt[0:KK, c0:N], start=False, stop=True,
                                        skip_group_check=True),
               reads=[res("Tst"), rLb], writes=[PB[Xb]])
            yield
            op("dve", lambda e: e.tensor_tensor(out=EA[0:KK, c0:N], in0=L2t[0:KK, c0:N], in1=ps[0:KK, Xb, c0:N], op=ALU.add),
               reads=[rL2, PB[Xb]], writes=[rEA])
            yield
            op("act", lambda e: e.activation(out=Wbt[0:KK, c0:N], in_=EA[0:KK, c0:N], func=AF.Exp, scale=-1.0), reads=[rEA], writes=[rWb])
            yield
            if mask is not None:
                m_ap, m0, m1 = mask
                op("pool", lambda e: e.tensor_tensor(out=Wbt[0:KK, m0:m1], in0=Wbt[0:KK, m0:m1], in1=m_ap, op=ALU.mult),
                   reads=[res("maskU"), res("maskS")], writes=[rWb])
                yield
            for (vl, a_, b_) in vparts:
                op("pe", lambda e, vl=vl, a_=a_, b_=b_: e.matmul(ps[opo:opo + 64, Ob, a_:b_], lhsT=vl, rhs=Wbt[0:KK, a_:b_], start=False, stop=True,
                                                                 skip_group_check=True),
                   reads=list(vres) + [rWb], writes=[PB[Ob]])
            op("pe", lambda e: e.matmul(ps[:, Xb, c0:N], lhsT=Umat[0:KK, :], rhs=Lbt[0:KK, c0:N], start=False, stop=True,
                                        skip_group_check=True),
               reads=[res("Umat"), rLb], writes=[PB[Xb]])
            if three == 4:
                for _ in range(SB4_DUMMY):
                    op("pe", lambda e: e.matmul(ps[:, Xb, 0:512], lhsT=zerosb[:, 0:128], rhs=zerosb[:, 0:512], start=False, stop=True,
                                                skip_group_check=True),
                       reads=[res("zerosb")], writes=[PB[Xb]])
            yield

    def conv_gen(tiles):
        for c in range(4):
            D = Dring[c % 2]
            dres = res(f"D{c % 2}")
            op("pool", lambda e, c=c, D=D: e.tensor_tensor(out=D[:], in0=identb[:].unsqueeze(1).broadcast_to([128, 31, 128]),
                                                           in1=dwT[:, c, :].unsqueeze(2).broadcast_to([128, 31, 128]), op=ALU.mult),
               reads=[res("identb"), res("dwT")], writes=[dres])
            for ti, (ufn, P, lo) in enumerate(tiles):
                for i in range(31):
                    op("pe", lambda e, ti=ti, ufn=ufn, P=P, c=c, i=i, D=D: e.matmul(
                        ps[0:P, ti, c * 128:(c + 1) * 128], lhsT=ufn(c, i, P), rhs=D[:, i, :], start=(i == 0), stop=(i == 30)),
                       reads=[dres, res("uext")], writes=[PB[ti]])
                yield

    def conv_module(tiles):
        nt = len(tiles)
        for c in range(4):
            D = Dring[c % 2]
            dres = res(f"D{c % 2}")
            op("pool", lambda e, c=c, D=D: e.tensor_tensor(out=D[:], in0=identb[:].unsqueeze(1).broadcast_to([128, 31, 128]),
                                                           in1=dwT[:, c, :].unsqueeze(2).broadcast_to([128, 31, 128]), op=ALU.mult),
               reads=[res("identb"), res("dwT")], writes=[dres])
            for ti, (ufn, P, lo) in enumerate(tiles):
                for i in range(31):
                    op("pe", lambda e, ti=ti, ufn=ufn, P=P, c=c, i=i, D=D: e.matmul(
                        ps[0:P, ti, c * 128:(c + 1) * 128], lhsT=ufn(c, i, P), rhs=D[:, i, :], start=(i == 0), stop=(i == 30)),
                       reads=[dres, res("uext")], writes=[PB[ti]])
                    if CONV_DUMMY and P == 128 and i % CONV_DUMMY == CONV_DUMMY - 1:
                        op("pe", lambda e: e.matmul(ps[:, 6, :], lhsT=zerosb[:, 0:128], rhs=zerosb[:, 0:512], start=True, stop=True,
                                                    skip_group_check=True), reads=[res("zerosb")], writes=[PB[6]])
        for ti, (ufn, P, lo) in enumerate(tiles):
            conv_epilogue(ti, P, lo)

    def conv_epilogue(ti, P, lo):
        if ti % 2 == 0:
            yb, cb_, ry, rc = ysb, ctm, res("ysb"), res("ctm")
        else:
            yb, cb_, ry, rc = ysb2, ctm2, res("stg"), res("stg")
        so = 32 + 16 * (ti % 2)
        rst = res(f"cstat{ti % 2}")
        op("dve", lambda e: e.tensor_tensor(out=yb[0:P, :], in0=ps[0:P, ti, :], in1=dwb_bc[0:P, :], op=ALU.add),
           reads=[PB[ti], res("cb")], writes=[ry])
        op("dve", lambda e: e.bn_stats(out=stat[0:P, so:so + 6], in_=yb[0:P, :]), reads=[ry], writes=[rst])
        op("dve", lambda e: e.bn_aggr(out=stat[0:P, so + 6:so + 8], in_=stat[0:P, so:so + 6]), reads=[rst], writes=[rst])
        op("act", lambda e: e.activation(out=stat[0:P, so + 8:so + 9], in_=stat[0:P, so + 7:so + 8], func=AF.Sqrt, scale=1.0, bias=LN_EPS),
           reads=[rst], writes=[rst])
        op("dve", lambda e: e.reciprocal(out=stat[0:P, so + 9:so + 10], in_=stat[0:P, so + 8:so + 9]), reads=[rst], writes=[rst])
        op("dve", lambda e: e.tensor_scalar(out=yb[0:P, :], in0=yb[0:P, :], scalar1=stat[0:P, so + 6:so + 7], scalar2=stat[0:P, so + 9:so + 10],
                                            op0=ALU.subtract, op1=ALU.mult),
           reads=[rst], writes=[ry])
        op("pool", lambda e: e.tensor_tensor(out=yb[0:P, :], in0=yb[0:P, :], in1=lng_bc[0:P, :], op=ALU.mult),
           reads=[res("cb")], writes=[ry])
        op("pool", lambda e: e.tensor_tensor(out=yb[0:P, :], in0=yb[0:P, :], in1=lnb_bc[0:P, :], op=ALU.add),
           reads=[res("cb")], writes=[ry])
        if rc is ry:
            op("act", lambda e: e.activation(out=cb_[0:P, :], in_=yb[0:P, :], func=AF.Silu), reads=[], writes=[ry])
        else:
            op("act", lambda e: e.activation(out=cb_[0:P, :], in_=yb[0:P, :], func=AF.Silu), reads=[ry], writes=[rc])
        b = 4 + (ti % 2)
        for c in range(4):
            op("pe", lambda e, c=c: e.transpose(out=psb16(b)[:, c * P:(c + 1) * P], in_=cb_[0:P, c * 128:(c + 1) * 128],
                                                identity=identb[0:P, 0:P]),
               reads=[rc, res("identb")], writes=[PB[b]])
        op("act", lambda e: e.copy(out=cT[:, :, lo:lo + P], in_=psb16(b)[:, 0:4 * P].rearrange("p (c j) -> p c j", c=4)),
           reads=[PB[b]], writes=[res("cT")])

    def split512(a, e_):
        out = []
        while a < e_:
            bnd = (a // 512 + 1) * 512
            x = min(e_, bnd)
            out.append((a, x))
            a = x
        return out

    def band_head(slot, h, P, qcol, segs, groups, kmin, W):
        c, po = h // 2, 64 * (h % 2)
        sb_ = 2 * slot
        started = set()
        for (kfn, n, dst, kr) in segs:
            for (a, e_) in split512(dst, dst + n):
                bb = sb_ + a // 512
                first = bb not in started
                started.add(bb)
                op("pe", lambda e, a=a, e_=e_, bb=bb, kfn=kfn, dst=dst, first=first: e.matmul(
                    ps[0:P, bb, a % 512:(a % 512) + (e_ - a)], lhsT=qT1[po:po + 64, c, qcol:qcol + P],
                    rhs=kfn(c, po)[:, a - dst:e_ - dst], start=first, stop=True, skip_group_check=True),
                   reads=[res("qT1"), kr], writes=[PB[bb]])
        yield
        for (a, e_) in split512(kmin, W):
            bb = sb_ + a // 512
            op("pe", lambda e, a=a, e_=e_, bb=bb: e.matmul(ps[0:P, bb, a % 512:(a % 512) + (e_ - a)], lhsT=identb[0:P, 0:P],
                                                           rhs=Bb[0:P, h, a:e_], start=False, stop=True, skip_group_check=True),
               reads=[res("identb"), res("Bb")], writes=[PB[bb]])
        yield
        spv = ps[0:P, sb_:sb_ + 2, :].rearrange("p a b -> p (a b)")
        rneg = res(f"negm{slot}")
        op("dve", lambda e: e.tensor_reduce(out=stat[0:P, 48 + slot:49 + slot], in_=spv[:, kmin:W], axis=AX.X, op=ALU.max, negate=True),
           reads=[PB[sb_], PB[sb_ + 1]], writes=[rneg])
        yield
        op("act", lambda e: e.activation(out=Pe[slot][0:P, kmin:W], in_=spv[:, kmin:W], func=AF.Exp, bias=stat[0:P, 48 + slot:49 + slot],
                                         scale=1.0, accum_out=rs16[0:P, h:h + 1]),
           reads=[PB[sb_], PB[sb_ + 1], rneg], writes=[res(f"Pe{slot}"), res(f"rs{h}")])
        yield
        tb = 4 + slot
        ng = len(groups)
        for gi, (g0, KK, vfn, vr) in enumerate(groups):
            op("pe", lambda e, gi=gi, g0=g0, KK=KK: e.transpose(out=psb16(tb)[0:KK, gi * 128:gi * 128 + P],
                                                               in_=Pe[slot][0:P, g0:g0 + KK], identity=identb[0:P, 0:P]),
               reads=[res(f"Pe{slot}"), res("identb")], writes=[PB[tb]])
        yield
        eng = "act" if h % 2 == 0 else "dve"
        kkmax = max(g[1] for g in groups)
        if all(g[1] == kkmax for g in groups):
            evac_copy(eng, PT[slot][0:kkmax, 0:ng, 0:P], psb16(tb)[0:kkmax, 0:ng * 128].rearrange("p (g j) -> p g j", g=ng)[:, :, 0:P],
                      reads=[PB[tb]], writes=[res(f"PT{slot}")])
        else:
            evac_copy(eng, PT[slot][0:128, 0:ng - 1, 0:P], psb16(tb)[0:128, 0:(ng - 1) * 128].rearrange("p (g j) -> p g j", g=ng - 1)[:, :, 0:P],
                      reads=[PB[tb]], writes=[res(f"PT{slot}")])
            kl = groups[-1][1]
            evac_copy(eng, PT[slot][0:kl, ng - 1, 0:P], psb16(tb)[0:kl, (ng - 1) * 128:(ng - 1) * 128 + P],
                      reads=[PB[tb]], writes=[res(f"PT{slot}")])
        yield
        ob = 6 + h // 8
        for gi, (g0, KK, vfn, vr) in enumerate(groups):
            op("pe", lambda e, gi=gi, KK=KK, vfn=vfn: e.matmul(
                ps[0:P, ob, (h % 8) * 64:(h % 8 + 1) * 64], lhsT=PT[slot][0:KK, gi, 0:P], rhs=vfn(h),
                start=(gi == 0), stop=(gi == ng - 1)),
               reads=[res(f"PT{slot}"), vr], writes=[PB[ob]])
        yield

    def band_qtile(P, qcol, segs, groups, kmin, W, osb_done_cb, par):
        Ob = 6
        state = {"done": set(), "started8": False, "norm": [False, False]}

        def normalize(hb):
            state["norm"][hb] = True
            op("dve", lambda e: e.reciprocal(out=rs16[0:P, 16 + hb * 8:24 + hb * 8], in_=rs16[0:P, hb * 8:hb * 8 + 8]),
               reads=[res(f"rs{h}") for h in range(hb * 8, hb * 8 + 8)], writes=[res(f"rinv{hb}")])
            op("dve", lambda e: e.tensor_tensor(
                out=osb[0:P, hb * 512:(hb + 1) * 512].rearrange("p (h d) -> p h d", h=8),
                in0=ps[0:P, Ob + hb, :].rearrange("p (h d) -> p h d", h=8),
                in1=rs16[0:P, 16 + hb * 8:24 + hb * 8].unsqueeze(2).broadcast_to([P, 8, 64]), op=ALU.mult),
               reads=[PB[Ob + hb], res(f"rinv{hb}")], writes=[res(f"osb{hb}")])

        def head_gen(slot, h):
            if h >= 8:
                state["started8"] = True
            yield from band_head(slot, h, P, qcol, segs, groups, kmin, W)
            state["done"].add(h)
            if not state["norm"][0] and all(x in state["done"] for x in range(8)):
                normalize(0)

        def filler(rnd):
            if not BAND_DUMMY or P < 128:
                return
            if not state["started8"]:
                bk = Ob + 1
            elif state["norm"][0]:
                bk = Ob
            else:
                return
            for _ in range(BAND_DUMMY):
                op("pe", lambda e: e.matmul(ps[:, bk, 0:BAND_DUMMY_N], lhsT=zerosb[:, 0:128], rhs=zerosb[:, 0:BAND_DUMMY_N],
                                            start=True, stop=True, skip_group_check=True),
                   reads=[res("zerosb")], writes=[PB[bk]])
        run_pool([(lambda slot, h=h: head_gen(slot, h)) for h in range(16)], 2, stagger=STG_BAND, filler=filler)
        if not state["norm"][0]:
            normalize(0)
        normalize(1)
        tb = 4 + par
        for kc in range(8):
            op("pe", lambda e, kc=kc, tb=tb: e.transpose(out=psb16(tb)[:, kc * P:(kc + 1) * P], in_=osb[0:P, kc * 128:(kc + 1) * 128],
                                                         identity=identb[0:P, 0:P]),
               reads=[res(f"osb{kc // 4}"), res("identb")], writes=[PB[tb]])
        op("act", lambda e, tb=tb: e.copy(out=oT[:, :, qcol:qcol + P], in_=psb16(tb)[:, 0:8 * P].rearrange("p (k j) -> p k j", k=8)),
           reads=[PB[tb]], writes=[res("oT")])

    def mlp(l, P, NT):
        N = P * NT
        rmsnorm(P, NT, 1 + 2 * l)
        for j in range(8):
            wu, wr = wnext()

            def ev(c, b, j=j):
                i2 = (j * 4 + c) % 2
                op("act", lambda e: e.activation(out=rl[i2][:, 0:N], in_=ps[:, b, 0:N], func=AF.Relu),
                   reads=[PB[b]], writes=[res(("kst", "vst")[i2])] + ([res("vst1")] if (j * 4 + c) < 2 else []))
                op("dve", lambda e: e.tensor_tensor(out=hidT[:, j * 4 + c, 0:N], in0=rl[i2][:, 0:N], in1=rl[i2][:, 0:N], op=ALU.mult),
                   reads=[res(("kst", "vst")[i2])], writes=[res("hidT")] + ([res("kst1")] if (j * 4 + c) == 0 else []))
            proj_fm(wu, wr, 0, 4, N, ev)
        for o in range(2):
            for g in range(4):
                wd, wr = wnext()
                for t in range(NT):
                    for j in range(8):
                        op("pe", lambda e, t=t, j=j, g=g, wd=wd: e.matmul(ps[0:P, t, :], lhsT=hidT[:, g * 8 + j, t * P:(t + 1) * P], rhs=wd[:, j, :],
                                                                          start=(g == 0 and j == 0), stop=(g == 3 and j == 7)),
                           reads=[wr, res("hidT")], writes=[PB[t]])
            for t in range(NT):
                op("dve", lambda e, t=t, o=o: e.tensor_tensor(out=xtok[0:P, t, o * 512:(o + 1) * 512], in0=xtok[0:P, t, o * 512:(o + 1) * 512],
                                                              in1=ps[0:P, t, :], op=ALU.add),
                   reads=[PB[t]], writes=[res("xtok")])

    def out_proj_resid(P, NT, parts_fn):
        for o in range(2):
            parts = parts_fn(o)
            for t in range(NT):
                b = bank4()
                n = len(parts)
                for pi, (lf, rhs, rd) in enumerate(parts):
                    op("pe", lambda e, lf=lf, rhs=rhs, t=t, b=b, pi=pi: e.matmul(ps[0:P, b, :], lhsT=lf(t), rhs=rhs, start=(pi == 0), stop=(pi == n - 1)),
                       reads=rd, writes=[PB[b]])
                op("dve", lambda e, t=t, o=o, b=b: e.tensor_tensor(out=xtok[0:P, t, o * 512:(o + 1) * 512], in0=xtok[0:P, t, o * 512:(o + 1) * 512],
                                                                    in1=ps[0:P, b, :], op=ALU.add),
                   reads=[PB[b]], writes=[res("xtok")])

    FIRST_L1_DONE = [False]
    HI = {}

    def run_block(sample, s=0, i=0):
        def st(k):
            MARKS.append((("S" if sample else f"P{s}{i}"), k, S.nops["pe"]))
            stop(k + (100 if sample else 0))
        st(0)
        if sample:
            P, NT, N = 64, 1, 64
            kvt = [(16 * b, 16) for b in range(4)]
            dma(xtok[0:64, 0, :], xs, writes=[res("xtok")], grp=grp("xtok"))
        else:
            P, NT, N = 128, 4, 512
            kvt = [(128 * t, 128) for t in range(4)]
            t0 = 512 * i
            dma(xtok[:, :, :], xp[s, t0:t0 + 512, :].rearrange("(t p) f -> p t f", p=128), writes=[res("xtok")], grp=grp("xtok"))
            if i == 0:
                op("pool", lambda e: e.memset(uext[:, :, 0:30], 0.0), writes=[res("uext")])

        if not sample:
            DUMP[0] = lambda: dma(yp[s, t0:t0 + 512, :].rearrange("(t p) f -> p t f", p=128), xtok[:, :, :], reads=[res("xtok")], grp=grp("ystore"))
        rmsnorm(P, NT, 0)
        wt, wr = wnext()
        proj_fm(wt, wr, 0, 4, N, lambda c, b: op("act", lambda e: e.activation(out=sg[:, c, 0:N], in_=ps[:, b, 0:N], func=AF.Sigmoid),
                                                 reads=[PB[b]], writes=[res("sg")]))
        wt, wr = wnext()

        def ev_a(c, b):
            if sample:
                for q in range(4):
                    op("dve", lambda e, q=q: e.tensor_tensor(out=uexts[:, q, c, 30:46], in0=ps[:, b, 16 * q:16 * q + 16], in1=sg[:, c, 16 * q:16 * q + 16], op=ALU.mult),
                       reads=[PB[b], res("sg")], writes=[res("uext")])
                op("dve", lambda e: e.tensor_tensor(out=utail[:, c, 0:64], in0=ps[:, b, 0:64], in1=sg[:, c, 0:64], op=ALU.mult),
                   reads=[PB[b], res("sg")], writes=[res("utail")])
            else:
                op("dve", lambda e: e.tensor_tensor(out=uext[:, c, 30:542], in0=ps[:, b, 0:512], in1=sg[:, c, 0:512], op=ALU.mult),
                   reads=[PB[b], res("sg")], writes=[res("uext")])
                if i == 3:
                    op("dve", lambda e: e.tensor_tensor(out=utail[:, c, 0:32], in0=ps[:, b, 480:512], in1=sg[:, c, 480:512], op=ALU.mult),
                       reads=[PB[b], res("sg")], writes=[res("utail")])
        proj_fm(wt, wr, 0, 4, N, ev_a)

        def qkv_gen():
            wt, wr = wnext()
            yield from proj_fm_g(wt, wr, 0, 4, N, lambda c, b: evac_copy("act", qT0[:, c, 0:N], ps[:, b, 0:N], [PB[b]], [res("qT0")], scale=0.125))
            wt, wr = wnext()
            if sample:
                yield from proj_fm_g(wt, wr, 0, 4, N, lambda c, b: evac_copy("dve", kTs[:, c, 0:N], ps[:, b, 0:N], [PB[b]], [res("kT0")]))
            else:
                yield from proj_fm_g(wt, wr, 0, 4, N, lambda c, b: evac_copy("dve", kT0[:, c, t0:t0 + N], ps[:, b, 0:N], [PB[b]], [res("kT0")]))

            def ev_k(ti, lo, Pk, b):
                evac_copy("act", kst[0:Pk, :], ps[0:Pk, b, :], [PB[b]], [res("kst")])
                dst = o_sbk_s[lo:lo + Pk, :] if sample else o_sbk_p[s, t0 + lo:t0 + lo + Pk, :]
                dma(dst, kst[0:Pk, :], reads=[res("kst")], grp=grp("kst"))
            yield from proj_tm_g(wt, wr, kvt, ev_k)
            wt, wr = wnext()

            def ev_v(ti, lo, Pk, b):
                evac_copy("act", vst[0:Pk, :], ps[0:Pk, b, :], [PB[b]], [res("vst")])
                if sample:
                    pass
                else:
                    evac_copy("dve", v0[:, 4 * i + ti, :], ps[:, b, :], [PB[b]], [res("v0")])
                dst = o_sbv_s[lo:lo + Pk, :] if sample else o_sbv_p[s, t0 + lo:t0 + lo + Pk, :]
                dma(dst, vst[0:Pk, :], reads=[res("vst")], grp=grp("vst"))
                if sample:
                    evac_copy("dve", vnew0[0:16, ti, :], ps[0:16, b, :], [PB[b]], [res("vnew")])
            yield from proj_tm_g(wt, wr, kvt, ev_v)

        st(2)
        if sample:
            for q in range(4):
                dma(stg[0:30, 0:512], cconv[q], writes=[res("stg")], grp=grp("stg"))
                for c in range(4):
                    op("pe", lambda e, c=c: e.transpose(out=ps[:, 3, c * 32:c * 32 + 30], in_=stg[0:30, c * 128:(c + 1) * 128], identity=identf[0:30, 0:30]),
                       reads=[res("stg"), res("identf")], writes=[PB[3]])
                op("dve", lambda e, q=q: e.tensor_copy(out=uexts[:, q, :, 0:30], in_=ps[:, 3, 0:128].rearrange("p (c j) -> p c j", c=4)[:, :, 0:30]),
                   reads=[PB[3]], writes=[res("uext")])
                for c in range(4):
                    op("pe", lambda e, c=c, q=q: e.transpose(out=ps[0:16, 2, c * 128:(c + 1) * 128], in_=utail[:, c, 16 * q:16 * q + 16], identity=identf[:, :]),
                       reads=[res("utail"), res("identf")], writes=[PB[2]])
                op("act", lambda e: e.copy(out=cst[0:16, :], in_=ps[0:16, 2, :]), reads=[PB[2]], writes=[res("ysb")])
                dma(o_conv_s[q, 14:30, :], cst[0:16, :], reads=[res("ysb")], grp=grp("cst"))
                dma(o_conv_s[q, 0:14, :], cconv[q, 16:30, :], grp=grp("cst2"))
        elif i == 3:
            for c in range(4):
                op("pe", lambda e, c=c: e.transpose(out=ps[0:32, 2, c * 128:(c + 1) * 128], in_=utail[:, c, 0:32], identity=identf[:, :]),
                   reads=[res("utail"), res("identf")], writes=[PB[2]])
            op("act", lambda e: e.copy(out=cst[0:32, :], in_=ps[0:32, 2, :]), reads=[PB[2]], writes=[res("ysb")])
            dma(o_conv_p[s, :, :], cst[2:32, :], reads=[res("ysb")], grp=grp("cst"))

        if sample:
            ctiles = [((lambda c, ii, Pq, q=q: uexts[:, q, c, ii:ii + 16]), 16, 16 * q) for q in range(4)]
        else:
            ctiles = [((lambda c, ii, Pq, t=t: uext[:, c, t * 128 + ii:t * 128 + ii + 128]), 128, 128 * t) for t in range(4)]
        BANKSEL[0] = (4, 5, 6, 7)
        run_pool([lambda slot: conv_gen(ctiles), lambda slot: qkv_gen()], 2)
        BANKSEL[0] = None
        if not sample:
            op("pool", lambda e: e.tensor_copy(out=uext[:, :, 0:30], in_=uext[:, :, 512:542]), reads=[], writes=[res("uext")])
        for ti, (ufn, Pc, lo) in enumerate(ctiles):
            conv_epilogue(ti, Pc, lo)

        st(3)
        HI.clear()
        if sample:
            CS = [(kT0, v0, [res("kT0")], [res("v0")]),
                  (kT0b, v0b, [res("kT1_0"), res("kT1_1")], [res("v1_0"), res("v1_1"), res("v1s")])]
            KSTG = [(stg[:, 0:512], res("stg"), grp("stg")), (kst[:, :], res("kst"), grp("kst"))]
            VSTG = [(stg[:, 512:1024], res("stgv"), grp("stgv")), (vst[:, :], res("vst"), grp("vst"))]

            def sb_load(q):
                kc_, vc_, kr_, vr_ = CS[q % 2]
                for kt in range(16):
                    kb, krs, kg = KSTG[kt % 2]
                    vb, vrs, vg = VSTG[kt % 2]
                    tbk = 2 + kt % 2
                    dma(kb, csk[q, kt * 128:(kt + 1) * 128, :], writes=[krs], grp=kg)
                    for c in range(4):
                        op("pe", lambda e, c=c, kb=kb, tbk=tbk: e.transpose(out=ps[:, tbk, c * 128:(c + 1) * 128], in_=kb[:, c * 128:(c + 1) * 128], identity=identf[:, :]),
                           reads=[krs, res("identf")], writes=[PB[tbk]])
                    evac_copy("act" if kt % 2 == 0 else "dve", kc_[:, :, kt * 128:(kt + 1) * 128],
                              ps[:, tbk, :].rearrange("p (c j) -> p c j", c=4), [PB[tbk]], kr_)
                    dma(vb, csv[q, kt * 128:(kt + 1) * 128, :], writes=[vrs], grp=vg)
                    op("pool", lambda e, kt=kt, vb=vb, vc_=vc_: e.tensor_copy(out=vc_[:, kt, :], in_=vb), reads=[vrs], writes=vr_)
                op("pool", lambda e, vc_=vc_: e.tensor_copy(out=vc_[0:16, 16, :], in_=vnew0[0:16, q, :]), reads=[res("vnew")], writes=vr_)

            def sb_run(q):
                kc_, vc_, kr_, vr_ = CS[q % 2]
                facts_s = []
                for par_ in range(2):
                    steps = []
                    for kt in range(16, -1, -1):
                        KK = 16 if kt == 16 else 128
                        zp, vp = [], []
                        for j_ in range(4):
                            h = 2 * j_ + par_
                            c, po = h // 2, 64 * (h % 2)
                            if kt == 16:
                                l = kTs[po:po + 64, c, 16 * q:16 * q + 16]
                            else:
                                l = kc_[po:po + 64, c, kt * 128:(kt + 1) * 128]
                            zp.append((l, qT0[po:po + 64, c, 16 * q:16 * q + 16], 0, j_ * 16, j_ * 16 + 16))
                            vp.append((vc_[0:KK, kt, h * 64:(h + 1) * 64], j_ * 16, j_ * 16 + 16))
                        mask = (maskS[:, 0:64], 0, 64) if kt == 16 else None
                        steps.append((KK, zp, 0, 64, mask, vp, kr_ + [res("kT0")], vr_))

                    def evac_s(Ob, opo, q=q, par_=par_):
                        op("act", lambda e: e.copy(
                            out=aoT[0:64, :, 16 * q:16 * q + 16].rearrange("p (j two) t -> p two j t", two=2)[:, par_, :, :],
                            in_=ps[0:64, Ob, 0:64].rearrange("p (j t) -> p j t", j=4)),
                           reads=[PB[Ob]], writes=[res("aoT")])
                    facts_s.append(lambda slot, steps=steps, evac_s=evac_s: sb_head(slot, steps, evac_s))
                run_pool(facts_s, 2, stagger=STG_SBS)
            sb_load(0)
            for q in range(4):
                if q + 1 < 4:
                    sb_load(q + 1)
                sb_run(q)
        else:
            facts = []
            for h in range(8):
                c, po = h // 2, 64 * (h % 2)
                steps = []
                for kt in range(4 * i + 3, -1, -1):
                    o_ = kt - 4 * i
                    c0 = 128 * o_ if o_ >= 0 else 0
                    mask = (maskU[:, :], c0, c0 + 128) if o_ >= 0 else None
                    zp = [(kT0[po:po + 64, c, kt * 128:(kt + 1) * 128], qT0[po:po + 64, c, c0:512], 0, c0, 512)]
                    vp = [(v0[:, kt, h * 64:(h + 1) * 64], c0, 512)]
                    steps.append((128, zp, c0, 512, mask, vp, [res("kT0")], [res("v0")]))

                def evac_p(Ob, opo, h=h):
                    HI[h] = opo
                    op("act", lambda e: e.copy(out=aoT[opo:opo + 64, h, 0:512], in_=ps[opo:opo + 64, Ob, :]), reads=[PB[Ob]], writes=[res("aoT")])
                facts.append(lambda slot, steps=steps, evac_p=evac_p: sb_head(slot, steps, evac_p, three=(4 if SB_SLOTS == 4 else True)))
            def sb_filler(rnd):
                for _ in range(SB_DUMMY):
                    op("pe", lambda e: e.matmul(ps[64:128, 7, 0:SB_DUMMY_N], lhsT=zerosb[:, 0:64], rhs=zerosb[:, 0:SB_DUMMY_N],
                                                start=True, stop=True, skip_group_check=True),
                       reads=[res("zerosb")], writes=[PB[7]])
            if SB_SLOTS == 4:
                run_pool(facts, 4, stagger=STG_SB4)
            else:
                run_pool(facts, 3, stagger=STG_SB, filler=sb_filler if SB_DUMMY else None)

        st(4)
        def parts_ab(o):
            wa, ra = wnext()
            wc, rc = wnext()
            pr_lo, pr_hi, pr_c = [], [], []
            for h in range(8):
                po_ = HI.get(h, 0)
                ent = ((lambda t, h=h, po_=po_: aoT[po_:po_ + 64, h, t * P:(t + 1) * P]), wa[po_:po_ + 64, h, :], [ra, res("aoT")])
                (pr_hi if po_ else pr_lo).append(ent)
            for c in range(4):
                pr_c.append(((lambda t, c=c: cT[:, c, t * P:(t + 1) * P]), wc[:, c, :], [rc, res("cT")]))
            return pr_lo + pr_c + pr_hi
        out_proj_resid(P, NT, parts_ab)
        st(45)
        mlp(0, P, NT)
        st(5)

        if not FIRST_L1_DONE[0]:
            FIRST_L1_DONE[0] = True
            bg_flush()
            S.barrier()
            setup_bias()
        rmsnorm(P, NT, 2)
        own = i % 2 if not sample else 1
        prev = 1 - own
        need_kv = sample or i == 3
        for half in range(2):
            wt, wr = wnext()
            proj_fm(wt, wr, 0, 4, N, lambda c, b, half=half: evac_copy("act", qT1[:, half * 4 + c, 0:N], ps[:, b, 0:N], [PB[b]], [res("qT1")], scale=0.125))
        for half in range(2):
            wt, wr = wnext()
            if sample:
                proj_fm(wt, wr, 0, 4, N, lambda c, b, half=half: evac_copy("dve", kT1s[:, half * 4 + c, 0:N], ps[:, b, 0:N], [PB[b]], [res("kT1s")]))
            else:
                proj_fm(wt, wr, 0, 4, N, lambda c, b, half=half: evac_copy("dve", kT1[own][:, half * 4 + c, 0:N], ps[:, b, 0:N], [PB[b]], [res(f"kT1_{own}")]))
            if need_kv:
                def ev_k1(ti, lo, Pk, b, half=half):
                    evac_copy("act", kst1[0:Pk, 0:512], ps[0:Pk, b, :], [PB[b]], [res("kst1"), res("hidT"), res("sbLb0"), res("sbWb0")])
                    dst = o_bk_s[lo:lo + Pk, half * 512:(half + 1) * 512] if sample else o_bk_p[s, lo:lo + Pk, half * 512:(half + 1) * 512]
                    dma(dst, kst1[0:Pk, 0:512], reads=[res("kst1")], grp=grp("kst1"))
                proj_tm(wt, wr, kvt, ev_k1)
        for half in range(2):
            wt, wr = wnext()

            def ev_v1(ti, lo, Pk, b, half=half):
                if sample:
                    evac_copy("dve", vnew1[0:16, ti, half * 512:(half + 1) * 512], ps[0:16, b, :], [PB[b]], [res("vnew")])
                else:
                    evac_copy("dve", v1[own][:, ti, half * 512:(half + 1) * 512], ps[:, b, :], [PB[b]], [res(f"v1_{own}")])
                if need_kv:
                    evac_copy("act", vst1[0:Pk, 0:512], ps[0:Pk, b, :], [PB[b]], [res("vst1"), res("kst"), res("vst")])
                    dst = o_bv_s[lo:lo + Pk, half * 512:(half + 1) * 512] if sample else o_bv_p[s, lo:lo + Pk, half * 512:(half + 1) * 512]
                    dma(dst, vst1[0:Pk, 0:512], reads=[res("vst1")], grp=grp("vst1"))
            proj_tm(wt, wr, kvt, ev_v1)

        st(6)
        if sample:
            def band_load(q):
                kb_, vb_ = kT1[q % 2], v1[q % 2]
                rk, rv = res(f"kT1_{q % 2}"), res(f"v1_{q % 2}")
                for kt in range(4):
                    dma(stg[:, :], cbk[q, kt * 128:(kt + 1) * 128, :], writes=[res("stg"), res("stgv")], grp=grp("stg"))
                    for hh in range(2):
                        for c in range(4):
                            op("pe", lambda e, c=c, hh=hh: e.transpose(out=ps[:, 3, c * 128:(c + 1) * 128], in_=stg[:, (hh * 4 + c) * 128:(hh * 4 + c + 1) * 128], identity=identf[:, :]),
                               reads=[res("stg"), res("stgv"), res("identf")], writes=[PB[3]])
                        evac_copy("act" if hh == 0 else "dve", kb_[:, hh * 4:hh * 4 + 4, kt * 128:(kt + 1) * 128],
                                  ps[:, 3, :].rearrange("p (c j) -> p c j", c=4), [PB[3]], [rk])
                    dma(kst1[:, :], cbv[q, kt * 128:(kt + 1) * 128, :], writes=[res("kst1"), res("hidT"), res("sbLb0"), res("sbWb0")], grp=grp("kst1"))
                    op("pool", lambda e, kt=kt: e.tensor_copy(out=vb_[:, kt, :], in_=kst1[:, :]), reads=[res("kst1")], writes=[rv])

            def band_run(q):
                kb_, vb_ = kT1[q % 2], v1[q % 2]
                rk, rv = res(f"kT1_{q % 2}"), res(f"v1_{q % 2}")
                op("pool", lambda e: e.tensor_copy(out=v1s[0:16, :], in_=vnew1[0:16, q, :]), reads=[res("vnew")], writes=[res("v1s")])
                segs = [((lambda c, po: kb_[po:po + 64, c, 0:512]), 512, 0, rk),
                        ((lambda c, po: kT1s[po:po + 64, c, 16 * q:16 * q + 16]), 16, 512, res("kT1s"))]
                groups = [(128 * g, 128, (lambda h, g=g: vb_[:, g, h * 64:(h + 1) * 64]), rv) for g in range(4)]
                groups.append((512, 16, (lambda h: v1s[0:16, h * 64:(h + 1) * 64]), res("v1s")))
                band_qtile(16, 16 * q, segs, groups, 0, 528, None, q % 2)
            band_load(0)
            for q in range(4):
                if q + 1 < 4:
                    band_load(q + 1)
                band_run(q)
        else:
            for r in range(4):
                segs = []
                if i > 0:
                    segs.append(((lambda c, po, r=r: kT1[prev][po:po + 64, c, 128 * r:512]), 512 - 128 * r, 0, res(f"kT1_{prev}")))
                segs.append(((lambda c, po, r=r: kT1[own][po:po + 64, c, 0:128 * (r + 1)]), 128 * (r + 1), 512 - 128 * r, res(f"kT1_{own}")))
                kmin = 0 if i > 0 else 512 - 128 * r
                groups = []
                for g in range(5):
                    if 128 * g < kmin:
                        continue
                    tt = r + g
                    if tt < 4:
                        groups.append((128 * g, 128, (lambda h, tt=tt: v1[prev][:, tt, h * 64:(h + 1) * 64]), res(f"v1_{prev}")))
                    else:
                        groups.append((128 * g, 128, (lambda h, tt=tt: v1[own][:, tt - 4, h * 64:(h + 1) * 64]), res(f"v1_{own}")))
                band_qtile(128, 128 * r, segs, groups, kmin, 640, None, r % 2)

        st(7)
        def parts_c(o):
            wo, ro = wnext()
            return [((lambda t, kc=kc: oT[:, kc, t * P:(t + 1) * P]), wo[:, kc, :], [ro, res("oT")]) for kc in range(8)]
        out_proj_resid(P, NT, parts_c)
        st(75)
        mlp(1, P, NT)
        DUMP[0] = None

        rmsnorm(P, NT, None, final=True)
        st(99)
        if sample:
            dma(ys, xtok[0:64, 0, :], reads=[res("xtok")], grp=grp("ystore"))
        else:
            dma(yp[s, t0:t0 + 512, :].rearrange("(t p) f -> p t f", p=128), xtok[:, :, :], reads=[res("xtok")], grp=grp("ystore"))

    import os
    try:
        if not os.environ.get("NO_CONSTS"):
            setup_consts()
        BG["gen"] = convert_gen()
        bg_advance(9)
        stop(1)
        for s in range(2):
            if os.environ.get("NO_PROMPT"):
                bg_flush()
                S.barrier()
                break
            for i in range(4):
                run_block(False, s, i)
                stop(8 + 4 * s + i)
        run_block(True)
    except StopBuild:
        pass
    S.emit()
    return nc, S


_CACHE = {}


def _get_program():
    if "nc" not in _CACHE:
        _CACHE["nc"] = build_program()[0]
    return _CACHE["nc"]


def kernel(x_prompt, x_sample, cache_sb_k, cache_sb_v, cache_conv, cache_band_k, cache_band_v,
           norm_mix, norm_ffn, norm_final, w_in_ab, w_out_ab, dw_w, dw_b, conv_ln_g, conv_ln_b,
           w_in_c, w_out_c, rel_bias, w_up, w_down):
    f = lambda a: np.ascontiguousarray(np.asarray(a, dtype=np.float32))
    x_prompt, x_sample = f(x_prompt), f(x_sample)
    q = np.arange(128)[:, None]
    k = np.arange(640)[None, :]
    idx = np.clip(q + 512 - k, -128, 128) + 128
    bfull = f(np.asarray(rel_bias, dtype=np.float32)[0][:, idx])
    shared = {
        "norm_mix": f(norm_mix), "norm_ffn": f(norm_ffn), "norm_final": f(norm_final),
        "w_in_ab": f(w_in_ab)[0], "w_out_ab": f(w_out_ab)[0], "dw_w": f(dw_w)[0], "dw_b": f(dw_b)[0],
        "ln_g": f(conv_ln_g)[0], "ln_b": f(conv_ln_b)[0], "w_in_c": f(w_in_c)[0], "w_out_c": f(w_out_c)[0],
        "bfull": bfull, "w_up": f(w_up), "w_down": f(w_down),
    }
    csk, csv = f(cache_sb_k)[0], f(cache_sb_v)[0]
    cconv, cbk, cbv = f(cache_conv)[0], f(cache_band_k)[0], f(cache_band_v)[0]
    in_maps = []
    for c in range(NCORES):
        m = dict(shared)
        m["xp"] = x_prompt[2 * c:2 * c + 2]
        m["xs"] = x_sample[4 * c:4 * c + 4].reshape(64, 1024)
        m["csk"] = csk[4 * c:4 * c + 4].reshape(4, 2048, 512)
        m["csv"] = csv[4 * c:4 * c + 4].reshape(4, 2048, 512)
        m["cconv"] = cconv[4 * c:4 * c + 4]
        m["cbk"] = cbk[4 * c:4 * c + 4].reshape(4, 512, 1024)
        m["cbv"] = cbv[4 * c:4 * c + 4].reshape(4, 512, 1024)
        in_maps.append({k_: np.ascontiguousarray(v) for k_, v in m.items()})
    nc = _get_program()
    res = run_bass_kernel_spmd(nc, in_maps, core_ids=list(range(NCORES)))
    R = res.results
    cat = lambda key: np.concatenate([np.asarray(r[key], dtype=np.float32) for r in R], axis=0)
    y_prompt = cat("yp")
    y_sample = cat("ys").reshape(32, 16, 1024)
    sbk_p = cat("o_sbk_p").reshape(1, 16, 2048, 8, 64)
    sbv_p = cat("o_sbv_p").reshape(1, 16, 2048, 8, 64)
    conv_p = cat("o_conv_p").reshape(1, 16, 30, 512)
    bk_p = cat("o_bk_p").reshape(1, 16, 512, 16, 64)
    bv_p = cat("o_bv_p").reshape(1, 16, 512, 16, 64)
    sbk_s = cat("o_sbk_s").reshape(1, 32, 16, 8, 64)
    sbv_s = cat("o_sbv_s").reshape(1, 32, 16, 8, 64)
    conv_s = cat("o_conv_s").reshape(1, 32, 30, 512)
    bk_s = cat("o_bk_s").reshape(1, 32, 16, 16, 64)
    bv_s = cat("o_bv_s").reshape(1, 32, 16, 16, 64)
    return (y_prompt, y_sample, sbk_p, sbv_p, conv_p, bk_p, bv_p, sbk_s, sbv_s, conv_s, bk_s, bv_s)
```

```python
import numpy as np
import concourse.bass as bass
import concourse.mybir as mybir
from concourse.bass_utils import run_bass_kernel_spmd

F32 = mybir.dt.float32
BF16 = mybir.dt.bfloat16
AF = mybir.ActivationFunctionType
ALU = mybir.AluOpType
AX = mybir.AxisListType

EPOCH = 3000
NCORES = 8
RMS_EPS = 1e-6
LN_EPS = 1e-5
NEG = -30000.0


class Res:
    __slots__ = ("name", "last_w", "readers", "excl")

    def __init__(self, name, excl=False):
        self.name = name
        self.excl = excl
        self.last_w = None
        self.readers = []


class DmaGroup:
    def __init__(self, sched, name):
        self.sem = sched.new_sem("dg_" + name)
        self.count = 0


class Sched:
    ENGS = ("pe", "act", "dve", "pool", "sp")

    def __init__(self, nc):
        self.nc = nc
        self.ops = {e: [] for e in self.ENGS}
        self.nops = {e: 0 for e in self.ENGS}
        self.esems = {e: [] for e in self.ENGS}
        self.seen = {e: {} for e in self.ENGS}
        self._semctx = []
        self.groups = []

    def new_sem(self, name):
        ctx = self.nc.semaphore(name)
        s = ctx.__enter__()
        self._semctx.append(ctx)
        return s

    def group(self, name):
        g = DmaGroup(self, name)
        self.groups.append(g)
        return g

    def _eng_ticket(self, eng):
        n = self.nops[eng]
        ep, v = divmod(n, EPOCH)
        while len(self.esems[eng]) <= ep:
            self.esems[eng].append(self.new_sem(f"e_{eng}_{len(self.esems[eng])}"))
        self.nops[eng] = n + 1
        return (self.esems[eng][ep], v + 1, eng)

    def op(self, eng, fn, reads=(), writes=(), dma=None):
        deps = []
        reads = [r for r in reads if r is not None]
        writes = [w for w in writes if w is not None]
        ex = [r for r in reads if r.excl]
        if ex:
            reads = [r for r in reads if not r.excl]
            writes = list(writes) + [r for r in ex if r not in writes]
        for r in reads:
            if r.last_w is not None:
                deps.append(r.last_w)
        for w in writes:
            if w.last_w is not None:
                deps.append(w.last_w)
            deps.extend(w.readers)
        seen = self.seen[eng]
        best = {}
        for (sem, val, src) in deps:
            if src == eng and eng == "pe":
                continue
            k = id(sem)
            if seen.get(k, 0) >= val:
                continue
            if k not in best or best[k][1] < val:
                best[k] = (sem, val)
        for k, (sem, val) in best.items():
            seen[k] = val
        waits = list(best.values())
        if dma is not None:
            dma.count += 16
            ticket = (dma.sem, dma.count, "dma")
            inc = (dma.sem, 16)
        else:
            ticket = self._eng_ticket(eng)
            inc = (ticket[0], 1)
        self.ops[eng].append((waits, fn, inc))
        for r in reads:
            r.readers.append(ticket)
        for w in writes:
            w.last_w = ticket
            w.readers = []
        return ticket

    def barrier(self):
        waits = []
        for e in self.ENGS:
            n = self.nops[e]
            if n > 0:
                ep, v = divmod(n - 1, EPOCH)
                waits.append((self.esems[e][ep], v + 1))
        for g in self.groups:
            if g.count > 0:
                waits.append((g.sem, g.count))
        for e in self.ENGS:
            t = self._eng_ticket(e)
            self.ops[e].append((list(waits), lambda eng: eng.nop(), (t[0], 1)))
            for sem, val in waits:
                k = id(sem)
                if self.seen[e].get(k, 0) < val:
                    self.seen[e][k] = val

    def emit(self):
        nc = self.nc
        fin = [(g.sem, g.count) for g in self.groups if g.count > 0]
        self.ops["sp"].append((fin, lambda e: e.nop(), None))
        engmap = {"pe": "tensor", "act": "scalar", "dve": "vector", "pool": "gpsimd", "sp": "sync"}
        with nc.Block() as block:
            for e in self.ENGS:
                ops = self.ops[e]

                def body(eng, ops=ops):
                    for waits, fn, inc in ops:
                        for sem, val in waits:
                            eng.wait_ge(sem, val)
                        ins = fn(eng)
                        if inc is not None:
                            ins.then_inc(inc[0], inc[1])
                getattr(block, engmap[e])(body)


N_WT = 49
import os as _os
SB_SLOTS = int(_os.environ.get('SB_SLOTS', '4'))
STG_SB4 = int(_os.environ.get('STG_SB4', '2'))
SB4_DUMMY = int(_os.environ.get('SB4_DUMMY', '1'))
SB_DUMMY = int(_os.environ.get('SB_DUMMY', '1'))
CONV_DUMMY = int(_os.environ.get('CONV_DUMMY', '0'))
BAND_DUMMY = int(_os.environ.get('BAND_DUMMY', '0'))
BAND_DUMMY_N = int(_os.environ.get('BAND_DUMMY_N', '512'))
SB_DUMMY_N = int(_os.environ.get('SB_DUMMY_N', '512'))
STG_SB = int(_os.environ.get('STG_SB', '1'))
STG_BAND = int(_os.environ.get('STG_BAND', '0'))
STG_SBS = int(_os.environ.get('STG_SBS', '4'))
DEBUG_STOP = None


class StopBuild(Exception):
    pass


DUMP = [None]
MARKS = []


def stop(k):
    if DEBUG_STOP == k:
        if DUMP[0] is not None:
            DUMP[0]()
        raise StopBuild()


def build_program():
    nc = bass.Bass("TRN2", target_bir_lowering=False, dynamic_dma_scratch_size=512)
    S = Sched(nc)

    def din(name, shape, dt=F32):
        return nc.dram_tensor(name, list(shape), dt, kind="ExternalInput").ap()

    def dout(name, shape, dt=F32):
        return nc.dram_tensor(name, list(shape), dt, kind="ExternalOutput").ap()

    xp = din("xp", [2, 2048, 1024])
    xs = din("xs", [64, 1024])
    csk = din("csk", [4, 2048, 512])
    csv = din("csv", [4, 2048, 512])
    cconv = din("cconv", [4, 30, 512])
    cbk = din("cbk", [4, 512, 1024])
    cbv = din("cbv", [4, 512, 1024])
    norm_mix = din("norm_mix", [2, 1024])
    norm_ffn = din("norm_ffn", [2, 1024])
    norm_final = din("norm_final", [1024])
    w_in_ab = din("w_in_ab", [1024, 2560])
    w_out_ab = din("w_out_ab", [1024, 1024])
    dw_w = din("dw_w", [31, 512])
    dw_b = din("dw_b", [512])
    ln_g = din("ln_g", [512])
    ln_b = din("ln_b", [512])
    w_in_c = din("w_in_c", [1024, 3072])
    w_out_c = din("w_out_c", [1024, 1024])
    bfull = din("bfull", [16, 128, 640])
    w_up = din("w_up", [2, 1024, 4096])
    w_down = din("w_down", [2, 4096, 1024])

    yp = dout("yp", [2, 2048, 1024])
    ys = dout("ys", [64, 1024])
    o_sbk_p = dout("o_sbk_p", [2, 2048, 512])
    o_sbv_p = dout("o_sbv_p", [2, 2048, 512])
    o_conv_p = dout("o_conv_p", [2, 30, 512])
    o_bk_p = dout("o_bk_p", [2, 512, 1024])
    o_bv_p = dout("o_bv_p", [2, 512, 1024])
    o_sbk_s = dout("o_sbk_s", [64, 512])
    o_sbv_s = dout("o_sbv_s", [64, 512])
    o_conv_s = dout("o_conv_s", [4, 30, 512])
    o_bk_s = dout("o_bk_s", [64, 1024])
    o_bv_s = dout("o_bv_s", [64, 1024])

    wsc = nc.dram_tensor("wsc", [N_WT, 128, 4096], BF16, kind="Internal").ap()

    cur = [1024]

    def alloc(name, shape, dt, at=None):
        esz = 4 if dt == F32 else 2
        n = 1
        for d in shape[1:]:
            n *= d
        nbytes = (n * esz + 31) // 32 * 32
        if at is None:
            at = cur[0]
            cur[0] += nbytes
        assert at + nbytes <= 229376, (name, at, nbytes)
        return nc.alloc_sbuf_tensor_at(name, list(shape), dt, offset=at)

    wring = [alloc(f"wring{i}", [128, 4096], BF16) for i in range(3)]
    xtok = alloc("xtok", [128, 4, 1024], F32)
    kT0 = alloc("kT0", [128, 4, 2048], BF16)
    v0 = alloc("v0", [128, 17, 512], BF16)
    KT1_BASE = cur[0]
    kT1 = [alloc(f"kT1_{i}", [128, 8, 512], BF16) for i in range(2)]
    v1 = [alloc(f"v1_{i}", [128, 4, 1024], BF16) for i in range(2)]
    v1s = alloc("v1s", [16, 1024], BF16)
    assert cur[0] >= KT1_BASE + 32768
    BB_BASE = cur[0]
    Bb = alloc("Bb", [128, 16, 640], BF16)
    uext = alloc("uext", [128, 4, 542], BF16)
    Dring = [alloc(f"D{i}", [128, 31, 128], BF16) for i in range(2)]
    identb = alloc("identb", [128, 128], BF16)
    identf = alloc("identf", [128, 128], F32)
    Tst = alloc("Tst", [128, 128], BF16)
    onesb = alloc("onesb", [128, 128], BF16)
    Umat = alloc("Umat", [128, 128], BF16)
    zerosb = alloc("zerosb", [128, 512], BF16)
    maskU = alloc("maskU", [128, 128], BF16)
    maskS = alloc("maskS", [16, 128], BF16)
    gT = alloc("gT", [128, 4, 8], F32)
    gfin = alloc("gfin", [128, 1024], F32)
    dwb_bc = alloc("dwb_bc", [128, 512], F32)
    lng_bc = alloc("lng_bc", [128, 512], F32)
    lnb_bc = alloc("lnb_bc", [128, 512], F32)
    dwT = alloc("dwT", [128, 4, 31], F32)
    stat = alloc("stat", [128, 64], F32)
    rs16 = alloc("rs16", [128, 32], F32)
    ARENA = cur[0]
    ARENA_SZ = 229376 - ARENA

    def ar(name, shape, dt, off):
        return alloc(name, shape, dt, at=ARENA + off)

    hT = ar("hT", [128, 8, 512], BF16, 0)
    qT0 = ar("qT0", [128, 4, 512], BF16, 8192)
    sg = ar("sg", [128, 4, 512], F32, 12288)
    aoT = ar("aoT", [128, 8, 512], BF16, 20480)
    cT = ar("cT", [128, 4, 512], BF16, 28672)
    SBT = [(ar("sbEA0", [128, 512], F32, 32768), ar("sbL20", [128, 512], F32, 34816),
            ar("sbLb0", [128, 512], BF16, 36864), ar("sbWb0", [128, 512], BF16, 37888)),
           (ar("sbEA1", [128, 512], F32, 12288), ar("sbL21", [128, 512], F32, 14336),
            ar("sbLb1", [128, 512], BF16, 16384), ar("sbWb1", [128, 512], BF16, 17408)),
           (ar("sbEA2", [128, 512], F32, 0), ar("sbL22", [128, 512], F32, 2048),
            ar("sbLb2", [128, 512], BF16, 4096), ar("sbWb2", [128, 512], BF16, 5120))]
    SBT.append((ar("sbEA3", [128, 512], F32, 40960), ar("sbL23", [128, 512], F32, 43008),
                ar("sbLb3", [128, 512], BF16, 45056), ar("sbWb3", [128, 512], BF16, 46080)))
    Eb = ar("Eb", [128, 512], F32, 32768)
    junk = ar("junk", [128, 1024], BF16, 32768)
    L2b = ar("L2b", [128, 512], F32, 34816)
    Ab = ar("Ab", [128, 512], F32, 36864)
    Lbb = ar("Lbb", [128, 512], BF16, 38912)
    Wbb = ar("Wbb", [128, 512], BF16, 39936)
    kst = ar("kst", [128, 512], F32, 40960)
    vst = ar("vst", [128, 512], F32, 43008)
    stg = ar("stg", [128, 1024], F32, 45056)
    ysb = ar("ysb", [128, 512], F32, 49152)
    cst = ar("cst", [32, 512], F32, 49152)
    ctm = ar("ctm", [128, 512], BF16, 51200)
    ysb2 = ar("ysb2", [128, 512], F32, 45056)
    ctm2 = ar("ctm2", [128, 512], BF16, 45056 + 2048)
    hn = ar("hn", [128, 1024], BF16, 52224)
    hn2 = ar("hn2", [128, 1024], BF16, 61440)
    utail = ar("utail", [128, 4, 64], F32, 54272)
    uexts = ar("uexts", [128, 4, 4, 46], BF16, 55296)
    kTs = ar("kTs", [128, 4, 64], BF16, 56832)
    L0_END = 57344
    hidT = ar("hidT", [128, 32, 512], BF16, 8192)
    rl = [ar(f"rl{i}", [128, 512], F32, 40960 + 2048 * i) for i in range(2)]
    qT1 = ar("qT1", [128, 8, 512], BF16, 8192)
    oT = ar("oT", [128, 8, 512], BF16, 16384)
    Sp = [ar(f"Sp{i}", [128, 640], F32, 24576 + 2560 * i) for i in range(2)]
    Pe = [ar(f"Pe{i}", [128, 640], BF16, 29696 + 1280 * i) for i in range(2)]
    PT = [ar(f"PT{i}", [128, 5, 128], BF16, 32256 + 1280 * i) for i in range(2)]
    osb = ar("osb", [128, 1024], BF16, 34816)
    kst1 = ar("kst1", [128, 1024], F32, 36864)
    vst1 = ar("vst1", [128, 1024], F32, 40960)
    kT1s = ar("kT1s", [128, 8, 64], BF16, 54272)
    assert 63488 <= ARENA_SZ, (ARENA, ARENA_SZ)
    kT0b = alloc("kT0b", [128, 4, 2048], BF16, at=KT1_BASE)
    v0b = alloc("v0b", [128, 17, 512], BF16, at=KT1_BASE + 16384)
    cvf = [alloc("cvf0", [128, 4096], F32, at=KT1_BASE), alloc("cvf1", [128, 4096], F32, at=KT1_BASE + 16384)]
    cvb = [alloc("cvb0", [128, 4096], BF16, at=BB_BASE), alloc("cvb1", [128, 4096], BF16, at=BB_BASE + 8192)]
    vnew0 = ar("vnew0", [16, 4, 512], BF16, 57344)
    vnew1 = ar("vnew1", [16, 4, 1024], BF16, 55296)

    ps = nc.alloc_psum_tensor("ps", [128, 8, 512], F32)
    PB = [Res(f"psb{i}", excl=True) for i in range(8)]

    def psf(b):
        return ps[:, b, :]

    def psb16(b):
        return ps[:, b, :].bitcast(BF16)

    R = {}

    def res(name):
        if name not in R:
            R[name] = Res(name)
        return R[name]


    dcount = [0]

    def dma(out, in_, reads=(), writes=(), grp=None, slow=False):
        if grp is None:
            grp = S.group(f"d{dcount[0]}")
            dcount[0] += 1
        if slow:
            f = lambda e: e.dma_start(out=out, in_=in_, allow_slow_non_contiguous=True)
        else:
            f = lambda e: e.dma_start(out=out, in_=in_)
        return op("sp", f, reads=reads, writes=writes, dma=grp)

    G = {}

    def grp(name):
        if name not in G:
            G[name] = S.group(name)
        return G[name]

    def setup_consts():
        op("pool", lambda e: e.memset(identb[:], 1.0), writes=[res("identb")])
        op("pool", lambda e: e.affine_select(out=identb[:], in_=identb[:], pattern=[[-1, 128]], compare_op=ALU.is_equal,
                                             fill=0.0, base=0, channel_multiplier=1), writes=[res("identb")])
        op("pool", lambda e: e.memset(identf[:], 1.0), writes=[res("identf")])
        op("pool", lambda e: e.affine_select(out=identf[:], in_=identf[:], pattern=[[-1, 128]], compare_op=ALU.is_equal,
                                             fill=0.0, base=0, channel_multiplier=1), writes=[res("identf")])
        op("pool", lambda e: e.memset(Tst[:], 1.0), writes=[res("Tst")])
        op("pool", lambda e: e.affine_select(out=Tst[:], in_=Tst[:], pattern=[[-1, 128]], compare_op=ALU.is_gt,
                                             fill=0.0, base=0, channel_multiplier=1), writes=[res("Tst")])
        op("pool", lambda e: e.memset(maskU[:], 1.0), writes=[res("maskU")])
        op("pool", lambda e: e.affine_select(out=maskU[:], in_=maskU[:], pattern=[[1, 128]], compare_op=ALU.is_gt,
                                             fill=0.0, base=0, channel_multiplier=-1), writes=[res("maskU")])
        op("pool", lambda e: e.memset(maskS[:], 1.0), writes=[res("maskS")])
        op("pool", lambda e: e.affine_select(out=maskS[:], in_=maskS[:], pattern=[[0, 8], [1, 16]], compare_op=ALU.is_gt,
                                             fill=0.0, base=0, channel_multiplier=-1), writes=[res("maskS")])
        op("pool", lambda e: e.memset(onesb[:], 1.0), writes=[res("onesb")])
        op("pool", lambda e: e.memset(Umat[:], 1.0), writes=[res("Umat")])
        op("pool", lambda e: e.affine_select(out=Umat[:], in_=Umat[:], pattern=[[1, 128]], compare_op=ALU.is_ge,
                                             fill=0.0, base=0, channel_multiplier=-1), writes=[res("Umat")])
        op("pool", lambda e: e.memset(zerosb[:], 0.0), writes=[res("zerosb")])
        op("pool", lambda e: e.memset(uext[:], 0.0), writes=[res("uext")])
        for n, src in enumerate([norm_mix[0], norm_ffn[0], norm_mix[1], norm_ffn[1]]):
            dma(gT[:, n, :], src.rearrange("(c p) -> p c", p=128), writes=[res("gT")], grp=grp("gT"), slow=True)
        dma(gfin[:], norm_final.partition_broadcast(128), writes=[res("gfin")], grp=grp("gfin"))
        dma(dwb_bc[:], dw_b.partition_broadcast(128), writes=[res("cb")], grp=grp("cb"))
        dma(lng_bc[:], ln_g.partition_broadcast(128), writes=[res("cb")], grp=grp("cb"))
        dma(lnb_bc[:], ln_b.partition_broadcast(128), writes=[res("cb")], grp=grp("cb"))
        dma(stg[0:31, 0:512], dw_w, writes=[res("stg")], grp=grp("stg"))
        for c in range(4):
            op("pe", lambda e, c=c: e.transpose(out=ps[:, 0, c * 32:c * 32 + 31], in_=stg[0:31, c * 128:(c + 1) * 128],
                                                identity=identf[0:31, 0:31]),
               reads=[res("stg"), res("identf")], writes=[PB[0]])
        op("dve", lambda e: e.tensor_copy(out=dwT[:], in_=ps[:, 0, 0:128].rearrange("p (c i) -> p c i", c=4)[:, :, 0:31]),
           reads=[PB[0]], writes=[res("dwT")])
    def setup_bias():
        for h in range(16):
            dma(stg[:, 0:640], bfull[h], writes=[res("stg")], grp=grp("stg"))
            eng = "act" if h % 2 == 0 else "dve"
            if eng == "act":
                op("act", lambda e, h=h: e.copy(out=Bb[:, h, :], in_=stg[:, 0:640]), reads=[res("stg")], writes=[res("Bb")])
            else:
                op("dve", lambda e, h=h: e.tensor_copy(out=Bb[:, h, :], in_=stg[:, 0:640]), reads=[res("stg")], writes=[res("Bb")])
        op("pool", lambda e: e.memset(Bb[0:64, :, 576:640], NEG), writes=[res("Bb")])
        op("pool", lambda e: e.memset(Bb[64:128, :, 0:64], NEG), writes=[res("Bb")])

    def wtile_specs():
        sp = []

        def colt(w, c0):
            return (w[:, c0:c0 + 512].rearrange("(k p) c -> p k c", p=128), 128, 8)
        for c0 in (2048, 1536, 0, 512, 1024):
            sp.append(colt(w_in_ab, c0))
        for o in range(2):
            sp.append((w_out_ab[0:512, o * 512:(o + 1) * 512].rearrange("(h p) c -> p h c", p=64), 64, 8))
            sp.append((w_out_ab[512:1024, o * 512:(o + 1) * 512].rearrange("(k p) c -> p k c", p=128), 128, 4))
        for l in range(2):
            if l == 1:
                for c0 in range(0, 3072, 512):
                    sp.append(colt(w_in_c, c0))
                for o in range(2):
                    sp.append(colt(w_out_c, o * 512))
            for j in range(8):
                sp.append(colt(w_up[l], j * 512))
            for o in range(2):
                for g in range(4):
                    sp.append((w_down[l][g * 1024:(g + 1) * 1024, o * 512:(o + 1) * 512].rearrange("(k p) c -> p k c", p=128), 128, 8))
        assert len(sp) == N_WT
        return sp

    WSPEC = wtile_specs()

    def convert_gen():
        def load(n):
            src, P, K = WSPEC[n]
            i = n % 2
            fv = cvf[i][0:P, 0:K * 512].rearrange("p (k c) -> p k c", k=K)
            dma(fv, src, writes=[res(f"cvf{i}")], grp=grp(f"cvf{i}"))

        def cast_store(n):
            src, P, K = WSPEC[n]
            i = n % 2
            eng = ("act", "dve", "pool")[n % 3]
            if eng == "act":
                op("act", lambda e: e.copy(out=cvb[i][0:P, 0:K * 512], in_=cvf[i][0:P, 0:K * 512]),
                   reads=[res(f"cvf{i}")], writes=[res(f"cvb{i}")])
            else:
                op(eng, lambda e: e.tensor_copy(out=cvb[i][0:P, 0:K * 512], in_=cvf[i][0:P, 0:K * 512]),
                   reads=[res(f"cvf{i}")], writes=[res(f"cvb{i}")])
            dma(wsc[n, 0:P, 0:K * 512], cvb[i][0:P, 0:K * 512], reads=[res(f"cvb{i}")], writes=[res(f"wsc{n}")],
                grp=grp(f"cvb{i}"))
        load(0)
        for n in range(N_WT):
            if n + 1 < N_WT:
                load(n + 1)
            cast_store(n)
            yield

    BG = {"gen": None, "busy": False, "every": 45, "cnt": 0}

    def bg_advance(k=1):
        if BG["gen"] is None or BG["busy"]:
            return
        BG["busy"] = True
        try:
            for _ in range(k):
                next(BG["gen"])
        except StopIteration:
            BG["gen"] = None
        BG["busy"] = False

    def bg_flush():
        while BG["gen"] is not None:
            bg_advance(1)

    _raw_op = S.op

    def op(eng, fn, reads=(), writes=(), dma=None):
        t = _raw_op(eng, fn, reads=reads, writes=writes, dma=dma)
        if BG["gen"] is not None and not BG["busy"]:
            BG["cnt"] += 1
            if BG["cnt"] >= BG["every"]:
                BG["cnt"] = 0
                bg_advance(1)
        return t

    class WStream:
        def __init__(self):
            self.issued = 0
            self.taken = 0

        def _issue(self):
            n = self.issued
            t = n % N_WT
            slot = n % 3
            _, P, K = WSPEC[t]
            dma(wring[slot][0:P, 0:K * 512], wsc[t, 0:P, 0:K * 512], reads=[res(f"wsc{t}")],
                writes=[res(f"wring{slot}")], grp=grp(f"wring{slot}"))
            if P == 64:
                dma(wring[slot][64:128, 0:K * 512], wsc[t, 0:P, 0:K * 512], reads=[res(f"wsc{t}")],
                    writes=[res(f"wring{slot}")], grp=grp(f"wring{slot}"))
            self.issued += 1

        def next(self, total):
            while self.issued < min(self.taken + 2, total):
                self._issue()
            n = self.taken
            self.taken += 1
            slot = n % 3
            _, P, K = WSPEC[n % N_WT]
            P = 128 if P == 64 else P
            return wring[slot][0:P, 0:K * 512].rearrange("p (k c) -> p k c", k=K), res(f"wring{slot}")

    WS = WStream()
    TOTAL_TILES = 9 * N_WT

    def wnext():
        return WS.next(TOTAL_TILES)

    bank_rr = [0]

    BANKSEL = [None]

    def bank4():
        sel = BANKSEL[0] or (0, 1, 2, 3)
        b = sel[bank_rr[0] % len(sel)]
        bank_rr[0] += 1
        return b

    def evac_copy(eng, out, in_, reads, writes, scale=None):
        if eng == "act":
            if scale is None:
                op("act", lambda e: e.copy(out=out, in_=in_), reads=reads, writes=writes)
            else:
                op("act", lambda e: e.activation(out=out, in_=in_, func=AF.Copy, scale=scale), reads=reads, writes=writes)
        else:
            if scale is None:
                op(eng, lambda e: e.tensor_copy(out=out, in_=in_), reads=reads, writes=writes)
            else:
                op(eng, lambda e: e.tensor_scalar(out=out, in0=in_, scalar1=scale, scalar2=None, op0=ALU.mult),
                   reads=reads, writes=writes)

    def rmsnorm(P, NT, gidx, final=False):
        for t in range(NT):
            op("act", lambda e, t=t: e.activation(out=junk[0:P, :], in_=xtok[0:P, t, :], func=AF.Square,
                                                  accum_out=stat[0:P, t:t + 1]),
               reads=[res("xtok")], writes=[res(f"nss{t}"), res("junk")])
        op("act", lambda e: e.activation(out=stat[0:P, 8:8 + NT], in_=stat[0:P, 0:NT], func=AF.Sqrt,
                                         scale=1.0 / 1024.0, bias=RMS_EPS),
           reads=[res(f"nss{t}") for t in range(NT)], writes=[res("nsd")])
        op("dve", lambda e: e.reciprocal(out=stat[0:P, 16:16 + NT], in_=stat[0:P, 8:8 + NT]),
           reads=[res("nsd")], writes=[res("nrs")])
        if final:
            for t in range(NT):
                op("dve", lambda e, t=t: e.scalar_tensor_tensor(out=xtok[0:P, t, :], in0=xtok[0:P, t, :],
                                                                scalar=stat[0:P, 16 + t:17 + t], in1=gfin[0:P, :],
                                                                op0=ALU.mult, op1=ALU.mult),
                   reads=[res("nrs"), res("gfin")], writes=[res("xtok")])
            return
        hbuf = [hn, hn2]

        def scale(t):
            op("dve", lambda e: e.tensor_scalar(out=hbuf[t % 2][0:P, :], in0=xtok[0:P, t, :], scalar1=stat[0:P, 16 + t:17 + t],
                                                scalar2=None, op0=ALU.mult),
               reads=[res("xtok"), res("nrs")], writes=[res(f"hn{t % 2}")])

        def tr_evac(t):
            bk = 4 + (t % 2)
            for kc in range(8):
                op("pe", lambda e, kc=kc: e.transpose(out=psb16(bk)[:, kc * P:(kc + 1) * P],
                                                      in_=hbuf[t % 2][0:P, kc * 128:(kc + 1) * 128], identity=identb[0:P, 0:P]),
                   reads=[res(f"hn{t % 2}"), res("identb")], writes=[PB[bk]])
            op("dve", lambda e: e.tensor_tensor(
                out=hT[:, :, t * P:(t + 1) * P],
                in0=psb16(bk)[:, 0:8 * P].rearrange("p (k j) -> p k j", k=8),
                in1=gT[:, gidx, :].unsqueeze(2).broadcast_to([128, 8, P]), op=ALU.mult),
               reads=[PB[bk], res("gT")], writes=[res("hT")])
        scale(0)
        if NT > 1:
            scale(1)
        for t in range(NT):
            tr_evac(t)
            if t + 2 < NT:
                scale(t + 2)

    def proj_fm(wt, wres, col0, nchunks, N, evac):
        for c in range(nchunks):
            b = bank4()
            for kc in range(8):
                op("pe", lambda e, c=c, kc=kc, b=b: e.matmul(ps[:, b, 0:N], lhsT=wt[:, kc, col0 + c * 128:col0 + (c + 1) * 128],
                                                             rhs=hT[:, kc, 0:N], start=(kc == 0), stop=(kc == 7)),
                   reads=[wres, res("hT")], writes=[PB[b]])
            evac(c, b)

    def proj_fm_g(wt, wres, col0, nchunks, N, evac):
        for c in range(nchunks):
            b = bank4()
            for kc in range(8):
                op("pe", lambda e, c=c, kc=kc, b=b: e.matmul(ps[:, b, 0:N], lhsT=wt[:, kc, col0 + c * 128:col0 + (c + 1) * 128],
                                                             rhs=hT[:, kc, 0:N], start=(kc == 0), stop=(kc == 7)),
                   reads=[wres, res("hT")], writes=[PB[b]])
            evac(c, b)
            yield

    def proj_tm_g(wt, wres, tiles, evac):
        for ti, (lo, P) in enumerate(tiles):
            b = bank4()
            for kc in range(8):
                op("pe", lambda e, kc=kc, b=b, lo=lo, P=P: e.matmul(ps[0:P, b, :], lhsT=hT[:, kc, lo:lo + P], rhs=wt[:, kc, :],
                                                                    start=(kc == 0), stop=(kc == 7)),
                   reads=[wres, res("hT")], writes=[PB[b]])
            evac(ti, lo, P, b)
            yield

    def proj_tm(wt, wres, tiles, evac):
        for ti, (lo, P) in enumerate(tiles):
            b = bank4()
            for kc in range(8):
                op("pe", lambda e, kc=kc, b=b, lo=lo, P=P: e.matmul(ps[0:P, b, :], lhsT=hT[:, kc, lo:lo + P], rhs=wt[:, kc, :],
                                                                    start=(kc == 0), stop=(kc == 7)),
                   reads=[wres, res("hT")], writes=[PB[b]])
            evac(ti, lo, P, b)

    def run_pool(factories, nslots, stagger=0, filler=None):
        pending = list(factories)
        active = {}
        rnd = 0
        while pending or active:
            for slot in range(nslots):
                if slot not in active and pending and rnd >= slot * stagger:
                    active[slot] = pending.pop(0)(slot)
            for slot in list(active):
                try:
                    next(active[slot])
                except StopIteration:
                    del active[slot]
            rnd += 1
            if filler is not None and active:
                filler(rnd)

    def sb_banks(slot, n_, three):
        if three == 4:
            return slot % 2, 2 + slot, 6 + slot % 2, (0, 0, 64, 64)[slot]
        if three:
            return slot, 3 + slot, (6, 7, 6)[slot], (0, 0, 64)[slot]
        return slot, 4 + slot, 6 + slot, 0

    def sb_head(slot, steps, evac, split=False, three=False):
        EA, L2t, Lbt, Wbt = SBT[slot]
        if slot == 3:
            rEA, rL2, rLb, rWb = res("kst"), res("vst"), res("stg"), res("stg")
        else:
            rEA, rL2, rLb, rWb = [res(f"sb{n}{slot}") for n in ("EA", "L2", "Lb", "Wb")]
        _, Xb, Ob, opo = sb_banks(slot, 0, three)
        op("pe", lambda e: e.matmul(ps[:, Xb, :], lhsT=zerosb[:, 0:128], rhs=zerosb[:, 0:512], start=True, stop=True),
           reads=[res("zerosb")], writes=[PB[Xb]])
        op("pe", lambda e: e.matmul(ps[opo:opo + 64, Ob, :], lhsT=zerosb[:, 0:64], rhs=zerosb[:, 0:512], start=True, stop=True),
           reads=[res("zerosb")], writes=[PB[Ob]])
        yield
        for n_, stp in enumerate(steps):
            yield from sb_one_step(slot, n_, stp, split, EA, L2t, Lbt, Wbt, rEA, rL2, rLb, rWb, Xb, Ob, opo, three)
        evac(Ob, opo)
        yield

    def sb_one_step(slot, n_, stp, split, EA, L2t, Lbt, Wbt, rEA, rL2, rLb, rWb, Xb, Ob, opo, three):
        if True:
            KK, zparts, c0, N, mask, vparts, kres, vres = stp
            if split:
                zb0 = 2 * slot
                zv = ps[0:KK, zb0:zb0 + 2, 0:64]
                zres = [PB[zb0], PB[zb0 + 1]]
                v3 = lambda ap: ap.rearrange("p (a b) -> p a b", a=2)
            else:
                zb0 = sb_banks(slot, n_, three)[0]
                zv = ps[0:KK, zb0, c0:N]
                zres = [PB[zb0]]
                v3 = lambda ap: ap
            for (l, r, zsel, a_, b_) in zparts:
                bk = zb0 + zsel
                op("pe", lambda e, l=l, r=r, bk=bk, a_=a_, b_=b_: e.matmul(ps[0:KK, bk, a_:b_], lhsT=l, rhs=r, start=True, stop=True),
                   reads=list(kres) + [res("qT0")], writes=[PB[bk]])
            yield
            op("act", lambda e: e.activation(out=v3(EA[0:KK, c0:N]), in_=zv, func=AF.Exp, scale=-1.0), reads=zres, writes=[rEA])
            yield
            op("act", lambda e: e.activation(out=L2t[0:KK, c0:N], in_=EA[0:KK, c0:N], func=AF.Ln, bias=1.0), reads=[rEA], writes=[rL2])
            yield
            op("dve", lambda e: e.tensor_tensor(out=v3(Lbt[0:KK, c0:N]), in0=v3(L2t[0:KK, c0:N]), in1=zv, op=ALU.add),
               reads=[rL2] + zres, writes=[rLb])
            yield
            if mask is not None:
                m_ap, m0, m1 = mask
                op("pool", lambda e: e.tensor_tensor(out=Lbt[0:KK, m0:m1], in0=Lbt[0:KK, m0:m1], in1=m_ap, op=ALU.mult),
                   reads=[res("maskU"), res("maskS")], writes=[rLb])
                yield
            op("pe", lambda e: e.matmul(ps[0:KK, Xb, c0:N], lhsT=Tst[0:KK, 0:KK], rhs=Lbt[0:KK, c0:N], start=False, stop=True,
                                        skip_group_check=True),
               reads=[res("Tst"), rLb], writes=[PB[Xb]])
            yield
            op("dve", lambda e: e.tensor_tensor(out=EA[0:KK, c0:N], in0=L2t[0:KK, c0:N], in1=ps[0:KK, Xb, c0:N], op=ALU.add),
               reads=[rL2, PB[Xb]], writes=[rEA])
            yield
            op("act", lambda e: e.activation(out=Wbt[0:KK, c0:N], in_=EA[0:KK, c0:N], func=AF.Exp, scale=-1.0), reads=[rEA], writes=[rWb])
            yield
            if mask is not None:
                m_ap, m0, m1 = mask
                op("pool", lambda e: e.tensor_tensor(out=Wbt[0:KK, m0:m1], in0=Wbt[0:KK, m0:m1], in1=m_ap, op=ALU.mult),
                   reads=[res("maskU"), res("maskS")], writes=[rWb])
                yield
            for (vl, a_, b_) in vparts:
                op("pe", lambda e, vl=vl, a_=a_, b_=b_: e.matmul(ps[opo:opo + 64, Ob, a_:b_], lhsT=vl, rhs=Wbt[0:KK, a_:b_], start=False, stop=True,
                                                                 skip_group_check=True),
                   reads=list(vres) + [rWb], writes=[PB[Ob]])
            op("pe", lambda e: e.matmul(ps[:, Xb, c0:N], lhsT=Umat[0:KK, :], rhs=Lbt[0:KK, c0:N], start=False, stop=True,
                                        skip_group_check=True),
               reads=[res("Umat"), rLb], writes=[PB[Xb]])
            if three == 4:
                for _ in range(SB4_DUMMY):
                    op("pe", lambda e: e.matmul(ps[:, Xb, 0:512], lhsT=zerosb[:, 0:128], rhs=zerosb[:, 0:512], start=False, stop=True,
                                                skip_group_check=True),
                       reads=[res("zerosb")], writes=[PB[Xb]])
            yield

    def conv_gen(tiles):
        for c in range(4):
            D = Dring[c % 2]
            dres = res(f"D{c % 2}")
            op("pool", lambda e, c=c, D=D: e.tensor_tensor(out=D[:], in0=identb[:].unsqueeze(1).broadcast_to([128, 31, 128]),
                                                           in1=dwT[:, c, :].unsqueeze(2).broadcast_to([128, 31, 128]), op=ALU.mult),
               reads=[res("identb"), res("dwT")], writes=[dres])
            for ti, (ufn, P, lo) in enumerate(tiles):
                for i in range(31):
                    op("pe", lambda e, ti=ti, ufn=ufn, P=P, c=c, i=i, D=D: e.matmul(
                        ps[0:P, ti, c * 128:(c + 1) * 128], lhsT=ufn(c, i, P), rhs=D[:, i, :], start=(i == 0), stop=(i == 30)),
                       reads=[dres, res("uext")], writes=[PB[ti]])
                yield

    def conv_module(tiles):
        nt = len(tiles)
        for c in range(4):
            D = Dring[c % 2]
            dres = res(f"D{c % 2}")
            op("pool", lambda e, c=c, D=D: e.tensor_tensor(out=D[:], in0=identb[:].unsqueeze(1).broadcast_to([128, 31, 128]),
                                                           in1=dwT[:, c, :].unsqueeze(2).broadcast_to([128, 31, 128]), op=ALU.mult),
               reads=[res("identb"), res("dwT")], writes=[dres])
            for ti, (ufn, P, lo) in enumerate(tiles):
                for i in range(31):
                    op("pe", lambda e, ti=ti, ufn=ufn, P=P, c=c, i=i, D=D: e.matmul(
                        ps[0:P, ti, c * 128:(c + 1) * 128], lhsT=ufn(c, i, P), rhs=D[:, i, :], start=(i == 0), stop=(i == 30)),
                       reads=[dres, res("uext")], writes=[PB[ti]])
                    if CONV_DUMMY and P == 128 and i % CONV_DUMMY == CONV_DUMMY - 1:
                        op("pe", lambda e: e.matmul(ps[:, 6, :], lhsT=zerosb[:, 0:128], rhs=zerosb[:, 0:512], start=True, stop=True,
                                                    skip_group_check=True), reads=[res("zerosb")], writes=[PB[6]])
        for ti, (ufn, P, lo) in enumerate(tiles):
            conv_epilogue(ti, P, lo)

    def conv_epilogue(ti, P, lo):
        if ti % 2 == 0:
            yb, cb_, ry, rc = ysb, ctm, res("ysb"), res("ctm")
        else:
            yb, cb_, ry, rc = ysb2, ctm2, res("stg"), res("stg")
        so = 32 + 16 * (ti % 2)
        rst = res(f"cstat{ti % 2}")
        op("dve", lambda e: e.tensor_tensor(out=yb[0:P, :], in0=ps[0:P, ti, :], in1=dwb_bc[0:P, :], op=ALU.add),
           reads=[PB[ti], res("cb")], writes=[ry])
        op("dve", lambda e: e.bn_stats(out=stat[0:P, so:so + 6], in_=yb[0:P, :]), reads=[ry], writes=[rst])
        op("dve", lambda e: e.bn_aggr(out=stat[0:P, so + 6:so + 8], in_=stat[0:P, so:so + 6]), reads=[rst], writes=[rst])
        op("act", lambda e: e.activation(out=stat[0:P, so + 8:so + 9], in_=stat[0:P, so + 7:so + 8], func=AF.Sqrt, scale=1.0, bias=LN_EPS),
           reads=[rst], writes=[rst])
        op("dve", lambda e: e.reciprocal(out=stat[0:P, so + 9:so + 10], in_=stat[0:P, so + 8:so + 9]), reads=[rst], writes=[rst])
        op("dve", lambda e: e.tensor_scalar(out=yb[0:P, :], in0=yb[0:P, :], scalar1=stat[0:P, so + 6:so + 7], scalar2=stat[0:P, so + 9:so + 10],
                                            op0=ALU.subtract, op1=ALU.mult),
           reads=[rst], writes=[ry])
        op("pool", lambda e: e.tensor_tensor(out=yb[0:P, :], in0=yb[0:P, :], in1=lng_bc[0:P, :], op=ALU.mult),
           reads=[res("cb")], writes=[ry])
        op("pool", lambda e: e.tensor_tensor(out=yb[0:P, :], in0=yb[0:P, :], in1=lnb_bc[0:P, :], op=ALU.add),
           reads=[res("cb")], writes=[ry])
        if rc is ry:
            op("act", lambda e: e.activation(out=cb_[0:P, :], in_=yb[0:P, :], func=AF.Silu), reads=[], writes=[ry])
        else:
            op("act", lambda e: e.activation(out=cb_[0:P, :], in_=yb[0:P, :], func=AF.Silu), reads=[ry], writes=[rc])
        b = 4 + (ti % 2)
        for c in range(4):
            op("pe", lambda e, c=c: e.transpose(out=psb16(b)[:, c * P:(c + 1) * P], in_=cb_[0:P, c * 128:(c + 1) * 128],
                                                identity=identb[0:P, 0:P]),
               reads=[rc, res("identb")], writes=[PB[b]])
        op("act", lambda e: e.copy(out=cT[:, :, lo:lo + P], in_=psb16(b)[:, 0:4 * P].rearrange("p (c j) -> p c j", c=4)),
           reads=[PB[b]], writes=[res("cT")])

    def split512(a, e_):
        out = []
        while a < e_:
            bnd = (a // 512 + 1) * 512
            x = min(e_, bnd)
            out.append((a, x))
            a = x
        return out

    def band_head(slot, h, P, qcol, segs, groups, kmin, W):
        c, po = h // 2, 64 * (h % 2)
        sb_ = 2 * slot
        started = set()
        for (kfn, n, dst, kr) in segs:
            for (a, e_) in split512(dst, dst + n):
                bb = sb_ + a // 512
                first = bb not in started
                started.add(bb)
                op("pe", lambda e, a=a, e_=e_, bb=bb, kfn=kfn, dst=dst, first=first: e.matmul(
                    ps[0:P, bb, a % 512:(a % 512) + (e_ - a)], lhsT=qT1[po:po + 64, c, qcol:qcol + P],
                    rhs=kfn(c, po)[:, a - dst:e_ - dst], start=first, stop=True, skip_group_check=True),
                   reads=[res("qT1"), kr], writes=[PB[bb]])
        yield
        for (a, e_) in split512(kmin, W):
            bb = sb_ + a // 512
            op("pe", lambda e, a=a, e_=e_, bb=bb: e.matmul(ps[0:P, bb, a % 512:(a % 512) + (e_ - a)], lhsT=identb[0:P, 0:P],
                                                           rhs=Bb[0:P, h, a:e_], start=False, stop=True, skip_group_check=True),
               reads=[res("identb"), res("Bb")], writes=[PB[bb]])
        yield
        spv = ps[0:P, sb_:sb_ + 2, :].rearrange("p a b -> p (a b)")
        rneg = res(f"negm{slot}")
        op("dve", lambda e: e.tensor_reduce(out=stat[0:P, 48 + slot:49 + slot], in_=spv[:, kmin:W], axis=AX.X, op=ALU.max, negate=True),
           reads=[PB[sb_], PB[sb_ + 1]], writes=[rneg])
        yield
        op("act", lambda e: e.activation(out=Pe[slot][0:P, kmin:W], in_=spv[:, kmin:W], func=AF.Exp, bias=stat[0:P, 48 + slot:49 + slot],
                                         scale=1.0, accum_out=rs16[0:P, h:h + 1]),
           reads=[PB[sb_], PB[sb_ + 1], rneg], writes=[res(f"Pe{slot}"), res(f"rs{h}")])
        yield
        tb = 4 + slot
        ng = len(groups)
        for gi, (g0, KK, vfn, vr) in enumerate(groups):
            op("pe", lambda e, gi=gi, g0=g0, KK=KK: e.transpose(out=psb16(tb)[0:KK, gi * 128:gi * 128 + P],
                                                               in_=Pe[slot][0:P, g0:g0 + KK], identity=identb[0:P, 0:P]),
               reads=[res(f"Pe{slot}"), res("identb")], writes=[PB[tb]])
        yield
        eng = "act" if h % 2 == 0 else "dve"
        kkmax = max(g[1] for g in groups)
        if all(g[1] == kkmax for g in groups):
            evac_copy(eng, PT[slot][0:kkmax, 0:ng, 0:P], psb16(tb)[0:kkmax, 0:ng * 128].rearrange("p (g j) -> p g j", g=ng)[:, :, 0:P],
                      reads=[PB[tb]], writes=[res(f"PT{slot}")])
        else:
            evac_copy(eng, PT[slot][0:128, 0:ng - 1, 0:P], psb16(tb)[0:128, 0:(ng - 1) * 128].rearrange("p (g j) -> p g j", g=ng - 1)[:, :, 0:P],
                      reads=[PB[tb]], writes=[res(f"PT{slot}")])
            kl = groups[-1][1]
            evac_copy(eng, PT[slot][0:kl, ng - 1, 0:P], psb16(tb)[0:kl, (ng - 1) * 128:(ng - 1) * 128 + P],
                      reads=[PB[tb]], writes=[res(f"PT{slot}")])
        yield
        ob = 6 + h // 8
        for gi, (g0, KK, vfn, vr) in enumerate(groups):
            op("pe", lambda e, gi=gi, KK=KK, vfn=vfn: e.matmul(
                ps[0:P, ob, (h % 8) * 64:(h % 8 + 1) * 64], lhsT=PT[slot][0:KK, gi, 0:P], rhs=vfn(h),
                start=(gi == 0), stop=(gi == ng - 1)),
               reads=[res(f"PT{slot}"), vr], writes=[PB[ob]])
        yield

    def band_qtile(P, qcol, segs, groups, kmin, W, osb_done_cb, par):
        Ob = 6
        state = {"done": set(), "started8": False, "norm": [False, False]}

        def normalize(hb):
            state["norm"][hb] = True
            op("dve", lambda e: e.reciprocal(out=rs16[0:P, 16 + hb * 8:24 + hb * 8], in_=rs16[0:P, hb * 8:hb * 8 + 8]),
               reads=[res(f"rs{h}") for h in range(hb * 8, hb * 8 + 8)], writes=[res(f"rinv{hb}")])
            op("dve", lambda e: e.tensor_tensor(
                out=osb[0:P, hb * 512:(hb + 1) * 512].rearrange("p (h d) -> p h d", h=8),
                in0=ps[0:P, Ob + hb, :].rearrange("p (h d) -> p h d", h=8),
                in1=rs16[0:P, 16 + hb * 8:24 + hb * 8].unsqueeze(2).broadcast_to([P, 8, 64]), op=ALU.mult),
               reads=[PB[Ob + hb], res(f"rinv{hb}")], writes=[res(f"osb{hb}")])

        def head_gen(slot, h):
            if h >= 8:
                state["started8"] = True
            yield from band_head(slot, h, P, qcol, segs, groups, kmin, W)
            state["done"].add(h)
            if not state["norm"][0] and all(x in state["done"] for x in range(8)):
                normalize(0)

        def filler(rnd):
            if not BAND_DUMMY or P < 128:
                return
            if not state["started8"]:
                bk = Ob + 1
            elif state["norm"][0]:
                bk = Ob
            else:
                return
            for _ in range(BAND_DUMMY):
                op("pe", lambda e: e.matmul(ps[:, bk, 0:BAND_DUMMY_N], lhsT=zerosb[:, 0:128], rhs=zerosb[:, 0:BAND_DUMMY_N],
                                            start=True, stop=True, skip_group_check=True),
                   reads=[res("zerosb")], writes=[PB[bk]])
        run_pool([(lambda slot, h=h: head_gen(slot, h)) for h in range(16)], 2, stagger=STG_BAND, filler=filler)
        if not state["norm"][0]:
            normalize(0)
        normalize(1)
        tb = 4 + par
        for kc in range(8):
            op("pe", lambda e, kc=kc, tb=tb: e.transpose(out=psb16(tb)[:, kc * P:(kc + 1) * P], in_=osb[0:P, kc * 128:(kc + 1) * 128],
                                                         identity=identb[0:P, 0:P]),
               reads=[res(f"osb{kc // 4}"), res("identb")], writes=[PB[tb]])
        op("act", lambda e, tb=tb: e.copy(out=oT[:, :, qcol:qcol + P], in_=psb16(tb)[:, 0:8 * P].rearrange("p (k j) -> p k j", k=8)),
           reads=[PB[tb]], writes=[res("oT")])

    def mlp(l, P, NT):
        N = P * NT
        rmsnorm(P, NT, 1 + 2 * l)
        for j in range(8):
            wu, wr = wnext()

            def ev(c, b, j=j):
                i2 = (j * 4 + c) % 2
                op("act", lambda e: e.activation(out=rl[i2][:, 0:N], in_=ps[:, b, 0:N], func=AF.Relu),
                   reads=[PB[b]], writes=[res(("kst", "vst")[i2])] + ([res("vst1")] if (j * 4 + c) < 2 else []))
                op("dve", lambda e: e.tensor_tensor(out=hidT[:, j * 4 + c, 0:N], in0=rl[i2][:, 0:N], in1=rl[i2][:, 0:N], op=ALU.mult),
                   reads=[res(("kst", "vst")[i2])], writes=[res("hidT")] + ([res("kst1")] if (j * 4 + c) == 0 else []))
            proj_fm(wu, wr, 0, 4, N, ev)
        for o in range(2):
            for g in range(4):
                wd, wr = wnext()
                for t in range(NT):
                    for j in range(8):
                        op("pe", lambda e, t=t, j=j, g=g, wd=wd: e.matmul(ps[0:P, t, :], lhsT=hidT[:, g * 8 + j, t * P:(t + 1) * P], rhs=wd[:, j, :],
                                                                          start=(g == 0 and j == 0), stop=(g == 3 and j == 7)),
                           reads=[wr, res("hidT")], writes=[PB[t]])
            for t in range(NT):
                op("dve", lambda e, t=t, o=o: e.tensor_tensor(out=xtok[0:P, t, o * 512:(o + 1) * 512], in0=xtok[0:P, t, o * 512:(o + 1) * 512],
                                                              in1=ps[0:P, t, :], op=ALU.add),
                   reads=[PB[t]], writes=[res("xtok")])

    def out_proj_resid(P, NT, parts_fn):
        for o in range(2):
            parts = parts_fn(o)
            for t in range(NT):
                b = bank4()
                n = len(parts)
                for pi, (lf, rhs, rd) in enumerate(parts):
                    op("pe", lambda e, lf=lf, rhs=rhs, t=t, b=b, pi=pi: e.matmul(ps[0:P, b, :], lhsT=lf(t), rhs=rhs, start=(pi == 0), stop=(pi == n - 1)),
                       reads=rd, writes=[PB[b]])
                op("dve", lambda e, t=t, o=o, b=b: e.tensor_tensor(out=xtok[0:P, t, o * 512:(o + 1) * 512], in0=xtok[0:P, t, o * 512:(o + 1) * 512],
                                                                    in1=ps[0:P, b, :], op=ALU.add),
                   reads=[PB[b]], writes=[res("xtok")])

    FIRST_L1_DONE = [False]
    HI = {}

    def run_block(sample, s=0, i=0):
        def st(k):
            MARKS.append((("S" if sample else f"P{s}{i}"), k, S.nops["pe"]))
            stop(k + (100 if sample else 0))
        st(0)
        if sample:
            P, NT, N = 64, 1, 64
            kvt = [(16 * b, 16) for b in range(4)]
            dma(xtok[0:64, 0, :], xs, writes=[res("xtok")], grp=grp("xtok"))
        else:
            P, NT, N = 128, 4, 512
            kvt = [(128 * t, 128) for t in range(4)]
            t0 = 512 * i
            dma(xtok[:, :, :], xp[s, t0:t0 + 512, :].rearrange("(t p) f -> p t f", p=128), writes=[res("xtok")], grp=grp("xtok"))
            if i == 0:
                op("pool", lambda e: e.memset(uext[:, :, 0:30], 0.0), writes=[res("uext")])

        if not sample:
            DUMP[0] = lambda: dma(yp[s, t0:t0 + 512, :].rearrange("(t p) f -> p t f", p=128), xtok[:, :, :], reads=[res("xtok")], grp=grp("ystore"))
        rmsnorm(P, NT, 0)
        wt, wr = wnext()
        proj_fm(wt, wr, 0, 4, N, lambda c, b: op("act", lambda e: e.activation(out=sg[:, c, 0:N], in_=ps[:, b, 0:N], func=AF.Sigmoid),
                                                 reads=[PB[b]], writes=[res("sg")]))
        wt, wr = wnext()

        def ev_a(c, b):
            if sample:
                for q in range(4):
                    op("dve", lambda e, q=q: e.tensor_tensor(out=uexts[:, q, c, 30:46], in0=ps[:, b, 16 * q:16 * q + 16], in1=sg[:, c, 16 * q:16 * q + 16], op=ALU.mult),
                       reads=[PB[b], res("sg")], writes=[res("uext")])
                op("dve", lambda e: e.tensor_tensor(out=utail[:, c, 0:64], in0=ps[:, b, 0:64], in1=sg[:, c, 0:64], op=ALU.mult),
                   reads=[PB[b], res("sg")], writes=[res("utail")])
            else:
                op("dve", lambda e: e.tensor_tensor(out=uext[:, c, 30:542], in0=ps[:, b, 0:512], in1=sg[:, c, 0:512], op=ALU.mult),
                   reads=[PB[b], res("sg")], writes=[res("uext")])
                if i == 3:
                    op("dve", lambda e: e.tensor_tensor(out=utail[:, c, 0:32], in0=ps[:, b, 480:512], in1=sg[:, c, 480:512], op=ALU.mult),
                       reads=[PB[b], res("sg")], writes=[res("utail")])
        proj_fm(wt, wr, 0, 4, N, ev_a)

        def qkv_gen():
            wt, wr = wnext()
            yield from proj_fm_g(wt, wr, 0, 4, N, lambda c, b: evac_copy("act", qT0[:, c, 0:N], ps[:, b, 0:N], [PB[b]], [res("qT0")], scale=0.125))
            wt, wr = wnext()
            if sample:
                yield from proj_fm_g(wt, wr, 0, 4, N, lambda c, b: evac_copy("dve", kTs[:, c, 0:N], ps[:, b, 0:N], [PB[b]], [res("kT0")]))
            else:
                yield from proj_fm_g(wt, wr, 0, 4, N, lambda c, b: evac_copy("dve", kT0[:, c, t0:t0 + N], ps[:, b, 0:N], [PB[b]], [res("kT0")]))

            def ev_k(ti, lo, Pk, b):
                evac_copy("act", kst[0:Pk, :], ps[0:Pk, b, :], [PB[b]], [res("kst")])
                dst = o_sbk_s[lo:lo + Pk, :] if sample else o_sbk_p[s, t0 + lo:t0 + lo + Pk, :]
                dma(dst, kst[0:Pk, :], reads=[res("kst")], grp=grp("kst"))
            yield from proj_tm_g(wt, wr, kvt, ev_k)
            wt, wr = wnext()

            def ev_v(ti, lo, Pk, b):
                evac_copy("act", vst[0:Pk, :], ps[0:Pk, b, :], [PB[b]], [res("vst")])
                if sample:
                    pass
                else:
                    evac_copy("dve", v0[:, 4 * i + ti, :], ps[:, b, :], [PB[b]], [res("v0")])
                dst = o_sbv_s[lo:lo + Pk, :] if sample else o_sbv_p[s, t0 + lo:t0 + lo + Pk, :]
                dma(dst, vst[0:Pk, :], reads=[res("vst")], grp=grp("vst"))
                if sample:
                    evac_copy("dve", vnew0[0:16, ti, :], ps[0:16, b, :], [PB[b]], [res("vnew")])
            yield from proj_tm_g(wt, wr, kvt, ev_v)

        st(2)
        if sample:
            for q in range(4):
                dma(stg[0:30, 0:512], cconv[q], writes=[res("stg")], grp=grp("stg"))
                for c in range(4):
                    op("pe", lambda e, c=c: e.transpose(out=ps[:, 3, c * 32:c * 32 + 30], in_=stg[0:30, c * 128:(c + 1) * 128], identity=identf[0:30, 0:30]),
                       reads=[res("stg"), res("identf")], writes=[PB[3]])
                op("dve", lambda e, q=q: e.tensor_copy(out=uexts[:, q, :, 0:30], in_=ps[:, 3, 0:128].rearrange("p (c j) -> p c j", c=4)[:, :, 0:30]),
                   reads=[PB[3]], writes=[res("uext")])
                for c in range(4):
                    op("pe", lambda e, c=c, q=q: e.transpose(out=ps[0:16, 2, c * 128:(c + 1) * 128], in_=utail[:, c, 16 * q:16 * q + 16], identity=identf[:, :]),
                       reads=[res("utail"), res("identf")], writes=[PB[2]])
                op("act", lambda e: e.copy(out=cst[0:16, :], in_=ps[0:16, 2, :]), reads=[PB[2]], writes=[res("ysb")])
                dma(o_conv_s[q, 14:30, :], cst[0:16, :], reads=[res("ysb")], grp=grp("cst"))
                dma(o_conv_s[q, 0:14, :], cconv[q, 16:30, :], grp=grp("cst2"))
        elif i == 3:
            for c in range(4):
                op("pe", lambda e, c=c: e.transpose(out=ps[0:32, 2, c * 128:(c + 1) * 128], in_=utail[:, c, 0:32], identity=identf[:, :]),
                   reads=[res("utail"), res("identf")], writes=[PB[2]])
            op("act", lambda e: e.copy(out=cst[0:32, :], in_=ps[0:32, 2, :]), reads=[PB[2]], writes=[res("ysb")])
            dma(o_conv_p[s, :, :], cst[2:32, :], reads=[res("ysb")], grp=grp("cst"))

        if sample:
            ctiles = [((lambda c, ii, Pq, q=q: uexts[:, q, c, ii:ii + 16]), 16, 16 * q) for q in range(4)]
        else:
            ctiles = [((lambda c, ii, Pq, t=t: uext[:, c, t * 128 + ii:t * 128 + ii + 128]), 128, 128 * t) for t in range(4)]
        BANKSEL[0] = (4, 5, 6, 7)
        run_pool([lambda slot: conv_gen(ctiles), lambda slot: qkv_gen()], 2)
        BANKSEL[0] = None
        if not sample:
            op("pool", lambda e: e.tensor_copy(out=uext[:, :, 0:30], in_=uext[:, :, 512:542]), reads=[], writes=[res("uext")])
        for ti, (ufn, Pc, lo) in enumerate(ctiles):
            conv_epilogue(ti, Pc, lo)

        st(3)
        HI.clear()
        if sample:
            CS = [(kT0, v0, [res("kT0")], [res("v0")]),
                  (kT0b, v0b, [res("kT1_0"), res("kT1_1")], [res("v1_0"), res("v1_1"), res("v1s")])]
            KSTG = [(stg[:, 0:512], res("stg"), grp("stg")), (kst[:, :], res("kst"), grp("kst"))]
            VSTG = [(stg[:, 512:1024], res("stgv"), grp("stgv")), (vst[:, :], res("vst"), grp("vst"))]

            def sb_load(q):
                kc_, vc_, kr_, vr_ = CS[q % 2]
                for kt in range(16):
                    kb, krs, kg = KSTG[kt % 2]
                    vb, vrs, vg = VSTG[kt % 2]
                    tbk = 2 + kt % 2
                    dma(kb, csk[q, kt * 128:(kt + 1) * 128, :], writes=[krs], grp=kg)
                    for c in range(4):
                        op("pe", lambda e, c=c, kb=kb, tbk=tbk: e.transpose(out=ps[:, tbk, c * 128:(c + 1) * 128], in_=kb[:, c * 128:(c + 1) * 128], identity=identf[:, :]),
                           reads=[krs, res("identf")], writes=[PB[tbk]])
                    evac_copy("act" if kt % 2 == 0 else "dve", kc_[:, :, kt * 128:(kt + 1) * 128],
                              ps[:, tbk, :].rearrange("p (c j) -> p c j", c=4), [PB[tbk]], kr_)
                    dma(vb, csv[q, kt * 128:(kt + 1) * 128, :], writes=[vrs], grp=vg)
                    op("pool", lambda e, kt=kt, vb=vb, vc_=vc_: e.tensor_copy(out=vc_[:, kt, :], in_=vb), reads=[vrs], writes=vr_)
                op("pool", lambda e, vc_=vc_: e.tensor_copy(out=vc_[0:16, 16, :], in_=vnew0[0:16, q, :]), reads=[res("vnew")], writes=vr_)

            def sb_run(q):
                kc_, vc_, kr_, vr_ = CS[q % 2]
                facts_s = []
                for par_ in range(2):
                    steps = []
                    for kt in range(16, -1, -1):
                        KK = 16 if kt == 16 else 128
                        zp, vp = [], []
                        for j_ in range(4):
                            h = 2 * j_ + par_
                            c, po = h // 2, 64 * (h % 2)
                            if kt == 16:
                                l = kTs[po:po + 64, c, 16 * q:16 * q + 16]
                            else:
                                l = kc_[po:po + 64, c, kt * 128:(kt + 1) * 128]
                            zp.append((l, qT0[po:po + 64, c, 16 * q:16 * q + 16], 0, j_ * 16, j_ * 16 + 16))
                            vp.append((vc_[0:KK, kt, h * 64:(h + 1) * 64], j_ * 16, j_ * 16 + 16))
                        mask = (maskS[:, 0:64], 0, 64) if kt == 16 else None
                        steps.append((KK, zp, 0, 64, mask, vp, kr_ + [res("kT0")], vr_))

                    def evac_s(Ob, opo, q=q, par_=par_):
                        op("act", lambda e: e.copy(
                            out=aoT[0:64, :, 16 * q:16 * q + 16].rearrange("p (j two) t -> p two j t", two=2)[:, par_, :, :],
                            in_=ps[0:64, Ob, 0:64].rearrange("p (j t) -> p j t", j=4)),
                           reads=[PB[Ob]], writes=[res("aoT")])
                    facts_s.append(lambda slot, steps=steps, evac_s=evac_s: sb_head(slot, steps, evac_s))
                run_pool(facts_s, 2, stagger=STG_SBS)
            sb_load(0)
            for q in range(4):
                if q + 1 < 4:
                    sb_load(q + 1)
                sb_run(q)
        else:
            facts = []
            for h in range(8):
                c, po = h // 2, 64 * (h % 2)
                steps = []
                for kt in range(4 * i + 3, -1, -1):
                    o_ = kt - 4 * i
                    c0 = 128 * o_ if o_ >= 0 else 0
                    mask = (maskU[:, :], c0, c0 + 128) if o_ >= 0 else None
                    zp = [(kT0[po:po + 64, c, kt * 128:(kt + 1) * 128], qT0[po:po + 64, c, c0:512], 0, c0, 512)]
                    vp = [(v0[:, kt, h * 64:(h + 1) * 64], c0, 512)]
                    steps.append((128, zp, c0, 512, mask, vp, [res("kT0")], [res("v0")]))

                def evac_p(Ob, opo, h=h):
                    HI[h] = opo
                    op("act", lambda e: e.copy(out=aoT[opo:opo + 64, h, 0:512], in_=ps[opo:opo + 64, Ob, :]), reads=[PB[Ob]], writes=[res("aoT")])
                facts.append(lambda slot, steps=steps, evac_p=evac_p: sb_head(slot, steps, evac_p, three=(4 if SB_SLOTS == 4 else True)))
            def sb_filler(rnd):
                for _ in range(SB_DUMMY):
                    op("pe", lambda e: e.matmul(ps[64:128, 7, 0:SB_DUMMY_N], lhsT=zerosb[:, 0:64], rhs=zerosb[:, 0:SB_DUMMY_N],
                                                start=True, stop=True, skip_group_check=True),
                       reads=[res("zerosb")], writes=[PB[7]])
            if SB_SLOTS == 4:
                run_pool(facts, 4, stagger=STG_SB4)
            else:
                run_pool(facts, 3, stagger=STG_SB, filler=sb_filler if SB_DUMMY else None)

        st(4)
        def parts_ab(o):
            wa, ra = wnext()
            wc, rc = wnext()
            pr_lo, pr_hi, pr_c = [], [], []
            for h in range(8):
                po_ = HI.get(h, 0)
                ent = ((lambda t, h=h, po_=po_: aoT[po_:po_ + 64, h, t * P:(t + 1) * P]), wa[po_:po_ + 64, h, :], [ra, res("aoT")])
                (pr_hi if po_ else pr_lo).append(ent)
            for c in range(4):
                pr_c.append(((lambda t, c=c: cT[:, c, t * P:(t + 1) * P]), wc[:, c, :], [rc, res("cT")]))
            return pr_lo + pr_c + pr_hi
        out_proj_resid(P, NT, parts_ab)
        st(45)
        mlp(0, P, NT)
        st(5)

        if not FIRST_L1_DONE[0]:
            FIRST_L1_DONE[0] = True
            bg_flush()
            S.barrier()
            setup_bias()
        rmsnorm(P, NT, 2)
        own = i % 2 if not sample else 1
        prev = 1 - own
        need_kv = sample or i == 3
        for half in range(2):
            wt, wr = wnext()
            proj_fm(wt, wr, 0, 4, N, lambda c, b, half=half: evac_copy("act", qT1[:, half * 4 + c, 0:N], ps[:, b, 0:N], [PB[b]], [res("qT1")], scale=0.125))
        for half in range(2):
            wt, wr = wnext()
            if sample:
                proj_fm(wt, wr, 0, 4, N, lambda c, b, half=half: evac_copy("dve", kT1s[:, half * 4 + c, 0:N], ps[:, b, 0:N], [PB[b]], [res("kT1s")]))
            else:
                proj_fm(wt, wr, 0, 4, N, lambda c, b, half=half: evac_copy("dve", kT1[own][:, half * 4 + c, 0:N], ps[:, b, 0:N], [PB[b]], [res(f"kT1_{own}")]))
            if need_kv:
                def ev_k1(ti, lo, Pk, b, half=half):
                    evac_copy("act", kst1[0:Pk, 0:512], ps[0:Pk, b, :], [PB[b]], [res("kst1"), res("hidT"), res("sbLb0"), res("sbWb0")])
                    dst = o_bk_s[lo:lo + Pk, half * 512:(half + 1) * 512] if sample else o_bk_p[s, lo:lo + Pk, half * 512:(half + 1) * 512]
                    dma(dst, kst1[0:Pk, 0:512], reads=[res("kst1")], grp=grp("kst1"))
                proj_tm(wt, wr, kvt, ev_k1)
        for half in range(2):
            wt, wr = wnext()

            def ev_v1(ti, lo, Pk, b, half=half):
                if sample:
                    evac_copy("dve", vnew1[0:16, ti, half * 512:(half + 1) * 512], ps[0:16, b, :], [PB[b]], [res("vnew")])
                else:
                    evac_copy("dve", v1[own][:, ti, half * 512:(half + 1) * 512], ps[:, b, :], [PB[b]], [res(f"v1_{own}")])
                if need_kv:
                    evac_copy("act", vst1[0:Pk, 0:512], ps[0:Pk, b, :], [PB[b]], [res("vst1"), res("kst"), res("vst")])
                    dst = o_bv_s[lo:lo + Pk, half * 512:(half + 1) * 512] if sample else o_bv_p[s, lo:lo + Pk, half * 512:(half + 1) * 512]
                    dma(dst, vst1[0:Pk, 0:512], reads=[res("vst1")], grp=grp("vst1"))
            proj_tm(wt, wr, kvt, ev_v1)

        st(6)
        if sample:
            def band_load(q):
                kb_, vb_ = kT1[q % 2], v1[q % 2]
                rk, rv = res(f"kT1_{q % 2}"), res(f"v1_{q % 2}")
                for kt in range(4):
                    dma(stg[:, :], cbk[q, kt * 128:(kt + 1) * 128, :], writes=[res("stg"), res("stgv")], grp=grp("stg"))
                    for hh in range(2):
                        for c in range(4):
                            op("pe", lambda e, c=c, hh=hh: e.transpose(out=ps[:, 3, c * 128:(c + 1) * 128], in_=stg[:, (hh * 4 + c) * 128:(hh * 4 + c + 1) * 128], identity=identf[:, :]),
                               reads=[res("stg"), res("stgv"), res("identf")], writes=[PB[3]])
                        evac_copy("act" if hh == 0 else "dve", kb_[:, hh * 4:hh * 4 + 4, kt * 128:(kt + 1) * 128],
                                  ps[:, 3, :].rearrange("p (c j) -> p c j", c=4), [PB[3]], [rk])
                    dma(kst1[:, :], cbv[q, kt * 128:(kt + 1) * 128, :], writes=[res("kst1"), res("hidT"), res("sbLb0"), res("sbWb0")], grp=grp("kst1"))
                    op("pool", lambda e, kt=kt: e.tensor_copy(out=vb_[:, kt, :], in_=kst1[:, :]), reads=[res("kst1")], writes=[rv])

            def band_run(q):
                kb_, vb_ = kT1[q % 2], v1[q % 2]
                rk, rv = res(f"kT1_{q % 2}"), res(f"v1_{q % 2}")
                op("pool", lambda e: e.tensor_copy(out=v1s[0:16, :], in_=vnew1[0:16, q, :]), reads=[res("vnew")], writes=[res("v1s")])
                segs = [((lambda c, po: kb_[po:po + 64, c, 0:512]), 512, 0, rk),
                        ((lambda c, po: kT1s[po:po + 64, c, 16 * q:16 * q + 16]), 16, 512, res("kT1s"))]
                groups = [(128 * g, 128, (lambda h, g=g: vb_[:, g, h * 64:(h + 1) * 64]), rv) for g in range(4)]
                groups.append((512, 16, (lambda h: v1s[0:16, h * 64:(h + 1) * 64]), res("v1s")))
                band_qtile(16, 16 * q, segs, groups, 0, 528, None, q % 2)
            band_load(0)
            for q in range(4):
                if q + 1 < 4:
                    band_load(q + 1)
                band_run(q)
        else:
            for r in range(4):
                segs = []
                if i > 0:
                    segs.append(((lambda c, po, r=r: kT1[prev][po:po + 64, c, 128 * r:512]), 512 - 128 * r, 0, res(f"kT1_{prev}")))
                segs.append(((lambda c, po, r=r: kT1[own][po:po + 64, c, 0:128 * (r + 1)]), 128 * (r + 1), 512 - 128 * r, res(f"kT1_{own}")))
                kmin = 0 if i > 0 else 512 - 128 * r
                groups = []
                for g in range(5):
                    if 128 * g < kmin:
                        continue
                    tt = r + g
                    if tt < 4:
                        groups.append((128 * g, 128, (lambda h, tt=tt: v1[prev][:, tt, h * 64:(h + 1) * 64]), res(f"v1_{prev}")))
                    else:
                        groups.append((128 * g, 128, (lambda h, tt=tt: v1[own][:, tt - 4, h * 64:(h + 1) * 64]), res(f"v1_{own}")))
                band_qtile(128, 128 * r, segs, groups, kmin, 640, None, r % 2)

        st(7)
        def parts_c(o):
            wo, ro = wnext()
            return [((lambda t, kc=kc: oT[:, kc, t * P:(t + 1) * P]), wo[:, kc, :], [ro, res("oT")]) for kc in range(8)]
        out_proj_resid(P, NT, parts_c)
        st(75)
        mlp(1, P, NT)
        DUMP[0] = None

        rmsnorm(P, NT, None, final=True)
        st(99)
        if sample:
            dma(ys, xtok[0:64, 0, :], reads=[res("xtok")], grp=grp("ystore"))
        else:
            dma(yp[s, t0:t0 + 512, :].rearrange("(t p) f -> p t f", p=128), xtok[:, :, :], reads=[res("xtok")], grp=grp("ystore"))

    import os
    try:
        if not os.environ.get("NO_CONSTS"):
            setup_consts()
        BG["gen"] = convert_gen()
        bg_advance(9)
        stop(1)
        for s in range(2):
            if os.environ.get("NO_PROMPT"):
                bg_flush()
                S.barrier()
                break
            for i in range(4):
                run_block(False, s, i)
                stop(8 + 4 * s + i)
        run_block(True)
    except StopBuild:
        pass
    S.emit()
    return nc, S


_CACHE = {}


def _get_program():
    if "nc" not in _CACHE:
        _CACHE["nc"] = build_program()[0]
    return _CACHE["nc"]


def kernel(x_prompt, x_sample, cache_sb_k, cache_sb_v, cache_conv, cache_band_k, cache_band_v,
           norm_mix, norm_ffn, norm_final, w_in_ab, w_out_ab, dw_w, dw_b, conv_ln_g, conv_ln_b,
           w_in_c, w_out_c, rel_bias, w_up, w_down):
    f = lambda a: np.ascontiguousarray(np.asarray(a, dtype=np.float32))
    x_prompt, x_sample = f(x_prompt), f(x_sample)
    q = np.arange(128)[:, None]
    k = np.arange(640)[None, :]
    idx = np.clip(q + 512 - k, -128, 128) + 128
    bfull = f(np.asarray(rel_bias, dtype=np.float32)[0][:, idx])
    shared = {
        "norm_mix": f(norm_mix), "norm_ffn": f(norm_ffn), "norm_final": f(norm_final),
        "w_in_ab": f(w_in_ab)[0], "w_out_ab": f(w_out_ab)[0], "dw_w": f(dw_w)[0], "dw_b": f(dw_b)[0],
        "ln_g": f(conv_ln_g)[0], "ln_b": f(conv_ln_b)[0], "w_in_c": f(w_in_c)[0], "w_out_c": f(w_out_c)[0],
        "bfull": bfull, "w_up": f(w_up), "w_down": f(w_down),
    }
    csk, csv = f(cache_sb_k)[0], f(cache_sb_v)[0]
    cconv, cbk, cbv = f(cache_conv)[0], f(cache_band_k)[0], f(cache_band_v)[0]
    in_maps = []
    for c in range(NCORES):
        m = dict(shared)
        m["xp"] = x_prompt[2 * c:2 * c + 2]
        m["xs"] = x_sample[4 * c:4 * c + 4].reshape(64, 1024)
        m["csk"] = csk[4 * c:4 * c + 4].reshape(4, 2048, 512)
        m["csv"] = csv[4 * c:4 * c + 4].reshape(4, 2048, 512)
        m["cconv"] = cconv[4 * c:4 * c + 4]
        m["cbk"] = cbk[4 * c:4 * c + 4].reshape(4, 512, 1024)
        m["cbv"] = cbv[4 * c:4 * c + 4].reshape(4, 512, 1024)
        in_maps.append({k_: np.ascontiguousarray(v) for k_, v in m.items()})
    nc = _get_program()
    res = run_bass_kernel_spmd(nc, in_maps, core_ids=list(range(NCORES)))
    R = res.results
    cat = lambda key: np.concatenate([np.asarray(r[key], dtype=np.float32) for r in R], axis=0)
    y_prompt = cat("yp")
    y_sample = cat("ys").reshape(32, 16, 1024)
    sbk_p = cat("o_sbk_p").reshape(1, 16, 2048, 8, 64)
    sbv_p = cat("o_sbv_p").reshape(1, 16, 2048, 8, 64)
    conv_p = cat("o_conv_p").reshape(1, 16, 30, 512)
    bk_p = cat("o_bk_p").reshape(1, 16, 512, 16, 64)
    bv_p = cat("o_bv_p").reshape(1, 16, 512, 16, 64)
    sbk_s = cat("o_sbk_s").reshape(1, 32, 16, 8, 64)
    sbv_s = cat("o_sbv_s").reshape(1, 32, 16, 8, 64)
    conv_s = cat("o_conv_s").reshape(1, 32, 30, 512)
    bk_s = cat("o_bk_s").reshape(1, 32, 16, 16, 64)
    bv_s = cat("o_bv_s").reshape(1, 32, 16, 16, 64)
    return (y_prompt, y_sample, sbk_p, sbv_p, conv_p, bk_p, bv_p, sbk_s, sbv_s, conv_s, bk_s, bv_s)
```
